# Optimizing a Trainium2 kernel written in Bass

```python
import jax, jax.numpy as jnp
from jax import lax
import numpy as np

D_MODEL = 1024
BATCH = 32
SEQ = 2048
DEPTH = 4
DEC_BATCH = 32
DEC_SEQ = 64
PAST_LEN = 2048

CHUNK = 64
Q_BLOCK = 128
N_MIXERS = 3
N_MLA = (DEPTH + 2) // 3
N_SB = (DEPTH + 1) // 3
N_SWA = DEPTH // 3
RMS_EPS = 1e-6
ROPE_THETA = 10000.0
NEG = -1e30

MLA_HEADS = 16
MLA_NOPE = 128
MLA_ROPE = 64
MLA_V = 128
MLA_Q_LORA = 384
MLA_KV_LORA = 256
MLA_SCALE = (MLA_NOPE + MLA_ROPE) ** -0.5

SB_HEADS = 16
SB_HEAD_DIM = D_MODEL // SB_HEADS
SB_SCALE = SB_HEAD_DIM ** -0.5

SWA_HEADS = 16
SWA_KV_HEADS = 4
SWA_GROUP = SWA_HEADS // SWA_KV_HEADS
SWA_HEAD_DIM = 64
SWA_WINDOW = 128
SWA_CHUNKS_BACK = SWA_WINDOW // CHUNK
SWA_SCALE = SWA_HEAD_DIM ** -0.5

D_FF = 2816
CONV_W = 3
PLE_DIM = 256

kernel_name = 'hybrid_streaming_mla_sb_swa_convffn_step'


def rms_norm(x, g):
    x32 = x.astype(jnp.float32)
    y = x32 * lax.rsqrt(jnp.mean(x32 * x32, axis=-1, keepdims=True) + RMS_EPS)
    return (y * g.astype(jnp.float32)).astype(x.dtype)


def rope(x, pos):
    half = x.shape[-1] // 2
    inv = ROPE_THETA ** (-jnp.arange(half, dtype=jnp.float32) / half)
    ang = pos.astype(jnp.float32)[:, None] * inv[None, :]
    cos = jnp.cos(ang)[None, :, None, :]
    sin = jnp.sin(ang)[None, :, None, :]
    x32 = x.astype(jnp.float32)
    x1, x2 = x32[..., :half], x32[..., half:]
    return jnp.concatenate([x1 * cos - x2 * sin, x2 * cos + x1 * sin], axis=-1).astype(x.dtype)


def to_blocks(a, nb):
    b = a.reshape(a.shape[0], nb, a.shape[1] // nb, *a.shape[2:])
    return jnp.moveaxis(b, 1, 0)


def from_blocks(a):
    a = jnp.moveaxis(a, 0, 1)
    return a.reshape(a.shape[0], a.shape[1] * a.shape[2], *a.shape[3:])


def band_blocks(a, nb):
    pad = [(0, 0), (SWA_WINDOW, 0)] + [(0, 0)] * (a.ndim - 2)
    blk = to_blocks(jnp.pad(a, pad), nb + 1)
    return jnp.concatenate([blk[:-1], blk[1:]], axis=2)


def mla_project(h, pos, w_dq, g_q, w_uq, w_dkv, g_kv, w_uk):
    b, t, _ = h.shape
    cq = rms_norm(h @ w_dq, g_q)
    q = (cq @ w_uq).reshape(b, t, MLA_HEADS, MLA_NOPE + MLA_ROPE)
    q_rope = rope(q[..., MLA_NOPE:], pos)
    q_lat = jnp.einsum('bthn,lhn->bthl', q[..., :MLA_NOPE], w_uk)
    kv = h @ w_dkv
    ckv = rms_norm(kv[..., :MLA_KV_LORA], g_kv)
    krope = rope(kv[..., None, MLA_KV_LORA:], pos)[:, :, 0, :]
    return q_lat, q_rope, ckv, krope


def mla_core(q_lat, q_rope, q_pos, ckv, krope, k_pos, w_uv):
    s = jnp.einsum('bqhl,bkl->bhqk', q_lat, ckv).astype(jnp.float32)
    s = (s + jnp.einsum('bqhr,bkr->bhqk', q_rope, krope).astype(jnp.float32)) * MLA_SCALE
    mask = (k_pos[None, :] // CHUNK) <= (q_pos[:, None] // CHUNK)
    p = jax.nn.softmax(jnp.where(mask, s, NEG), axis=-1).astype(ckv.dtype)
    o_lat = jnp.einsum('bhqk,bkl->bqhl', p, ckv)
    return jnp.einsum('bqhl,lhv->bqhv', o_lat, w_uv)


def mla_mixer(hp, hs, cache_ckv, cache_krope, w_dq, g_q, w_uq, w_dkv, g_kv, w_uk, w_uv, w_o):
    b, s_len, _ = hp.shape
    db, t_len, _ = hs.shape
    nb = s_len // Q_BLOCK
    pos_p = jnp.arange(s_len, dtype=jnp.int32)
    ql, qr, ckv_p, kr_p = mla_project(hp, pos_p, w_dq, g_q, w_uq, w_dkv, g_kv, w_uk)
    o = lax.map(lambda a: mla_core(a[0], a[1], a[2], ckv_p, kr_p, pos_p, w_uv),
                (to_blocks(ql, nb), to_blocks(qr, nb), pos_p.reshape(nb, Q_BLOCK)))
    mix_p = from_blocks(o).reshape(b, s_len, MLA_HEADS * MLA_V) @ w_o
    pos_s = PAST_LEN + jnp.arange(t_len, dtype=jnp.int32)
    ql_s, qr_s, ckv_s, kr_s = mla_project(hs, pos_s, w_dq, g_q, w_uq, w_dkv, g_kv, w_uk)
    ckv_all = jnp.concatenate([cache_ckv.astype(ckv_s.dtype), ckv_s], axis=1)
    kr_all = jnp.concatenate([cache_krope.astype(kr_s.dtype), kr_s], axis=1)
    k_pos = jnp.arange(ckv_all.shape[1], dtype=jnp.int32)
    o_s = mla_core(ql_s, qr_s, pos_s, ckv_all, kr_all, k_pos, w_uv)
    mix_s = o_s.reshape(db, t_len, MLA_HEADS * MLA_V) @ w_o
    return mix_p, mix_s, ckv_p, kr_p, ckv_s, kr_s


def sb_core(q, q_pos, k, v, k_pos):
    z = jnp.einsum('bqhd,bkhd->bhqk', q, k).astype(jnp.float32) * SB_SCALE
    mask = k_pos[None, :] < q_pos[:, None]
    log_not = jnp.where(mask, jax.nn.log_sigmoid(-z), 0.0)
    tail = lax.cumsum(log_not, axis=3, reverse=True) - log_not
    a = jnp.where(mask, jnp.exp(jax.nn.log_sigmoid(z) + tail), 0.0).astype(v.dtype)
    return jnp.einsum('bhqk,bkhd->bqhd', a, v)


def sb_mixer(hp, hs, cache_k, cache_v, w_qkv, w_o):
    def proj(h):
        b, t, _ = h.shape
        qkv = (h @ w_qkv).reshape(b, t, 3, SB_HEADS, SB_HEAD_DIM)
        return qkv[:, :, 0], qkv[:, :, 1], qkv[:, :, 2]
    b, s_len, _ = hp.shape
    db, t_len, _ = hs.shape
    nb = s_len // Q_BLOCK
    pos_p = jnp.arange(s_len, dtype=jnp.int32)
    qp, kp, vp = proj(hp)
    o = lax.map(lambda a: sb_core(a[0], a[1], kp, vp, pos_p),
                (to_blocks(qp, nb), pos_p.reshape(nb, Q_BLOCK)))
    mix_p = from_blocks(o).reshape(b, s_len, SB_HEADS * SB_HEAD_DIM) @ w_o
    qs, ks, vs = proj(hs)
    pos_s = PAST_LEN + jnp.arange(t_len, dtype=jnp.int32)
    k_all = jnp.concatenate([cache_k.astype(ks.dtype), ks], axis=1)
    v_all = jnp.concatenate([cache_v.astype(vs.dtype), vs], axis=1)
    k_pos = jnp.arange(k_all.shape[1], dtype=jnp.int32)
    o_s = sb_core(qs, pos_s, k_all, v_all, k_pos)
    mix_s = o_s.reshape(db, t_len, SB_HEADS * SB_HEAD_DIM) @ w_o
    return mix_p, mix_s, kp, vp, ks, vs


def swa_core(q, q_pos, k, v, k_pos, sinks):
    s = jnp.einsum('bqkgd,bskd->bkgqs', q, k).astype(jnp.float32) * SWA_SCALE
    dc = q_pos[:, None] // CHUNK - k_pos[None, :] // CHUNK
    mask = (dc >= 0) & (dc <= SWA_CHUNKS_BACK) & (k_pos[None, :] >= 0)
    s = jnp.where(mask, s, NEG)
    sink = sinks.astype(jnp.float32).reshape(SWA_KV_HEADS, SWA_GROUP)[None, :, :, None, None]
    m = jnp.maximum(jnp.max(s, axis=-1, keepdims=True), sink)
    e = jnp.exp(s - m)
    p = (e / (jnp.sum(e, axis=-1, keepdims=True) + jnp.exp(sink - m))).astype(v.dtype)
    return jnp.einsum('bkgqs,bskd->bqkgd', p, v)


def swa_mixer(hp, hs, cache_k, cache_v, w_qkv, b_qkv, sinks, w_o):
    nq = SWA_HEADS * SWA_HEAD_DIM
    nk = SWA_KV_HEADS * SWA_HEAD_DIM
    def proj(h, pos):
        b, t, _ = h.shape
        qkv = h @ w_qkv + b_qkv
        q = rope(qkv[..., :nq].reshape(b, t, SWA_HEADS, SWA_HEAD_DIM), pos)
        k = rope(qkv[..., nq:nq + nk].reshape(b, t, SWA_KV_HEADS, SWA_HEAD_DIM), pos)
        v = qkv[..., nq + nk:].reshape(b, t, SWA_KV_HEADS, SWA_HEAD_DIM)
        return q.reshape(b, t, SWA_KV_HEADS, SWA_GROUP, SWA_HEAD_DIM), k, v
    b, s_len, _ = hp.shape
    db, t_len, _ = hs.shape
    nb = s_len // Q_BLOCK
    pos_p = jnp.arange(s_len, dtype=jnp.int32)
    qp, kp, vp = proj(hp, pos_p)
    pos_band = (jnp.arange(nb, dtype=jnp.int32)[:, None] * Q_BLOCK - SWA_WINDOW
                + jnp.arange(SWA_WINDOW + Q_BLOCK, dtype=jnp.int32)[None, :])
    o = lax.map(lambda a: swa_core(a[0], a[1], a[2], a[3], a[4], sinks),
                (to_blocks(qp, nb), pos_p.reshape(nb, Q_BLOCK), band_blocks(kp, nb),
                 band_blocks(vp, nb), pos_band))
    mix_p = from_blocks(o).reshape(b, s_len, nq) @ w_o
    pos_s = PAST_LEN + jnp.arange(t_len, dtype=jnp.int32)
    qs, ks, vs = proj(hs, pos_s)
    c_len = cache_k.shape[1]
    k_all = jnp.concatenate([cache_k.astype(ks.dtype), ks], axis=1)
    v_all = jnp.concatenate([cache_v.astype(vs.dtype), vs], axis=1)
    k_pos = PAST_LEN - c_len + jnp.arange(c_len + t_len, dtype=jnp.int32)
    o_s = swa_core(qs, pos_s, k_all, v_all, k_pos, sinks)
    mix_s = o_s.reshape(db, t_len, nq) @ w_o
    return (mix_p, mix_s, kp[:, -SWA_WINDOW:], vp[:, -SWA_WINDOW:],
            k_all[:, -SWA_WINDOW:], v_all[:, -SWA_WINDOW:])


def conv_ffn(h, conv_state, w_in, conv_w, conv_b, w_out):
    u = h @ w_in
    gate, up = u[..., :D_FF], u[..., D_FF:]
    gx = jnp.concatenate([conv_state.astype(gate.dtype), gate], axis=1)
    conv = lax.conv_general_dilated(gx, conv_w[:, None, :].astype(gx.dtype), window_strides=(1,),
                                    padding='VALID', dimension_numbers=('NWC', 'WIO', 'NWC'),
                                    feature_group_count=D_FF) + conv_b
    y = (jax.nn.gelu(conv, approximate=False) * up) @ w_out
    return y, gx[:, -(CONV_W - 1):]


def ple(x, p, g, w_gate, w_proj):
    return jax.nn.sigmoid(rms_norm(x, g) @ w_gate) * (p @ w_proj)


def setup_inputs(seed: int = 0) -> dict:
    key = jax.random.key(seed)
    ks = list(jax.random.split(key, 40))
    def nrm(shape, scale=1.0):
        return scale * jax.random.normal(ks.pop(), shape, jnp.float32)
    def gain(shape):
        return 1.0 + 0.01 * nrm(shape)
    swa_cache = min(SWA_WINDOW, PAST_LEN)
    qkv_w = (SWA_HEADS + 2 * SWA_KV_HEADS) * SWA_HEAD_DIM
    return {
        'x_prompt': nrm((BATCH, SEQ, D_MODEL)),
        'x_sample': nrm((DEC_BATCH, DEC_SEQ, D_MODEL)),
        'p_prompt': nrm((DEPTH, BATCH, SEQ, PLE_DIM)),
        'p_sample': nrm((DEPTH, DEC_BATCH, DEC_SEQ, PLE_DIM)),
        'cache_mla_ckv': nrm((N_MLA, DEC_BATCH, PAST_LEN, MLA_KV_LORA)),
        'cache_mla_krope': nrm((N_MLA, DEC_BATCH, PAST_LEN, MLA_ROPE)),
        'cache_sb_k': nrm((N_SB, DEC_BATCH, PAST_LEN, SB_HEADS, SB_HEAD_DIM)),
        'cache_sb_v': nrm((N_SB, DEC_BATCH, PAST_LEN, SB_HEADS, SB_HEAD_DIM)),
        'cache_swa_k': nrm((N_SWA, DEC_BATCH, swa_cache, SWA_KV_HEADS, SWA_HEAD_DIM)),
        'cache_swa_v': nrm((N_SWA, DEC_BATCH, swa_cache, SWA_KV_HEADS, SWA_HEAD_DIM)),
        'state_ffn_conv': nrm((DEPTH, DEC_BATCH, CONV_W - 1, D_FF)),
        'g_mix': gain((DEPTH, D_MODEL)),
        'g_ffn': gain((DEPTH, D_MODEL)),
        'g_ple': gain((DEPTH, D_MODEL)),
        'g_final': gain((D_MODEL,)),
        'w_mla_dq': nrm((N_MLA, D_MODEL, MLA_Q_LORA), D_MODEL ** -0.5),
        'g_mla_q': gain((N_MLA, MLA_Q_LORA)),
        'w_mla_uq': nrm((N_MLA, MLA_Q_LORA, MLA_HEADS * (MLA_NOPE + MLA_ROPE)), MLA_Q_LORA ** -0.5),
        'w_mla_dkv': nrm((N_MLA, D_MODEL, MLA_KV_LORA + MLA_ROPE), D_MODEL ** -0.5),
        'g_mla_kv': gain((N_MLA, MLA_KV_LORA)),
        'w_mla_uk': nrm((N_MLA, MLA_KV_LORA, MLA_HEADS, MLA_NOPE), MLA_KV_LORA ** -0.5),
        'w_mla_uv': nrm((N_MLA, MLA_KV_LORA, MLA_HEADS, MLA_V), MLA_KV_LORA ** -0.5),
        'w_mla_o': nrm((N_MLA, MLA_HEADS * MLA_V, D_MODEL), (MLA_HEADS * MLA_V) ** -0.5),
        'w_sb_qkv': nrm((N_SB, D_MODEL, 3 * SB_HEADS * SB_HEAD_DIM), D_MODEL ** -0.5),
        'w_sb_o': nrm((N_SB, SB_HEADS * SB_HEAD_DIM, D_MODEL), (SB_HEADS * SB_HEAD_DIM) ** -0.5),
        'w_swa_qkv': nrm((N_SWA, D_MODEL, qkv_w), D_MODEL ** -0.5),
        'b_swa_qkv': nrm((N_SWA, qkv_w), 0.02),
        'swa_sinks': nrm((N_SWA, SWA_HEADS), 0.5),
        'w_swa_o': nrm((N_SWA, SWA_HEADS * SWA_HEAD_DIM, D_MODEL), (SWA_HEADS * SWA_HEAD_DIM) ** -0.5),
        'w_ffn_in': nrm((DEPTH, D_MODEL, 2 * D_FF), D_MODEL ** -0.5),
        'ffn_conv_w': nrm((DEPTH, CONV_W, D_FF), CONV_W ** -0.5),
        'ffn_conv_b': nrm((DEPTH, D_FF), 0.02),
        'w_ffn_out': nrm((DEPTH, D_FF, D_MODEL), D_FF ** -0.5),
        'w_ple_gate': nrm((DEPTH, D_MODEL, D_MODEL), D_MODEL ** -0.5),
        'w_ple_proj': nrm((DEPTH, PLE_DIM, D_MODEL), PLE_DIM ** -0.5),
    }


def reference(x_prompt, x_sample, p_prompt, p_sample, cache_mla_ckv, cache_mla_krope,
              cache_sb_k, cache_sb_v, cache_swa_k, cache_swa_v, state_ffn_conv,
              g_mix, g_ffn, g_ple, g_final,
              w_mla_dq, g_mla_q, w_mla_uq, w_mla_dkv, g_mla_kv, w_mla_uk, w_mla_uv, w_mla_o,
              w_sb_qkv, w_sb_o, w_swa_qkv, b_swa_qkv, swa_sinks, w_swa_o,
              w_ffn_in, ffn_conv_w, ffn_conv_b, w_ffn_out, w_ple_gate, w_ple_proj):
    xp, xs = x_prompt, x_sample
    mla_ckv_p, mla_kr_p, mla_ckv_s, mla_kr_s = [], [], [], []
    sb_k_p, sb_v_p, sb_k_s, sb_v_s = [], [], [], []
    swa_k_p, swa_v_p, swa_k_s, swa_v_s = [], [], [], []
    conv_p, conv_s = [], []
    for i in range(DEPTH):
        j = i // N_MIXERS
        hp = rms_norm(xp, g_mix[i])
        hs = rms_norm(xs, g_mix[i])
        if i % N_MIXERS == 0:
            mp, ms, a0, a1, a2, a3 = mla_mixer(hp, hs, cache_mla_ckv[j], cache_mla_krope[j],
                                               w_mla_dq[j], g_mla_q[j], w_mla_uq[j], w_mla_dkv[j],
                                               g_mla_kv[j], w_mla_uk[j], w_mla_uv[j], w_mla_o[j])
            mla_ckv_p.append(a0); mla_kr_p.append(a1); mla_ckv_s.append(a2); mla_kr_s.append(a3)
        elif i % N_MIXERS == 1:
            mp, ms, a0, a1, a2, a3 = sb_mixer(hp, hs, cache_sb_k[j], cache_sb_v[j], w_sb_qkv[j], w_sb_o[j])
            sb_k_p.append(a0); sb_v_p.append(a1); sb_k_s.append(a2); sb_v_s.append(a3)
        else:
            mp, ms, a0, a1, a2, a3 = swa_mixer(hp, hs, cache_swa_k[j], cache_swa_v[j],
                                               w_swa_qkv[j], b_swa_qkv[j], swa_sinks[j], w_swa_o[j])
            swa_k_p.append(a0); swa_v_p.append(a1); swa_k_s.append(a2); swa_v_s.append(a3)
        xp = xp + mp
        xs = xs + ms
        zero_state = jnp.zeros((xp.shape[0], CONV_W - 1, D_FF), xp.dtype)
        fp, cp = conv_ffn(rms_norm(xp, g_ffn[i]), zero_state, w_ffn_in[i], ffn_conv_w[i], ffn_conv_b[i], w_ffn_out[i])
        fs, cs = conv_ffn(rms_norm(xs, g_ffn[i]), state_ffn_conv[i], w_ffn_in[i], ffn_conv_w[i], ffn_conv_b[i], w_ffn_out[i])
        conv_p.append(cp); conv_s.append(cs)
        xp = xp + fp
        xs = xs + fs
        xp = xp + ple(xp, p_prompt[i], g_ple[i], w_ple_gate[i], w_ple_proj[i])
        xs = xs + ple(xs, p_sample[i], g_ple[i], w_ple_gate[i], w_ple_proj[i])
    y_prompt = rms_norm(xp, g_final)
    y_sample = rms_norm(xs, g_final)
    return (y_prompt, y_sample,
            jnp.stack(mla_ckv_p), jnp.stack(mla_kr_p), jnp.stack(mla_ckv_s), jnp.stack(mla_kr_s),
            jnp.stack(sb_k_p), jnp.stack(sb_v_p), jnp.stack(sb_k_s), jnp.stack(sb_v_s),
            jnp.stack(swa_k_p), jnp.stack(swa_v_p), jnp.stack(swa_k_s), jnp.stack(swa_v_s),
            jnp.stack(conv_p), jnp.stack(conv_s))
```

```python
from contextlib import ExitStack
import numpy as np
import concourse.bass as bass
import concourse.mybir as mybir
from concourse.bass_utils import run_bass_kernel_spmd

F32 = mybir.dt.float32
BF16 = mybir.dt.bfloat16
AF = mybir.ActivationFunctionType
ALU = mybir.AluOpType

NCORES = 8
NPS = 4
NSS = 4
SEQ = 2048
TS = 64
D = 1024
DEPTH = 4
DFF = 2816
NJ = 22
EPS = 1e-6
MLA_SCALE = 192 ** -0.5
SB_SCALE = 0.125
SWA_SCALE = 0.125

SAME_ENG_SYNC = True
NDSEM = 24
ENGS = ["pe", "act", "dve", "pool", "sp"]
GRAN = 256


class Buf:
    __slots__ = ("ap", "lw", "rd", "rdd")

    def __init__(self, ap):
        self.ap = ap
        self.lw = None
        self.rd = {}
        self.rdd = []

    def __getitem__(self, k):
        return V([self], self.ap[k])

    @property
    def v(self):
        return V([self], self.ap)


class V:
    __slots__ = ("bufs", "ap")

    def __init__(self, bufs, ap):
        self.bufs = bufs
        self.ap = ap

    def __getitem__(self, k):
        return V(self.bufs, self.ap[k])

    def rearrange(self, pattern_, **kw):
        return V(self.bufs, self.ap.rearrange(pattern_, **kw))

    def bitcast(self, dt):
        return V(self.bufs, self.ap.bitcast(dt))


class Ins:
    __slots__ = ("fn", "deps", "dma", "sig", "sigval", "slot", "dval", "prevd")

    def __init__(self, fn, deps, dma):
        self.fn = fn
        self.deps = deps
        self.dma = dma
        self.sig = False
        self.sigval = 0
        self.slot = -1
        self.dval = 0
        self.prevd = 0


CFG = {"sample": True, "nprompt": NPS, "depth": DEPTH, "stop": None}


class Prog:
    def __init__(self, nc):
        self.nc = nc
        self.ins = {e: [] for e in ENGS}
        self.stopped = False

    def mark(self, label):
        if CFG["stop"] == label:
            self.stopped = True

    def emit(self, eng, fn, reads=(), writes=(), dma=False):
        if self.stopped:
            return None
        lst = self.ins[eng]
        idx = len(lst)
        node = (eng, idx)
        deps = set()
        for b in reads:
            if b.lw is not None:
                deps.add(b.lw)
        for b in writes:
            if b.lw is not None:
                deps.add(b.lw)
            for e, i in b.rd.items():
                deps.add((e, i))
            for n in b.rdd:
                deps.add(n)
        for b in reads:
            if dma:
                b.rdd.append(node)
            elif b.rd.get(eng, -1) < idx:
                b.rd[eng] = idx
        for b in writes:
            b.lw = node
            b.rd = {}
            b.rdd = []
        deps.discard(node)
        lst.append(Ins(fn, deps, dma))
        return node

    def op(self, eng, method, **kw):
        reads, writes, real = [], [], {}
        for k, v in kw.items():
            if isinstance(v, V):
                (writes if k in ("out", "accum_out") else reads).extend(v.bufs)
                real[k] = v.ap
            else:
                real[k] = v
        return self.emit(eng, lambda e: getattr(e, method)(**real), reads, writes)

    def mm(self, out, lhsT, rhs, start=True, stop=True, **kw):
        o, l, r = out.ap, lhsT.ap, rhs.ap
        return self.emit("pe", lambda e: e.matmul(o, lhsT=l, rhs=r, start=start, stop=stop, **kw),
                         lhsT.bufs + rhs.bufs, out.bufs)

    def tr(self, out, in_, ident):
        o, i, d = out.ap, in_.ap, ident.ap
        return self.emit("pe", lambda e: e.transpose(o, i, d), in_.bufs + ident.bufs, out.bufs)

    def dma(self, q, out, in_, nc_ok=False):
        reads, writes = [], []
        o, i = out, in_
        if isinstance(out, V):
            writes = out.bufs
            o = out.ap
        if isinstance(in_, V):
            reads = in_.bufs
            i = in_.ap
        if nc_ok:
            fn = lambda e: e.dma_start(out=o, in_=i, allow_slow_non_contiguous=True)
        else:
            fn = lambda e: e.dma_start(out=o, in_=i)
        return self.emit(q, fn, reads, writes, dma=True)

    def finalize(self, block, sems, dsems):
        ins = self.ins
        for e in ENGS:
            for x in ins[e]:
                for (f, j) in x.deps:
                    y = ins[f][j]
                    if y.dma:
                        continue
                    if f == e and (e == "pe" or not SAME_ENG_SYNC):
                        continue
                    y.sig = True
        final_dma = {}
        for e in ENGS:
            c = 0
            nd = 0
            last = [0] * NDSEM
            for x in ins[e]:
                if x.dma:
                    x.slot = nd % NDSEM
                    x.prevd = last[x.slot]
                    last[x.slot] += 16
                    x.dval = last[x.slot]
                    nd += 1
                elif x.sig:
                    c += 1
                    x.sigval = c
            final_dma[e] = last

        def run(e, eng):
            waited = {}
            for x in ins[e]:
                need = {}
                for (f, j) in x.deps:
                    y = ins[f][j]
                    if y.dma:
                        key = ("d", f, y.slot)
                        val = y.dval
                    else:
                        if f == e and (e == "pe" or not SAME_ENG_SYNC):
                            continue
                        key = ("c", f)
                        val = y.sigval
                    if need.get(key, 0) < val:
                        need[key] = val
                if x.dma and x.prevd > 0:
                    key = ("d", e, x.slot)
                    if need.get(key, 0) < x.prevd:
                        need[key] = x.prevd
                for key, val in need.items():
                    if waited.get(key, 0) >= val:
                        continue
                    waited[key] = val
                    sem = sems[key[1]] if key[0] == "c" else dsems[key[1]][key[2]]
                    eng.wait_ge(sem, val)
                r = x.fn(eng)
                if x.dma:
                    r.then_inc(dsems[e][x.slot], 16)
                elif x.sig:
                    r.then_inc(sems[e], 1)
            if e == "sp":
                for q in ENGS:
                    for s, val in enumerate(final_dma[q]):
                        if val > 0 and waited.get(("d", q, s), 0) < val:
                            eng.wait_ge(dsems[q][s], val)

        @block.tensor
        def _(eng):
            run("pe", eng)

        @block.scalar
        def _(eng):
            run("act", eng)

        @block.vector
        def _(eng):
            run("dve", eng)

        @block.gpsimd
        def _(eng):
            run("pool", eng)

        @block.sync
        def _(eng):
            run("sp", eng)


class Arena:
    def __init__(self, ap, nwords):
        self.ap = ap
        self.n = nwords // GRAN
        self.bufs = [Buf(ap[:, i * GRAN:(i + 1) * GRAN]) for i in range(self.n)]
        self.off = 0

    def reset(self):
        self.off = 0

    def alloc(self, shape, dt):
        free = int(np.prod(shape[1:]))
        words = free if dt == F32 else (free + 1) // 2
        g = (words + GRAN - 1) // GRAN
        g0 = self.off
        assert g0 + g <= self.n, ("arena overflow", g0, g, self.n)
        self.off += g
        ap = self.ap[0:shape[0], g0 * GRAN:g0 * GRAN + words]
        if dt != F32:
            ap = ap.bitcast(dt)[:, 0:free]
        if len(shape) > 2:
            names = "abcdefg"[:len(shape) - 1]
            pat = "p (" + " ".join(names) + ") -> p " + " ".join(names)
            ap = ap.rearrange(pat, **{n: shape[i + 1] for i, n in enumerate(names[:-1])})
        return V(self.bufs[g0:g0 + g], ap)


def build_program():
    nc = bass.Bass("TRN2", target_bir_lowering=False)
    dr = {}

    def din(name, shape):
        dr[name] = nc.dram_tensor(name, list(shape), F32, kind="ExternalInput").ap()

    def dout(name, shape):
        dr[name] = nc.dram_tensor(name, list(shape), F32, kind="ExternalOutput").ap()

    din("x_prompt", (NPS, SEQ, D)); din("x_sample", (NSS, TS, D))
    din("p_prompt", (DEPTH, NPS, SEQ, 256)); din("p_sample", (DEPTH, NSS, TS, 256))
    din("cache_mla_ckv", (2, NSS, 2048, 256)); din("cache_mla_krope", (2, NSS, 2048, 64))
    din("cache_sb_k", (1, NSS, 2048, 16, 64)); din("cache_sb_v", (1, NSS, 2048, 16, 64))
    din("cache_swa_k", (1, NSS, 128, 4, 64)); din("cache_swa_v", (1, NSS, 128, 4, 64))
    din("state_ffn_conv", (DEPTH, NSS, 2, DFF))
    din("g_mix", (DEPTH, D)); din("g_ffn", (DEPTH, D)); din("g_ple", (DEPTH, D)); din("g_final", (D,))
    din("w_mla_dq", (2, D, 384)); din("g_mla_q", (2, 384)); din("w_mla_uq", (2, 384, 3072))
    din("w_mla_dkv", (2, D, 320)); din("g_mla_kv", (2, 256))
    din("w_mla_uk", (2, 256, 16, 128)); din("w_mla_uv", (2, 256, 16, 128)); din("w_mla_o", (2, 2048, D))
    din("w_sb_qkv", (1, D, 3072)); din("w_sb_o", (1, D, D))
    din("w_swa_qkv", (1, D, 1536)); din("b_swa_qkv", (1, 1536)); din("swa_sinks", (1, 16)); din("w_swa_o", (1, D, D))
    din("w_ffn_in", (DEPTH, D, 2 * DFF)); din("ffn_conv_w", (DEPTH, 3, DFF)); din("ffn_conv_b", (DEPTH, DFF))
    din("w_ffn_out", (DEPTH, DFF, D)); din("w_ple_gate", (DEPTH, D, D)); din("w_ple_proj", (DEPTH, 256, D))
    din("c_ident", (128, 128)); din("c_cost", (17 * 128, 32)); din("c_sint", (17 * 128, 32))
    din("c_cos2", (128, 2560)); din("c_sin2", (128, 2560)); din("c_msk", (128, 1408))
    dout("y_prompt", (NPS, SEQ, D)); dout("y_sample", (NSS, TS, D))
    dout("mla_ckv_p", (2, NPS, SEQ, 256)); dout("mla_krope_p", (2, NPS, SEQ, 64))
    dout("mla_ckv_s", (2, NSS, TS, 256)); dout("mla_krope_s", (2, NSS, TS, 64))
    dout("sb_k_p", (1, NPS, SEQ, 16, 64)); dout("sb_v_p", (1, NPS, SEQ, 16, 64))
    dout("sb_k_s", (1, NSS, TS, 16, 64)); dout("sb_v_s", (1, NSS, TS, 16, 64))
    dout("swa_k_p", (1, NPS, 128, 4, 64)); dout("swa_v_p", (1, NPS, 128, 4, 64))
    dout("swa_k_s", (1, NSS, 128, 4, 64)); dout("swa_v_s", (1, NSS, 128, 4, 64))
    dout("ffn_conv_p", (DEPTH, NPS, 2, DFF)); dout("ffn_conv_s", (DEPTH, NSS, 2, DFF))

    es = ExitStack()
    with es:
        def sbt(name, shape, dt):
            return es.enter_context(nc.sbuf_tensor(name, shape, dt))

        P = Prog(nc)
        xt = sbt("xres", [128, 16, D], F32)
        X = [Buf(xt[:, i, :]) for i in range(16)]
        ident_f = Buf(sbt("identf", [128, 128], F32)[:])
        ident = Buf(sbt("identb", [128, 128], BF16)[:])
        cost = Buf(sbt("cost", [128, 17, 32], F32)[:])
        sint = Buf(sbt("sint", [128, 17, 32], F32)[:])
        msk = Buf(sbt("msk", [128, 1408], BF16)[:])
        gb = Buf(sbt("gb", [128, D], F32)[:])
        ssb = Buf(sbt("ssb", [128, 16], F32)[:])
        rstd = Buf(sbt("rstd", [128, 16], F32)[:])
        sm1 = Buf(sbt("sm1", [128, 8], F32)[:])
        own_ckvT = Buf(sbt("ownck", [128, 2, 256], BF16)[:]).v
        own_krT = Buf(sbt("ownkr", [64, 256], BF16)[:]).v
        AW = 130 * GRAN
        arena = Arena(sbt("arena", [128, AW], F32)[:], AW)
        PS = [Buf(es.enter_context(nc.psum_tensor(f"ps{i}", [128, 512], F32))[:]) for i in range(8)]
        sems = {e: es.enter_context(nc.semaphore("s_" + e)) for e in ENGS}
        dsems = {e: [es.enter_context(nc.semaphore(f"d_{e}{i}")) for i in range(NDSEM)] for e in ["sp", "pool", "act"]}
        block = es.enter_context(nc.Block())

        tri01 = msk[:, 0:128]
        negtri = msk[:, 128:256]
        ones = msk[:, 256:384]
        zeros = msk[:, 384:896]
        msk_s = msk[:, 896:1408]

        P.dma("sp", ident_f.v, dr["c_ident"])
        P.op("dve", "tensor_copy", out=ident.v, in_=ident_f.v)
        P.dma("sp", cost.v, dr["c_cost"].rearrange("(t p) f -> p t f", p=128))
        P.dma("sp", sint.v, dr["c_sint"].rearrange("(t p) f -> p t f", p=128))
        P.dma("pool", msk.v, dr["c_msk"])

        def wview(w2d):
            return w2d.rearrange("(c p) n -> p c n", p=128)

        def psb(i):
            return PS[i].v.bitcast(BF16)

        def rstd_from_ms(ntl, col0):
            P.op("act", "activation", out=rstd[:, col0:col0 + ntl], in_=ssb[:, col0:col0 + ntl], func=AF.Ln, bias=EPS, scale=1.0)
            P.op("act", "activation", out=rstd[:, col0:col0 + ntl], in_=rstd[:, col0:col0 + ntl], func=AF.Exp, scale=-0.5)

        def norm_T(tiles, g_ap, hT, junk, hb2):
            P.dma("sp", gb.v, g_ap.partition_broadcast(128))
            n = len(tiles)
            for k, t in enumerate(tiles):
                P.op("act", "activation", out=hb2[k % 2], in_=X[t].v, func=AF.Square, scale=1.0 / 32.0, accum_out=ssb[:, k:k + 1])
            rstd_from_ms(n, 0)
            for k, t in enumerate(tiles):
                hb = hb2[k % 2]
                P.op("dve", "scalar_tensor_tensor", out=hb, in0=X[t].v, scalar=rstd[:, k:k + 1], in1=gb.v, op0=ALU.mult, op1=ALU.mult)
                bank = 6 + (k % 2)
                pv = psb(bank).rearrange("p (c t) -> p c t", c=8)
                for c in range(8):
                    P.tr(pv[:, c, :], hb[:, c * 128:(c + 1) * 128], ident.v)
                P.op("act", "copy", out=hT[:, :, k * 128:(k + 1) * 128], in_=pv)

        def add_to_x(t, dh, ps):
            P.op("dve", "tensor_tensor", out=X[t][:, dh * 512:(dh + 1) * 512], in0=X[t][:, dh * 512:(dh + 1) * 512], in1=ps, op=ALU.add)

        def ple(layer, tiles, p_src):
            arena.reset()
            nt = len(tiles)
            TH = min(nt, 8)
            hT = arena.alloc([128, 8, TH * 128], BF16)
            junk = arena.alloc([128, D], BF16)
            hb2 = [arena.alloc([128, D], BF16) for _ in range(2)]
            wg = arena.alloc([128, 8, D], BF16)
            wp = arena.alloc([128, 2, D], BF16)
            pb2 = [arena.alloc([128, 256], BF16) for _ in range(2)]
            pT2 = [arena.alloc([128, 2, 128], BF16) for _ in range(2)]
            sg2 = [arena.alloc([128, 512], F32) for _ in range(2)]
            tt2 = [arena.alloc([128, 512], F32) for _ in range(2)]
            P.dma("pool", wg, wview(dr["w_ple_gate"][layer]))
            P.dma("pool", wp, wview(dr["w_ple_proj"][layer]))
            for h0 in range(0, nt, TH):
                sub = tiles[h0:h0 + TH]
                norm_T(sub, dr["g_ple"][layer], hT, junk, hb2)
                for k, t in enumerate(sub):
                    pb = pb2[k % 2]
                    pT = pT2[k % 2]
                    P.dma("pool", pb, p_src(t))
                    pv = psb(5).rearrange("p (c t) -> p c t", c=8)
                    for c in range(2):
                        P.tr(pv[:, c, :], pb[:, c * 128:(c + 1) * 128], ident.v)
                    P.op("act", "copy", out=pT, in_=pv[:, 0:2, :])
                    for dh in range(2):
                        pa = PS[dh].v
                        pbk = PS[2 + dh].v
                        for c in range(8):
                            P.mm(pa, hT[:, c, k * 128:(k + 1) * 128], wg[:, c, dh * 512:(dh + 1) * 512], start=(c == 0), stop=(c == 7))
                        for c in range(2):
                            P.mm(pbk, pT[:, c, :], wp[:, c, dh * 512:(dh + 1) * 512], start=(c == 0), stop=(c == 1))
                        sg = sg2[dh]
                        tt = tt2[dh]
                        P.op("act", "activation", out=sg, in_=pa, func=AF.Sigmoid)
                        P.op("dve", "tensor_tensor", out=tt, in0=sg, in1=pbk, op=ALU.mult)
                        P.op("dve", "tensor_tensor", out=X[t][:, dh * 512:(dh + 1) * 512], in0=X[t][:, dh * 512:(dh + 1) * 512], in1=tt, op=ALU.add)

        def ffn(layer, tiles, nseg, L, state_src, conv_dst):
            arena.reset()
            nt = len(tiles)
            TH = min(nt, 8)
            NB = nseg * L
            hT = arena.alloc([128, 8, TH * 128], BF16)
            junk = None
            hb2 = [arena.alloc([128, D], BF16) for _ in range(2)]
            actT = arena.alloc([128, NJ, TH * 128], BF16)
            wi2 = [arena.alloc([128, 8, 512], BF16) for _ in range(2)]
            wo2 = [arena.alloc([128, NJ, 256], BF16) for _ in range(2)]
            G2 = [arena.alloc([128, nseg, L + 2], F32) for _ in range(2)]
            t12 = [arena.alloc([128, nseg, L], F32) for _ in range(2)]
            ge2 = [arena.alloc([128, nseg, L], F32) for _ in range(2)]
            carry = arena.alloc([128, NJ, nseg, 2], F32)
            cw = arena.alloc([128, NJ, 3], F32)
            cb = arena.alloc([128, NJ], F32)
            for r in range(3):
                P.dma("sp", cw[:, :, r], dr["ffn_conv_w"][layer, r].rearrange("(j p) -> p j", p=128), nc_ok=True)
            P.dma("sp", cb, dr["ffn_conv_b"][layer].rearrange("(j p) -> p j", p=128), nc_ok=True)
            if state_src is None:
                P.op("dve", "memset", out=carry, constant=0.0) if False else P.emit("dve", lambda e, a=carry.ap: e.memset(a, 0.0), [], carry.bufs)
            else:
                for sg in range(nseg):
                    for r in range(2):
                        P.dma("sp", carry[:, :, sg, r], state_src[sg, r].rearrange("(j p) -> p j", p=128), nc_ok=True)
            win = wview(dr["w_ffn_in"][layer])
            wov = dr["w_ffn_out"][layer].rearrange("(j p) n -> p j n", p=128)
            uc = 0
            for h0 in range(0, nt, TH):
                sub = tiles[h0:h0 + TH]
                ntok = len(sub) * 128
                norm_T(sub, dr["g_ffn"][layer], hT, junk, hb2)
                nblk = ntok // NB
                for grp in range(NJ // 2):
                    wi = wi2[grp % 2]
                    P.dma("pool", wi[:, :, 0:256], win[:, :, grp * 256:(grp + 1) * 256])
                    P.dma("pool", wi[:, :, 256:512], win[:, :, DFF + grp * 256:DFF + (grp + 1) * 256])
                    for u in range(2):
                        j = grp * 2 + u
                        for blk in range(nblk):
                            cs = slice(blk * NB, (blk + 1) * NB)
                            pg = PS[uc % 2].v[:, 0:NB]
                            pu = PS[2 + uc % 2].v[:, 0:NB]
                            G = G2[uc % 2]
                            t1 = t12[uc % 2]
                            ge = ge2[uc % 2]
                            uc += 1
                            for c in range(8):
                                P.mm(pg, wi[:, c, u * 128:(u + 1) * 128], hT[:, c, cs], start=(c == 0), stop=(c == 7))
                            for c in range(8):
                                P.mm(pu, wi[:, c, 256 + u * 128:256 + (u + 1) * 128], hT[:, c, cs], start=(c == 0), stop=(c == 7))
                            P.op("act", "copy", out=G[:, :, 0:2], in_=carry[:, j])
                            P.op("act", "copy", out=G[:, :, 2:L + 2], in_=pg.rearrange("p (s l) -> p s l", s=nseg))
                            P.op("dve", "tensor_copy", out=carry[:, j], in_=G[:, :, L:L + 2])
                            P.op("act", "activation", out=t1, in_=G[:, :, 2:L + 2], func=AF.Identity, scale=cw[:, j, 2:3], bias=cb[:, j:j + 1])
                            P.op("dve", "scalar_tensor_tensor", out=t1, in0=G[:, :, 1:L + 1], scalar=cw[:, j, 1:2], in1=t1, op0=ALU.mult, op1=ALU.add)
                            P.op("dve", "scalar_tensor_tensor", out=t1, in0=G[:, :, 0:L], scalar=cw[:, j, 0:1], in1=t1, op0=ALU.mult, op1=ALU.add)
                            P.op("act", "activation", out=ge, in_=t1, func=AF.Gelu)
                            P.op("dve", "tensor_tensor", out=actT[:, j, cs].rearrange("p (s l) -> p s l", s=nseg), in0=ge, in1=pu.rearrange("p (s l) -> p s l", s=nseg), op=ALU.mult)
                for dq in range(4):
                    wo = wo2[dq % 2]
                    P.dma("pool", wo, wov[:, :, dq * 256:(dq + 1) * 256])
                    for k, t in enumerate(sub):
                        po = PS[4 + (dq * 8 + k) % 2].v[:, 0:256]
                        for j in range(NJ):
                            P.mm(po, actT[:, j, k * 128:(k + 1) * 128], wo[:, j, :], start=(j == 0), stop=(j == NJ - 1))
                        xs = X[t][:, dq * 256:(dq + 1) * 256]
                        P.op("dve", "tensor_tensor", out=xs, in0=xs, in1=po, op=ALU.add)
            for sg in range(nseg):
                for r in range(2):
                    P.dma("sp", conv_dst[sg, r].rearrange("(j p) -> p j", p=128), carry[:, :, sg, r], nc_ok=True)

        def softmax_group(chunks, N, scale, ndv, E3, rec, out_fn, sink=None, zb=0):
            den = PS[2].v[:, 0:N]
            O = [PS[3 + d].v[:, 0:N] for d in range(ndv)]
            zr = zeros[:, 0:N]
            P.mm(den, zeros[:, 0:128], zr, start=True, stop=False)
            for d in range(ndv):
                P.mm(O[d], zeros[:, 0:128], zr, start=True, stop=False)
            nch = len(chunks)
            for ci, ch in enumerate(chunks):
                nk, c0, c1 = ch["nk"], ch["c0"], ch["c1"]
                z = PS[(zb + ci) % 2].v[0:nk, c0:c1]
                E = E3[ci % 3][0:nk, c0:c1]
                kl = ch["kl"]
                for i, (l, r) in enumerate(kl):
                    P.mm(z, l, r, start=(i == 0), stop=(i == len(kl) - 1))
                P.op("act", "activation", out=E, in_=z, func=AF.Exp, scale=scale)
                for (p0, p1, a, b) in ch["fix"]:
                    P.emit("dve", lambda e, ap=E3[ci % 3][p0:p1, a:b].ap: e.memset(ap, 0.0), [], E.bufs)
                last = (ci == nch - 1)
                P.mm(den[:, c0:c1], ones[0:nk, :], E, start=False, stop=last, skip_group_check=True)
                for d in range(ndv):
                    P.mm(O[d][:, c0:c1], ch["vl"][d], E, start=False, stop=last, skip_group_check=True)
            if sink is not None:
                P.op("dve", "tensor_scalar", out=rec[:, 0:N], in0=den, scalar1=sink, scalar2=None, op0=ALU.add)
                P.op("dve", "reciprocal", out=rec[:, 0:N], in_=rec[:, 0:N])
            else:
                P.op("dve", "reciprocal", out=rec[:, 0:N], in_=den)
            out_fn(O, rec[:, 0:N])

        def mla(layer, tiles, sample):
            jl = layer // 3
            arena.reset()
            nt = len(tiles)
            T = nt * 128
            NK = 2112 if sample else 2048
            TW = 512 if sample else 2048
            tc0 = 2048 if sample else 0
            ckvT = arena.alloc([128, 2, NK], BF16)
            krT = arena.alloc([64, NK], BF16)
            ckvt = arena.alloc([128, 17 if sample else 16, 256], BF16)
            cqT = arena.alloc([128, 3, T], BF16)
            cos2 = arena.alloc([64, TW], BF16)
            sin2 = arena.alloc([64, TW], BF16)
            wukT = arena.alloc([128, 16, 256], BF16)
            wuv = arena.alloc([128, 2, 2048], BF16)
            mark = arena.off
            wo = arena.alloc([128, 16, D], BF16)
            OVT = arena.alloc([128, 16, 512 if not sample else T], BF16)
            wq4 = [arena.alloc([128, 3, 256], BF16) for _ in range(3)]
            E3 = [arena.alloc([128, 512], BF16) for _ in range(3)]
            rec = arena.alloc([128, 512], F32)
            qn = arena.alloc([128, 512], BF16)
            qr = arena.alloc([64, 512], BF16)
            qlat = arena.alloc([128, 2, 512], BF16)
            olat = arena.alloc([128, 2, 512], BF16)
            tq2 = [arena.alloc([64, 512], F32) for _ in range(2)]
            krc = arena.alloc([128, 16, 64], BF16) if sample else None
            arena.off = mark
            TH = min(nt, 8)
            hT = arena.alloc([128, 8, TH * 128], BF16)
            hb2 = [arena.alloc([128, D], BF16) for _ in range(2)]
            wdq = arena.alloc([128, 8, 384], BF16)
            wdkv = arena.alloc([128, 8, 320], BF16)
            gq = arena.alloc([128, 384], F32)
            gkv = arena.alloc([128, 256], F32)
            wuk = arena.alloc([128, 2, 2048], BF16)
            ckvf2 = [arena.alloc([128, 256], F32) for _ in range(2)]
            krf2 = [arena.alloc([128, 64], F32) for _ in range(2)]
            krt = arena.alloc([128, 4, 32], F32)
            krb2 = [arena.alloc([128, 64], BF16) for _ in range(2)]
            cqb2 = [arena.alloc([128, 384], BF16) for _ in range(2)]
            ckb_s = arena.alloc([128, 256], BF16)
            sqj = arena.alloc([128, 640], BF16)
            P.dma("pool", cos2, dr["c_cos2"][0:64, tc0:tc0 + TW])
            P.dma("pool", sin2, dr["c_sin2"][0:64, tc0:tc0 + TW])
            P.dma("pool", wdq, wview(dr["w_mla_dq"][jl]))
            P.dma("pool", wdkv, wview(dr["w_mla_dkv"][jl]))
            P.dma("sp", gq, dr["g_mla_q"][jl].partition_broadcast(128))
            P.dma("sp", gkv, dr["g_mla_kv"][jl].partition_broadcast(128))
            P.dma("pool", wuk, wview(dr["w_mla_uk"][jl].rearrange("l h n -> l (h n)")))
            P.dma("pool", wuv, wview(dr["w_mla_uv"][jl].rearrange("l h n -> l (h n)")))
            for h in range(16):
                pv = psb(5).rearrange("p (c t) -> p c t", c=8)
                for lc in range(2):
                    P.tr(pv[:, lc, :], wuk[:, lc, h * 128:(h + 1) * 128], ident.v)
                P.op("act", "copy", out=wukT[:, h, :].rearrange("p (c t) -> p c t", c=2), in_=pv[:, 0:2, :])
            for h0 in range(0, nt, TH):
                sub = tiles[h0:h0 + TH]
                norm_T(sub, dr["g_mix"][layer], hT, None, hb2)
                for k, t in enumerate(sub):
                    gt = h0 + k
                    ptile = 16 if sample else gt
                    hs = [hT[:, c, k * 128:(k + 1) * 128] for c in range(8)]
                    pkv = PS[0].v[:, 0:320]
                    pcq = PS[1].v[:, 0:384]
                    for c in range(8):
                        P.mm(pkv, hs[c], wdkv[:, c, :], start=(c == 0), stop=(c == 7))
                    for c in range(8):
                        P.mm(pcq, hs[c], wdq[:, c, :], start=(c == 0), stop=(c == 7))
                    P.op("act", "activation", out=sqj[:, 0:256], in_=pkv[:, 0:256], func=AF.Square, scale=1.0 / 16.0, accum_out=ssb[:, 8:9])
                    P.op("act", "activation", out=sqj[:, 256:640], in_=pcq, func=AF.Square, scale=384 ** -0.5, accum_out=ssb[:, 9:10])
                    rstd_from_ms(2, 8)
                    ckvf = ckvf2[k % 2]
                    krf = krf2[k % 2]
                    krb = krb2[k % 2]
                    cqb = cqb2[k % 2]
                    P.op("dve", "scalar_tensor_tensor", out=ckvf, in0=pkv[:, 0:256], scalar=rstd[:, 8:9], in1=gkv, op0=ALU.mult, op1=ALU.mult)
                    P.op("dve", "scalar_tensor_tensor", out=cqb, in0=pcq, scalar=rstd[:, 9:10], in1=gq, op0=ALU.mult, op1=ALU.mult)
                    x1 = pkv[:, 256:288]
                    x2 = pkv[:, 288:320]
                    cs_, sn_ = cost[:, ptile, :], sint[:, ptile, :]
                    P.op("dve", "tensor_tensor", out=krt[:, 0, :], in0=x1, in1=cs_, op=ALU.mult)
                    P.op("dve", "tensor_tensor", out=krt[:, 1, :], in0=x2, in1=sn_, op=ALU.mult)
                    P.op("dve", "tensor_tensor", out=krt[:, 2, :], in0=x2, in1=cs_, op=ALU.mult)
                    P.op("dve", "tensor_tensor", out=krt[:, 3, :], in0=x1, in1=sn_, op=ALU.mult)
                    P.op("dve", "tensor_tensor", out=krf[:, 0:32], in0=krt[:, 0, :], in1=krt[:, 1, :], op=ALU.subtract)
                    P.op("dve", "tensor_tensor", out=krf[:, 32:64], in0=krt[:, 2, :], in1=krt[:, 3, :], op=ALU.add)
                    if not sample:
                        si = tiles_seq
                        P.dma("sp", dr["mla_ckv_p"][jl, si, gt * 128:(gt + 1) * 128, :], ckvf)
                        P.dma("sp", dr["mla_krope_p"][jl, si, gt * 128:(gt + 1) * 128, :], krf)
                        ckb = ckvt[:, gt, :]
                    else:
                        P.dma("sp", dr["mla_ckv_s"][jl, 2 * gt:2 * gt + 2].rearrange("s t l -> (s t) l"), ckvf)
                        P.dma("sp", dr["mla_krope_s"][jl, 2 * gt:2 * gt + 2].rearrange("s t l -> (s t) l"), krf)
                        ckb = ckb_s
                    P.op("act", "copy", out=ckb, in_=ckvf)
                    P.op("act", "copy", out=krb, in_=krf)
                    pv = psb(5).rearrange("p (c t) -> p c t", c=8)
                    P.tr(pv[:, 0, :], ckb[:, 0:128], ident.v)
                    P.tr(pv[:, 1, :], ckb[:, 128:256], ident.v)
                    P.tr(pv[0:64, 2, :], krb, ident.v)
                    for c in range(3):
                        P.tr(pv[:, 3 + c, :], cqb[:, c * 128:(c + 1) * 128], ident.v)
                    if not sample:
                        kc0 = gt * 128
                        P.op("act", "copy", out=ckvT[:, :, kc0:kc0 + 128], in_=pv[:, 0:2, :])
                        P.op("act", "copy", out=krT[:, kc0:kc0 + 128], in_=pv[0:64, 2, :])
                    else:
                        P.op("act", "copy", out=own_ckvT[:, :, gt * 128:(gt + 1) * 128], in_=pv[:, 0:2, :])
                        P.op("act", "copy", out=own_krT[:, gt * 128:(gt + 1) * 128], in_=pv[0:64, 2, :])
                    P.op("act", "copy", out=cqT[:, :, gt * 128:(gt + 1) * 128], in_=pv[:, 3:6, :])
            P.mark("mla1")
            P.dma("pool", wo, wview(dr["w_mla_o"][jl]))
            uqv = dr["w_mla_uq"][jl].rearrange("(c p) n -> p c n", p=128)
            wqi = [0]

            def load_wq(h):
                wq = wq4[wqi[0] % 3]
                wqi[0] += 1
                P.dma("pool", wq[:, :, 0:192], uqv[:, :, h * 192:(h + 1) * 192])
                P.dma("pool", wq[:, :, 192:224], uqv[:, :, h * 192 + 160:h * 192 + 192])
                P.dma("pool", wq[:, :, 224:256], uqv[:, :, h * 192 + 128:h * 192 + 160])
                return wq

            def qproj(h, qcols, ocols):
                wq = load_wq(h)
                pq = PS[5].v[:, ocols]
                pr = PS[6].v[0:64, ocols]
                pt = PS[7].v[0:64, ocols]
                for c in range(3):
                    P.mm(pq, wq[:, c, 0:128], cqT[:, c, qcols], start=(c == 0), stop=(c == 2))
                for c in range(3):
                    P.mm(pr, wq[:, c, 128:192], cqT[:, c, qcols], start=(c == 0), stop=(c == 2))
                for c in range(3):
                    P.mm(pt, wq[:, c, 192:256], cqT[:, c, qcols], start=(c == 0), stop=(c == 2))

            def qfinish(N, tcols):
                P.op("act", "copy", out=qn[:, 0:N], in_=PS[5].v[:, 0:N])
                P.op("dve", "tensor_tensor", out=tq2[0][:, 0:N], in0=PS[6].v[0:64, 0:N], in1=cos2[:, tcols], op=ALU.mult)
                P.op("dve", "tensor_tensor", out=tq2[1][:, 0:N], in0=PS[7].v[0:64, 0:N], in1=sin2[:, tcols], op=ALU.mult)
                P.op("dve", "tensor_tensor", out=qr[:, 0:N], in0=tq2[0][:, 0:N], in1=tq2[1][:, 0:N], op=ALU.add)

            def qlat_from_qn(hlist, N):
                w = N // len(hlist)
                for lc in range(2):
                    pl = PS[5 + lc].v
                    for i, h in enumerate(hlist):
                        P.mm(pl[:, i * w:(i + 1) * w], wukT[:, h, lc * 128:(lc + 1) * 128], qn[:, i * w:(i + 1) * w], start=True, stop=True)
                    P.op("act", "copy", out=qlat[:, lc, 0:N], in_=pl[:, 0:N])

            if not sample:
                for qb in range(T // 512):
                    qc = slice(qb * 512, (qb + 1) * 512)
                    for h in range(16):
                        qproj(h, qc, slice(0, 512))
                        qfinish(512, qc)
                        qlat_from_qn([h], 512)
                        chunks = []
                        for kc in range(4 * qb + 4):
                            j = kc - 4 * qb
                            c0 = 128 * j if j >= 0 else 0
                            kcs = slice(kc * 128, (kc + 1) * 128)
                            chunks.append(dict(
                                kl=[(ckvT[:, 0, kcs], qlat[:, 0, c0:512]), (ckvT[:, 1, kcs], qlat[:, 1, c0:512]), (krT[:, kcs], qr[:, c0:512])],
                                nk=128, c0=c0, c1=512,
                                fix=[(64, 128, c0, c0 + 64)] if j >= 0 else [],
                                vl=[ckvt[:, kc, 0:128], ckvt[:, kc, 128:256]]))

                        def out_fn(O, r, h=h):
                            for lc in range(2):
                                P.op("dve", "tensor_tensor", out=olat[:, lc, :], in0=O[lc], in1=r, op=ALU.mult)
                            pv_ = PS[7].v
                            for lc in range(2):
                                P.mm(pv_, wuv[:, lc, h * 128:(h + 1) * 128], olat[:, lc, :], start=(lc == 0), stop=(lc == 1))
                            P.op("act", "copy", out=OVT[:, h, :], in_=pv_)
                        softmax_group(chunks, 512, MLA_SCALE, 2, E3, rec, out_fn, zb=h)
                    for k4 in range(4):
                        t = tiles[qb * 4 + k4]
                        for dh in range(2):
                            po = PS[5 + dh].v
                            for h in range(16):
                                P.mm(po, OVT[:, h, k4 * 128:(k4 + 1) * 128], wo[:, h, dh * 512:(dh + 1) * 512], start=(h == 0), stop=(h == 15))
                            add_to_x(t, dh, po)
            else:
                for s in range(NSS):
                    P.dma("pool", ckvt[:, 0:16, :], dr["cache_mla_ckv"][jl, s].rearrange("(t p) l -> p t l", p=128))
                    P.dma("pool", krc, dr["cache_mla_krope"][jl, s].rearrange("(t p) f -> p t f", p=128))
                    for kt in range(16):
                        pv = psb(5 + kt % 2).rearrange("p (c t) -> p c t", c=8)
                        P.tr(pv[:, 0, :], ckvt[:, kt, 0:128], ident.v)
                        P.tr(pv[:, 1, :], ckvt[:, kt, 128:256], ident.v)
                        P.tr(pv[0:64, 2, :], krc[:, kt, :], ident.v)
                        P.op("act", "copy", out=ckvT[:, :, kt * 128:(kt + 1) * 128], in_=pv[:, 0:2, :])
                        P.op("act", "copy", out=krT[:, kt * 128:(kt + 1) * 128], in_=pv[0:64, 2, :])
                    oc = slice(s * 64, (s + 1) * 64)
                    P.op("act", "copy", out=ckvT[:, :, 2048:2112], in_=own_ckvT[:, :, oc])
                    P.op("act", "copy", out=krT[:, 2048:2112], in_=own_krT[:, oc])
                    pv = psb(5).rearrange("p (c t) -> p c t", c=8)
                    for lc in range(2):
                        P.tr(pv[0:64, lc, :], ckvT[:, lc, 2048:2112], ident.v)
                    P.op("act", "copy", out=ckvt[0:64, 16, :].rearrange("p (c t) -> p c t", c=2), in_=pv[0:64, 0:2, :])
                    for g in range(2):
                        hl = list(range(8 * g, 8 * g + 8))
                        for i, h in enumerate(hl):
                            qproj(h, oc, slice(i * 64, (i + 1) * 64))
                        qfinish(512, slice(0, 512))
                        qlat_from_qn(hl, 512)
                        chunks = []
                        for kc in range(17):
                            nk = 128 if kc < 16 else 64
                            kcs = slice(kc * 128, kc * 128 + nk)
                            chunks.append(dict(
                                kl=[(ckvT[:, 0, kcs], qlat[:, 0, :]), (ckvT[:, 1, kcs], qlat[:, 1, :]), (krT[:, kcs], qr[:, :])],
                                nk=nk, c0=0, c1=512, fix=[],
                                vl=[ckvt[0:nk, kc, 0:128], ckvt[0:nk, kc, 128:256]]))

                        def out_fn(O, r, hl=hl, s=s, g=g):
                            for lc in range(2):
                                P.op("dve", "tensor_tensor", out=olat[:, lc, :], in0=O[lc], in1=r, op=ALU.mult)
                            pv_ = PS[7].v
                            for i, h in enumerate(hl):
                                for lc in range(2):
                                    P.mm(pv_[:, i * 64:(i + 1) * 64], wuv[:, lc, h * 128:(h + 1) * 128], olat[:, lc, i * 64:(i + 1) * 64], start=(lc == 0), stop=(lc == 1))
                            P.op("act", "copy", out=OVT[:, 8 * g:8 * g + 8, s * 64:(s + 1) * 64], in_=pv_.rearrange("p (h q) -> p h q", h=8))
                        softmax_group(chunks, 512, MLA_SCALE, 2, E3, rec, out_fn, zb=g)
                for k, t in enumerate(tiles):
                    for dh in range(2):
                        po = PS[5 + dh].v
                        for h in range(16):
                            P.mm(po, OVT[:, h, k * 128:(k + 1) * 128], wo[:, h, dh * 512:(dh + 1) * 512], start=(h == 0), stop=(h == 15))
                        add_to_x(t, dh, po)

        tiles_seq = 0

        def swa(layer, tiles, sample):
            arena.reset()
            nt = len(tiles)
            T = nt * 128
            TW = 512 if sample else 2048
            tc0 = 2048 if sample else 0
            BW = 256 if sample else 512
            KX = 128 if sample else 0
            wo = arena.alloc([128, 8, D], BF16)
            cos2 = arena.alloc([128, TW], BF16)
            sin2 = arena.alloc([128, TW], BF16)
            bq = arena.alloc([128, 8], F32)
            bqr = arena.alloc([128, 8], F32)
            bk = arena.alloc([128, 4], F32)
            bkr = arena.alloc([128, 4], F32)
            bvs = arena.alloc([128, 256], F32)
            bv = arena.alloc([128, 512], F32)
            sk = arena.alloc([128, 16], F32)
            kT = arena.alloc([128, 4, T + KX], BF16)
            Vd = arena.alloc([128, nt + (1 if sample else 0), 512], BF16)
            qT = arena.alloc([128, 8, BW], BF16)
            OT = arena.alloc([128, 8, BW], BF16)
            hT = arena.alloc([128, 8, BW], BF16)
            hb2 = [arena.alloc([128, D], BF16) for _ in range(2)]
            E3 = [arena.alloc([128, 512], BF16) for _ in range(3)]
            rec = arena.alloc([128, 512], F32)
            ta = [arena.alloc([128, 512], F32) for _ in range(2)]
            kf = arena.alloc([128, 4, 128], F32)
            vf = arena.alloc([128, 256], F32)
            kto = arena.alloc([128, 2, 256], F32) if sample else arena.alloc([128, 1, 256], F32)
            vown = arena.alloc([64, 512], BF16) if sample else None
            knat_s = arena.alloc([128, 8, 512], BF16) if sample else None
            mark = arena.off
            wk = arena.alloc([128, 8, 4, 2, 64], BF16)
            wkr = arena.alloc([128, 8, 4, 4, 32], BF16)
            wv = arena.alloc([128, 8, 4, 2, 64], BF16)
            arena.off = mark
            wqh = arena.alloc([128, 8, 512], BF16)
            wqrh = arena.alloc([128, 8, 8, 2, 32], BF16)
            W = dr["w_swa_qkv"][0]
            wv_ = wview(W)
            w5 = W.rearrange("(c p) (h two f) -> p c h two f", p=128, two=2, f=32)
            w4 = W.rearrange("(c p) (h d) -> p c h d", p=128, d=64)
            P.dma("pool", wo, wview(dr["w_swa_o"][0]))
            P.dma("pool", cos2, dr["c_cos2"][:, tc0:tc0 + TW])
            P.dma("pool", sin2, dr["c_sin2"][:, tc0:tc0 + TW])
            B = dr["b_swa_qkv"][0]
            P.dma("sp", bq, B[0:1024].rearrange("(c p) -> p c", p=128), nc_ok=True)
            b3 = B.rearrange("(h two f) -> two f h", two=2, f=32)
            for hh in range(2):
                for two in range(2):
                    r0 = hh * 64 + two * 32
                    P.dma("sp", bqr[r0:r0 + 32, :], b3[1 - two, :, hh:16:2], nc_ok=True)
                    P.dma("sp", bk[r0:r0 + 32, :], b3[two, :, 16:20], nc_ok=True)
                    P.dma("sp", bkr[r0:r0 + 32, :], b3[1 - two, :, 16:20], nc_ok=True)
            P.dma("sp", bvs, B[1280:1536].partition_broadcast(128))
            bv4 = bv.rearrange("p (h d f) -> p h d f", h=4, d=2)
            for dup in range(2):
                P.op("dve", "tensor_copy", out=bv4[:, :, dup, :], in_=bvs.rearrange("p (h f) -> p h f", h=4))
            P.dma("sp", sk, dr["swa_sinks"][0].partition_broadcast(128))
            P.op("act", "activation", out=sk, in_=sk, func=AF.Exp)

            def load_kv_w():
                knat = knat_s if sample else qT
                P.dma("pool", knat, wv_[:, :, 1024:1536])
                kn4 = knat[:, :, 0:256].rearrange("p c (h d) -> p c h d", h=4)
                vn4 = knat[:, :, 256:512].rearrange("p c (h d) -> p c h d", h=4)
                kn5 = knat[:, :, 0:256].rearrange("p c (h two f) -> p c h two f", h=4, two=2)
                for dup in range(2):
                    P.op("pool", "tensor_copy", out=wk[:, :, :, dup, :], in_=kn4)
                    P.op("pool", "tensor_copy", out=wv[:, :, :, dup, :], in_=vn4)
                    for two in range(2):
                        P.op("pool", "tensor_copy", out=wkr[:, :, :, dup * 2 + two, :], in_=kn5[:, :, :, 1 - two, :])

            def load_q_w(half):
                P.dma("pool", wqh, wv_[:, :, half * 512:(half + 1) * 512])
                q5 = wqh.rearrange("p c (h two f) -> p c h two f", h=8, two=2)
                for two in range(2):
                    P.op("pool", "tensor_copy", out=wqrh[:, :, :, two, :], in_=q5[:, :, :, 1 - two, :])

            def rope_evac(out, pz, pzr, b, br, tcols, n, f32out=None):
                P.op("act", "activation", out=ta[0][:, 0:n], in_=pz, func=AF.Identity, bias=b, scale=1.0)
                P.op("act", "activation", out=ta[1][:, 0:n], in_=pzr, func=AF.Identity, bias=br, scale=1.0)
                P.op("dve", "tensor_tensor", out=ta[0][:, 0:n], in0=ta[0][:, 0:n], in1=cos2[:, tcols], op=ALU.mult)
                P.op("dve", "tensor_tensor", out=ta[1][:, 0:n], in0=ta[1][:, 0:n], in1=sin2[:, tcols], op=ALU.mult)
                P.op("dve", "tensor_tensor", out=out, in0=ta[0][:, 0:n], in1=ta[1][:, 0:n], op=ALU.add)
                if f32out is not None:
                    P.op("dve", "tensor_tensor", out=f32out[:, 0:n], in0=ta[0][:, 0:n], in1=ta[1][:, 0:n], op=ALU.add)

            def project_block(t0, btiles):
                n = len(btiles) * 128
                tcl = slice(t0, t0 + n) if not sample else slice(0, n)
                lastb = (t0 + n == T)
                load_kv_w()
                for kvh in range(4):
                    pz = PS[5].v[:, 0:n]
                    pzr = PS[6].v[:, 0:n]
                    for c in range(8):
                        P.mm(pz, wk[:, c, kvh].rearrange("p a b -> p (a b)"), hT[:, c, 0:n], start=(c == 0), stop=(c == 7))
                    for c in range(8):
                        P.mm(pzr, wkr[:, c, kvh].rearrange("p a b -> p (a b)"), hT[:, c, 0:n], start=(c == 0), stop=(c == 7))
                    want32 = sample or lastb
                    rope_evac(kT[:, kvh, t0:t0 + n], pz, pzr, bk[:, kvh:kvh + 1], bkr[:, kvh:kvh + 1], tcl, n, f32out=rec if want32 else None)
                    if want32:
                        cols = [(n - 128, 0)] if not sample else [(k2 * 128, k2) for k2 in range(n // 128)]
                        for (cc, oi) in cols:
                            pt_ = PS[7].v[:, 0:128]
                            P.tr(pt_, rec[:, cc:cc + 128], ident_f.v)
                            P.op("act", "copy", out=kto[:, oi, kvh * 64:(kvh + 1) * 64], in_=pt_[:, 0:64])
                if sample:
                    for k2 in range(n // 128):
                        for s2 in range(2):
                            P.dma("sp", dr["swa_k_s"][0, 2 * k2 + s2, 64:128].rearrange("t h d -> t (h d)"), kto[s2 * 64:(s2 + 1) * 64, k2, :])
                elif lastb:
                    P.dma("sp", dr["swa_k_p"][0, tiles_seq].rearrange("t h d -> t (h d)"), kto[:, 0, :])
                for k, t in enumerate(btiles):
                    gt = t0 // 128 + k
                    pvv = PS[7].v
                    for c in range(8):
                        P.mm(pvv, hT[:, c, k * 128:(k + 1) * 128], wv[:, c].rearrange("p a b d -> p (a b d)"), start=(c == 0), stop=(c == 7))
                    P.op("dve", "tensor_tensor", out=rec, in0=pvv, in1=bv, op=ALU.add)
                    P.op("act", "copy", out=Vd[:, gt, :], in_=rec)
                    if sample or gt == nt - 1:
                        P.op("act", "copy", out=vf.rearrange("p (h f) -> p h f", h=4), in_=rec.rearrange("p (h d f) -> p h d f", h=4, d=2)[:, :, 0, :])
                        if not sample:
                            P.dma("sp", dr["swa_v_p"][0, tiles_seq].rearrange("t h d -> t (h d)"), vf)
                        else:
                            for s2 in range(2):
                                P.dma("sp", dr["swa_v_s"][0, 2 * gt + s2, 64:128].rearrange("t h d -> t (h d)"), vf[s2 * 64:(s2 + 1) * 64, :])
                for half in range(2):
                    load_q_w(half)
                    for p4 in range(4):
                        pr = half * 4 + p4
                        pz = PS[5].v[:, 0:n]
                        pzr = PS[6].v[:, 0:n]
                        for c in range(8):
                            P.mm(pz, wqh[:, c, p4 * 128:(p4 + 1) * 128], hT[:, c, 0:n], start=(c == 0), stop=(c == 7))
                        for c in range(8):
                            P.mm(pzr, wqrh[:, c, 2 * p4:2 * p4 + 2].rearrange("p a b d -> p (a b d)"), hT[:, c, 0:n], start=(c == 0), stop=(c == 7))
                        rope_evac(qT[:, pr, 0:n], pz, pzr, bq[:, pr:pr + 1], bqr[:, pr:pr + 1], tcl, n)

            def wo_apply(tlist):
                for k4, t in enumerate(tlist):
                    for dh in range(2):
                        po = PS[5 + dh].v
                        for pr in range(8):
                            P.mm(po, OT[:, pr, k4 * 128:(k4 + 1) * 128], wo[:, pr, dh * 512:(dh + 1) * 512], start=(pr == 0), stop=(pr == 7))
                        add_to_x(t, dh, po)

            if not sample:
                for qb in range(nt // 4):
                    sub = tiles[qb * 4:qb * 4 + 4]
                    norm_T(sub, dr["g_mix"][layer], hT, None, hb2)
                    project_block(qb * 512, sub)
                    for h in range(16):
                        kvh, half, pr = h // 4, h % 2, h // 2
                        r0 = 64 * half
                        chunks = []
                        for kc in range(max(4 * qb - 1, 0), 4 * qb + 4):
                            base = 128 * (kc - 4 * qb)
                            c0, c1 = max(0, base), min(512, base + 256)
                            fix = []
                            a, b_ = max(c0, base + 192), min(c1, base + 256)
                            if b_ > a:
                                fix.append((0, 64, a, b_))
                            a, b_ = max(c0, base), min(c1, base + 64)
                            if b_ > a:
                                fix.append((64, 128, a, b_))
                            chunks.append(dict(kl=[(kT[r0:r0 + 64, kvh, kc * 128:(kc + 1) * 128], qT[r0:r0 + 64, pr, c0:c1])],
                                               nk=128, c0=c0, c1=c1, fix=fix,
                                               vl=[Vd[:, kc, kvh * 128:(kvh + 1) * 128]]))

                        def out_fn(O, r, r0=r0, pr=pr):
                            P.op("dve", "tensor_tensor", out=OT[r0:r0 + 64, pr, :], in0=O[0][r0:r0 + 64, :], in1=r[r0:r0 + 64, :], op=ALU.mult)
                        softmax_group(chunks, 512, SWA_SCALE, 1, E3, rec, out_fn, sink=sk[:, h:h + 1], zb=h)
                    wo_apply(sub)
            else:
                norm_T(tiles, dr["g_mix"][layer], hT, None, hb2)
                project_block(0, tiles)
                ck = dr["cache_swa_k"][0]
                cv = dr["cache_swa_v"][0]
                for s in range(NSS):
                    P.dma("sp", dr["swa_k_s"][0, s, 0:64], ck[s, 64:128])
                    P.dma("sp", dr["swa_v_s"][0, s, 0:64], cv[s, 64:128])
                    kcb = E3[2].rearrange("p (h d f) -> p h d f", h=4, d=2)
                    for dup in range(2):
                        P.dma("pool", kcb[:, :, dup, :], ck[s])
                        P.dma("pool", Vd[:, nt, :].rearrange("p (h d f) -> p h d f", h=4, d=2)[:, :, dup, :], cv[s])
                    for kvh in range(4):
                        pv = psb(7)
                        P.tr(pv[:, 0:128], E3[2][:, kvh * 128:(kvh + 1) * 128], ident.v)
                        P.op("act", "copy", out=kT[:, kvh, T:T + 128], in_=pv[:, 0:128])
                    P.dma("sp", vown, Vd[(s % 2) * 64:(s % 2) * 64 + 64, s // 2, :])
                    oc = slice(s * 64, (s + 1) * 64)
                    for h in range(16):
                        kvh, half, pr = h // 4, h % 2, h // 2
                        r0 = 64 * half
                        chunks = [
                            dict(kl=[(kT[r0:r0 + 64, kvh, T:T + 128], qT[r0:r0 + 64, pr, oc])], nk=128, c0=0, c1=64, fix=[],
                                 vl=[Vd[:, nt, kvh * 128:(kvh + 1) * 128]]),
                            dict(kl=[(kT[r0:r0 + 64, kvh, oc], qT[r0:r0 + 64, pr, oc])], nk=64, c0=0, c1=64, fix=[],
                                 vl=[vown[:, kvh * 128:(kvh + 1) * 128]]),
                        ]

                        def out_fn(O, r, r0=r0, pr=pr, oc=oc):
                            P.op("dve", "tensor_tensor", out=OT[r0:r0 + 64, pr, oc], in0=O[0][r0:r0 + 64, :], in1=r[r0:r0 + 64, :], op=ALU.mult)
                        softmax_group(chunks, 64, SWA_SCALE, 1, E3[0:2] + [E3[1]], rec, out_fn, sink=sk[:, h:h + 1], zb=h)
                wo_apply(tiles)

        def sb(layer, tiles, sample):
            arena.reset()
            nt = len(tiles)
            T = nt * 128
            hT = arena.alloc([128, 8, T], BF16)
            junk = None
            hb2 = [arena.alloc([128, D], BF16) for _ in range(2)]
            kT = arena.alloc([128, 4, 2048 + 128], BF16)
            Vg = arena.alloc([128, 17, 512], BF16)
            qT = arena.alloc([128, 4, 512 if not sample else T], BF16)
            OT = arena.alloc([128, 4, 512 if not sample else T], BF16)
            w2 = [arena.alloc([128, 8, 512], BF16) for _ in range(2)]
            wo = arena.alloc([128, 4, D], BF16)
            ef2 = [arena.alloc([128, 512], F32) for _ in range(2)]
            sp2 = [arena.alloc([128, 512], BF16) for _ in range(2)]
            tf2 = [arena.alloc([128, 512], F32) for _ in range(2)]
            A2 = [arena.alloc([128, 512], BF16) for _ in range(2)]
            R = arena.alloc([128, 512], F32)
            of2 = [arena.alloc([128, 512], F32) for _ in range(2)]
            ob2 = [arena.alloc([128, 512], BF16) for _ in range(2)]
            vown = arena.alloc([64, 512], BF16)
            kown = arena.alloc([128, 4, 256], BF16)
            vownt = arena.alloc([128, 2, 512], BF16)
            Wv = wview(dr["w_sb_qkv"][0])
            norm_T(tiles, dr["g_mix"][layer], hT, junk, hb2)
            wi = [0]

            def load_w(col0):
                w = w2[wi[0] % 2]
                wi[0] += 1
                P.dma("pool", w, Wv[:, :, col0:col0 + 512])
                return w

            def proj_tile(w, k):
                pz = PS[5 + k % 2].v
                for c in range(8):
                    P.mm(pz, hT[:, c, k * 128:(k + 1) * 128], w[:, c, :], start=(c == 0), stop=(c == 7))
                return pz

            def to_T(dst, src_bf):
                pv = psb(7)
                for pr in range(4):
                    P.tr(pv[:, pr * 128:(pr + 1) * 128], src_bf[:, pr * 128:(pr + 1) * 128], ident.v)
                P.op("act", "copy", out=dst, in_=pv[:, 0:512].rearrange("p (a t) -> p a t", a=4))

            def unit(z, kl, nk, c0, c1, vl, O, mask, first, last, ui):
                zz = z[0:nk, c0:c1]
                ef = ef2[ui % 2][0:nk, c0:c1]
                sp = sp2[ui % 2][0:nk, c0:c1]
                tf = tf2[ui % 2][0:nk, c0:c1]
                A = A2[ui % 2][0:nk, c0:c1]
                P.op("act", "activation", out=ef, in_=zz, func=AF.Exp)
                P.op("act", "activation", out=sp, in_=ef, func=AF.Ln, bias=1.0, scale=1.0)
                if mask is not None:
                    mo, mv = mask
                    P.op("dve", "tensor_tensor", out=sp2[ui % 2][0:nk, mo], in0=sp2[ui % 2][0:nk, mo], in1=mv, op=ALU.mult)
                P.mm(zz, negtri[0:nk, 0:nk], sp, start=False, stop=True, skip_group_check=True)
                rs = PS[2].v
                P.mm(rs[:, c0:c1], ones[0:nk, :], sp, start=True, stop=True)
                P.op("dve", "tensor_tensor", out=tf, in0=zz, in1=R[0:nk, c0:c1], op=ALU.subtract)
                P.op("act", "activation", out=A, in_=tf, func=AF.Exp)
                if mask is not None:
                    P.op("dve", "tensor_tensor", out=A2[ui % 2][0:nk, mo], in0=A2[ui % 2][0:nk, mo], in1=mv, op=ALU.mult)
                P.op("dve", "tensor_tensor", out=R[:, c0:c1], in0=R[:, c0:c1], in1=rs[:, c0:c1], op=ALU.add)
                for (oap, vlhs, aap) in vl(A2[ui % 2]):
                    P.mm(oap, vlhs, aap, start=False, stop=last, skip_group_check=True)

            for g in range(2):
                P.dma("pool", wo, wview(dr["w_sb_o"][0])[:, 4 * g:4 * g + 4, :])
                for which in range(2):
                    w = load_w(1024 * (1 + which) + 512 * g)
                    for k, t in enumerate(tiles):
                        pz = proj_tile(w, k)
                        of = of2[k % 2]
                        ob = ob2[k % 2]
                        P.op("act", "copy", out=of, in_=pz)
                        name = ("sb_k_" if which == 0 else "sb_v_") + ("s" if sample else "p")
                        if not sample:
                            dst = dr[name][0, tiles_seq, k * 128:(k + 1) * 128, 8 * g:8 * g + 8].rearrange("t h d -> t (h d)")
                            P.dma("sp", dst, of)
                        else:
                            for s2 in range(2):
                                dst = dr[name][0, 2 * k + s2, :, 8 * g:8 * g + 8].rearrange("t h d -> t (h d)")
                                P.dma("sp", dst, of[s2 * 64:(s2 + 1) * 64, :])
                        if which == 0:
                            P.op("dve", "tensor_copy", out=ob, in_=of)
                            if not sample:
                                to_T(kT[:, :, k * 128:(k + 1) * 128], ob)
                            else:
                                to_T(kown[:, :, k * 128:(k + 1) * 128], ob)
                        else:
                            if not sample:
                                P.op("dve", "tensor_copy", out=Vg[:, k, :], in_=of)
                            else:
                                P.op("dve", "tensor_copy", out=vownt[:, k, :], in_=of)
                P.mark("sb_a")
                wq_ = load_w(512 * g)
                if not sample:
                    for qb in range(T // 512):
                        for k4 in range(4):
                            k = qb * 4 + k4
                            pz = proj_tile(wq_, k)
                            ob = ob2[k % 2]
                            P.op("act", "activation", out=ob, in_=pz, func=AF.Copy, scale=SB_SCALE)
                            to_T(qT[:, :, k4 * 128:(k4 + 1) * 128], ob)
                        ui = 0
                        for hh in range(8):
                            pr, half = hh // 2, hh % 2
                            r0 = 64 * half
                            P.emit("dve", lambda e, a=R.ap: e.memset(a, 0.0), [], R.bufs)
                            O = PS[3 + hh % 2].v
                            P.mm(O, zeros[:, 0:128], zeros, start=True, stop=False)
                            for kc in range(4 * qb + 3, -1, -1):
                                j = kc - 4 * qb
                                c0 = 128 * j if j >= 0 else 0
                                z = PS[ui % 2].v
                                P.mm(z[:, c0:512], kT[r0:r0 + 64, pr, kc * 128:(kc + 1) * 128], qT[r0:r0 + 64, pr, c0:512], start=True, stop=True)
                                mask = (slice(c0, c0 + 128), tri01) if j >= 0 else None
                                unit(z, None, 128, c0, 512,
                                     lambda Ab, O=O, kc=kc, pr=pr, c0=c0: [(O[:, c0:512], Vg[:, kc, pr * 128:(pr + 1) * 128], Ab[:, c0:512])],
                                     O, mask, False, kc == 0, ui)
                                ui += 1
                            P.op("act", "copy", out=OT[r0:r0 + 64, pr, :], in_=O[r0:r0 + 64, :])
                        for k4 in range(4):
                            t = tiles[qb * 4 + k4]
                            for dh in range(2):
                                po = PS[5 + dh].v
                                for pr in range(4):
                                    P.mm(po, OT[:, pr, k4 * 128:(k4 + 1) * 128], wo[:, pr, dh * 512:(dh + 1) * 512], start=(pr == 0), stop=(pr == 3))
                                add_to_x(t, dh, po)
                else:
                    for k, t in enumerate(tiles):
                        pz = proj_tile(wq_, k)
                        ob = A2[k % 2]
                        P.op("act", "activation", out=ob, in_=pz, func=AF.Copy, scale=SB_SCALE)
                        to_T(qT[:, :, k * 128:(k + 1) * 128], ob)
                    ck = dr["cache_sb_k"][0]
                    cv = dr["cache_sb_v"][0]
                    ui = 0
                    P.mark("sb_b")
                    for s in range(NSS):
                        P.dma("pool", Vg[:, 0:16, :], ck[s, :, 8 * g:8 * g + 8].rearrange("(t p) h d -> p t (h d)", p=128))
                        for kt in range(16):
                            to_T(kT[:, :, kt * 128:(kt + 1) * 128], Vg[:, kt, :])
                        P.dma("pool", Vg[:, 0:16, :], cv[s, :, 8 * g:8 * g + 8].rearrange("(t p) h d -> p t (h d)", p=128))
                        P.dma("sp", vown, vownt[(s % 2) * 64:(s % 2) * 64 + 64, s // 2, :])
                        oc = slice((s % 2) * 64, (s % 2) * 64 + 64)
                        qc = slice(s * 64, (s + 1) * 64)
                        P.mark("sb_c")
                        P.emit("dve", lambda e, a=R.ap: e.memset(a, 0.0), [], R.bufs)
                        O = PS[3 + s % 2].v
                        P.mm(O, zeros[:, 0:128], zeros, start=True, stop=False)
                        for kc in range(16, -1, -1):
                            nk = 64 if kc == 16 else 128
                            for half in range(2):
                                r0 = 64 * half
                                cb0 = half * 256
                                z = PS[ui % 2].v
                                P.mm(z[0:nk, cb0:cb0 + 256], zeros[:, 0:nk], zeros[:, 0:256], start=True, stop=False)
                                for pr in range(4):
                                    ksrc = kown[r0:r0 + 64, pr, qc] if kc == 16 else kT[r0:r0 + 64, pr, kc * 128:(kc + 1) * 128]
                                    P.mm(z[0:nk, cb0 + pr * 64:cb0 + (pr + 1) * 64], ksrc, qT[r0:r0 + 64, pr, qc], start=False, stop=(pr == 3), skip_group_check=True)
                                mask = (slice(cb0, cb0 + 256), msk_s[0:64, 0:256]) if kc == 16 else None

                                def vl(Ab, O=O, kc=kc, nk=nk, cb0=cb0):
                                    res = []
                                    for pr in range(4):
                                        vsrc = vown[:, pr * 128:(pr + 1) * 128] if kc == 16 else Vg[:, kc, pr * 128:(pr + 1) * 128]
                                        res.append((O[:, cb0 + pr * 64:cb0 + (pr + 1) * 64], vsrc, Ab[0:nk, cb0 + pr * 64:cb0 + (pr + 1) * 64]))
                                    return res
                                unit(z, None, nk, cb0, cb0 + 256, vl, O, mask, False, kc == 0, ui)
                                ui += 1
                        for hh in range(8):
                            pr, half = hh // 2, hh % 2
                            r0 = 64 * half
                            P.op("act", "copy", out=OT[r0:r0 + 64, pr, qc], in_=O[r0:r0 + 64, half * 256 + pr * 64:half * 256 + (pr + 1) * 64])
                    for k, t in enumerate(tiles):
                        for dh in range(2):
                            po = PS[5 + dh].v
                            for pr in range(4):
                                P.mm(po, OT[:, pr, k * 128:(k + 1) * 128], wo[:, pr, dh * 512:(dh + 1) * 512], start=(pr == 0), stop=(pr == 3))
                            add_to_x(t, dh, po)

        def final_norm(tiles, dst_fn):
            arena.reset()
            yb2 = [arena.alloc([128, D], F32) for _ in range(2)]
            junk = arena.alloc([128, D], BF16)
            P.dma("sp", gb.v, dr["g_final"].partition_broadcast(128))
            for k, t in enumerate(tiles):
                P.op("act", "activation", out=junk, in_=X[t].v, func=AF.Square, scale=1.0 / 32.0, accum_out=ssb[:, k:k + 1])
            rstd_from_ms(len(tiles), 0)
            for k, t in enumerate(tiles):
                yb = yb2[k % 2]
                P.op("dve", "scalar_tensor_tensor", out=yb, in0=X[t].v, scalar=rstd[:, k:k + 1], in1=gb.v, op0=ALU.mult, op1=ALU.mult)
                P.dma("sp", dst_fn(k), yb)

        def run_pass(sample, si):
            nonlocal tiles_seq
            tiles_seq = si
            if not sample:
                tiles = list(range(16))
                for t in tiles:
                    P.dma("sp", X[t].v, dr["x_prompt"][si, t * 128:(t + 1) * 128, :])
            else:
                tiles = [0, 1]
                for t in tiles:
                    P.dma("sp", X[t].v, dr["x_sample"][2 * t:2 * t + 2].rearrange("s t d -> (s t) d"))
            for layer in range(CFG["depth"]):
                m = layer % 3
                if m == 0:
                    mla(layer, tiles, sample)
                elif m == 1:
                    sb(layer, tiles, sample)
                else:
                    swa(layer, tiles, sample)
                P.mark("mix%d" % layer)
                if not sample:
                    ffn(layer, tiles, 1, 512, None, dr["ffn_conv_p"][layer, si:si + 1])
                else:
                    ffn(layer, tiles, 4, 64, dr["state_ffn_conv"][layer], dr["ffn_conv_s"][layer])
                P.mark("ffn%d" % layer)
                if not sample:
                    ple(layer, tiles, lambda t, layer=layer: dr["p_prompt"][layer, si, t * 128:(t + 1) * 128, :])
                else:
                    ple(layer, tiles, lambda t, layer=layer: dr["p_sample"][layer, 2 * t:2 * t + 2].rearrange("s t f -> (s t) f"))
                P.mark("ple%d" % layer)
            if not sample:
                final_norm(tiles, lambda k: dr["y_prompt"][si, k * 128:(k + 1) * 128, :])
            else:
                final_norm(tiles, lambda k: dr["y_sample"][2 * k:2 * k + 2].rearrange("s t d -> (s t) d"))

        if CFG["sample"]:
            run_pass(True, 0)
            P.stopped = False
        for si in range(CFG["nprompt"]):
            run_pass(False, si)
            P.stopped = False
        P.finalize(block, sems, dsems)
    return nc


def _consts():
    c = {}
    c["c_ident"] = np.eye(128, dtype=np.float32)
    half = 32
    inv = (10000.0 ** (-np.arange(half, dtype=np.float32) / half)).astype(np.float32)
    pos_t = np.concatenate([np.arange(2048), 2048 + (np.arange(128) % 64)]).astype(np.float32)
    ang = pos_t[:, None] * inv[None, :]
    c["c_cost"] = np.cos(ang).astype(np.float32)
    c["c_sint"] = np.sin(ang).astype(np.float32)
    pos_f = np.concatenate([np.arange(2048), 2048 + (np.arange(512) % 64)]).astype(np.float32)
    angf = (inv[:, None] * pos_f[None, :]).astype(np.float32)
    cf, sf = np.cos(angf).astype(np.float32), np.sin(angf).astype(np.float32)
    c["c_cos2"] = np.concatenate([cf, cf, cf, cf], 0)
    c["c_sin2"] = np.concatenate([-sf, sf, -sf, sf], 0)
    k = np.arange(128)[:, None]
    q = np.arange(128)[None, :]
    tri01 = (k < q).astype(np.float32)
    negtri = -(k >= q).astype(np.float32)
    ones = np.ones((128, 128), np.float32)
    zeros = np.zeros((128, 512), np.float32)
    m64 = np.zeros((128, 64), np.float32)
    m64[0:64, :] = (np.arange(64)[:, None] < np.arange(64)[None, :])
    msk_s = np.tile(m64, (1, 8))
    c["c_msk"] = np.concatenate([tri01, negtri, ones, zeros, msk_s], 1).astype(np.float32)
    return c


_NC = None
OUT_NAMES = ["y_prompt", "y_sample", "mla_ckv_p", "mla_krope_p", "mla_ckv_s", "mla_krope_s",
             "sb_k_p", "sb_v_p", "sb_k_s", "sb_v_s", "swa_k_p", "swa_v_p", "swa_k_s", "swa_v_s",
             "ffn_conv_p", "ffn_conv_s"]
BATCH_AXIS = {"x_prompt": 0, "x_sample": 0, "p_prompt": 1, "p_sample": 1, "cache_mla_ckv": 1, "cache_mla_krope": 1,
              "cache_sb_k": 1, "cache_sb_v": 1, "cache_swa_k": 1, "cache_swa_v": 1, "state_ffn_conv": 1}


def kernel(**inputs):
    global _NC
    if _NC is None:
        _NC = build_program()
    nc = _NC
    consts = _consts()
    in_maps = []
    for c in range(NCORES):
        m = dict(consts)
        for name, arr in inputs.items():
            a = np.asarray(arr, dtype=np.float32)
            if name in BATCH_AXIS:
                ax = BATCH_AXIS[name]
                sl = [slice(None)] * a.ndim
                sl[ax] = slice(4 * c, 4 * c + 4)
                a = np.ascontiguousarray(a[tuple(sl)])
            m[name] = a
        in_maps.append(m)
    res = run_bass_kernel_spmd(nc, in_maps, core_ids=list(range(NCORES)))
    outs = []
    for name in OUT_NAMES:
        ax = 0 if name in ("y_prompt", "y_sample") else 1
        outs.append(np.concatenate([np.asarray(r[name]) for r in res.results], axis=ax).astype(np.float32))
    return tuple(outs)
```

```python
from contextlib import ExitStack
import numpy as np
import concourse.bass as bass
import concourse.mybir as mybir
from concourse.bass_utils import run_bass_kernel_spmd

F32 = mybir.dt.float32
BF16 = mybir.dt.bfloat16
AF = mybir.ActivationFunctionType
ALU = mybir.AluOpType

NCORES = 8
NPS = 4
NSS = 4
SEQ = 2048
TS = 64
D = 1024
DEPTH = 4
DFF = 2816
NJ = 22
EPS = 1e-6
MLA_SCALE = 192 ** -0.5
SB_SCALE = 0.125
SWA_SCALE = 0.125

SAME_ENG_SYNC = True
NDSEM = 24
ENGS = ["pe", "act", "dve", "pool", "sp"]
GRAN = 256


class Buf:
    __slots__ = ("ap", "lw", "rd", "rdd")

    def __init__(self, ap):
        self.ap = ap
        self.lw = None
        self.rd = {}
        self.rdd = []

    def __getitem__(self, k):
        return V([self], self.ap[k])

    @property
    def v(self):
        return V([self], self.ap)


class V:
    __slots__ = ("bufs", "ap")

    def __init__(self, bufs, ap):
        self.bufs = bufs
        self.ap = ap

    def __getitem__(self, k):
        return V(self.bufs, self.ap[k])

    def rearrange(self, pattern_, **kw):
        return V(self.bufs, self.ap.rearrange(pattern_, **kw))

    def bitcast(self, dt):
        return V(self.bufs, self.ap.bitcast(dt))


class Ins:
    __slots__ = ("fn", "deps", "dma", "sig", "sigval", "slot", "dval", "prevd")

    def __init__(self, fn, deps, dma):
        self.fn = fn
        self.deps = deps
        self.dma = dma
        self.sig = False
        self.sigval = 0
        self.slot = -1
        self.dval = 0
        self.prevd = 0


CFG = {"sample": True, "nprompt": NPS, "depth": DEPTH, "stop": None}


class Prog:
    def __init__(self, nc):
        self.nc = nc
        self.ins = {e: [] for e in ENGS}
        self.stopped = False

    def mark(self, label):
        if CFG["stop"] == label:
            self.stopped = True

    def emit(self, eng, fn, reads=(), writes=(), dma=False):
        if self.stopped:
            return None
        lst = self.ins[eng]
        idx = len(lst)
        node = (eng, idx)
        deps = set()
        for b in reads:
            if b.lw is not None:
                deps.add(b.lw)
        for b in writes:
            if b.lw is not None:
                deps.add(b.lw)
            for e, i in b.rd.items():
                deps.add((e, i))
            for n in b.rdd:
                deps.add(n)
        for b in reads:
            if dma:
                b.rdd.append(node)
            elif b.rd.get(eng, -1) < idx:
                b.rd[eng] = idx
        for b in writes:
            b.lw = node
            b.rd = {}
            b.rdd = []
        deps.discard(node)
        lst.append(Ins(fn, deps, dma))
        return node

    def op(self, eng, method, **kw):
        reads, writes, real = [], [], {}
        for k, v in kw.items():
            if isinstance(v, V):
                (writes if k in ("out", "accum_out") else reads).extend(v.bufs)
                real[k] = v.ap
            else:
                real[k] = v
        return self.emit(eng, lambda e: getattr(e, method)(**real), reads, writes)

    def mm(self, out, lhsT, rhs, start=True, stop=True, **kw):
        o, l, r = out.ap, lhsT.ap, rhs.ap
        return self.emit("pe", lambda e: e.matmul(o, lhsT=l, rhs=r, start=start, stop=stop, **kw),
                         lhsT.bufs + rhs.bufs, out.bufs)

    def tr(self, out, in_, ident):
        o, i, d = out.ap, in_.ap, ident.ap
        return self.emit("pe", lambda e: e.transpose(o, i, d), in_.bufs + ident.bufs, out.bufs)

    def dma(self, q, out, in_, nc_ok=False):
        reads, writes = [], []
        o, i = out, in_
        if isinstance(out, V):
            writes = out.bufs
            o = out.ap
        if isinstance(in_, V):
            reads = in_.bufs
            i = in_.ap
        if nc_ok:
            fn = lambda e: e.dma_start(out=o, in_=i, allow_slow_non_contiguous=True)
        else:
            fn = lambda e: e.dma_start(out=o, in_=i)
        return self.emit(q, fn, reads, writes, dma=True)

    def finalize(self, block, sems, dsems):
        ins = self.ins
        for e in ENGS:
            for x in ins[e]:
                for (f, j) in x.deps:
                    y = ins[f][j]
                    if y.dma:
                        continue
                    if f == e and (e == "pe" or not SAME_ENG_SYNC):
                        continue
                    y.sig = True
        final_dma = {}
        for e in ENGS:
            c = 0
            nd = 0
            last = [0] * NDSEM
            for x in ins[e]:
                if x.dma:
                    x.slot = nd % NDSEM
                    x.prevd = last[x.slot]
                    last[x.slot] += 16
                    x.dval = last[x.slot]
                    nd += 1
                elif x.sig:
                    c += 1
                    x.sigval = c
            final_dma[e] = last

        def run(e, eng):
            waited = {}
            for x in ins[e]:
                need = {}
                for (f, j) in x.deps:
                    y = ins[f][j]
                    if y.dma:
                        key = ("d", f, y.slot)
                        val = y.dval
                    else:
                        if f == e and (e == "pe" or not SAME_ENG_SYNC):
                            continue
                        key = ("c", f)
                        val = y.sigval
                    if need.get(key, 0) < val:
                        need[key] = val
                if x.dma and x.prevd > 0:
                    key = ("d", e, x.slot)
                    if need.get(key, 0) < x.prevd:
                        need[key] = x.prevd
                for key, val in need.items():
                    if waited.get(key, 0) >= val:
                        continue
                    waited[key] = val
                    sem = sems[key[1]] if key[0] == "c" else dsems[key[1]][key[2]]
                    eng.wait_ge(sem, val)
                r = x.fn(eng)
                if x.dma:
                    r.then_inc(dsems[e][x.slot], 16)
                elif x.sig:
                    r.then_inc(sems[e], 1)
            if e == "sp":
                for q in ENGS:
                    for s, val in enumerate(final_dma[q]):
                        if val > 0 and waited.get(("d", q, s), 0) < val:
                            eng.wait_ge(dsems[q][s], val)

        @block.tensor
        def _(eng):
            run("pe", eng)

        @block.scalar
        def _(eng):
            run("act", eng)

        @block.vector
        def _(eng):
            run("dve", eng)

        @block.gpsimd
        def _(eng):
            run("pool", eng)

        @block.sync
        def _(eng):
            run("sp", eng)


class Arena:
    def __init__(self, ap, nwords):
        self.ap = ap
        self.n = nwords // GRAN
        self.bufs = [Buf(ap[:, i * GRAN:(i + 1) * GRAN]) for i in range(self.n)]
        self.off = 0

    def reset(self):
        self.off = 0

    def alloc(self, shape, dt):
        free = int(np.prod(shape[1:]))
        words = free if dt == F32 else (free + 1) // 2
        g = (words + GRAN - 1) // GRAN
        g0 = self.off
        assert g0 + g <= self.n, ("arena overflow", g0, g, self.n)
        self.off += g
        ap = self.ap[0:shape[0], g0 * GRAN:g0 * GRAN + words]
        if dt != F32:
            ap = ap.bitcast(dt)[:, 0:free]
        if len(shape) > 2:
            names = "abcdefg"[:len(shape) - 1]
            pat = "p (" + " ".join(names) + ") -> p " + " ".join(names)
            ap = ap.rearrange(pat, **{n: shape[i + 1] for i, n in enumerate(names[:-1])})
        return V(self.bufs[g0:g0 + g], ap)


def build_program():
    nc = bass.Bass("TRN2", target_bir_lowering=False)
    dr = {}

    def din(name, shape):
        dr[name] = nc.dram_tensor(name, list(shape), F32, kind="ExternalInput").ap()

    def dout(name, shape):
        dr[name] = nc.dram_tensor(name, list(shape), F32, kind="ExternalOutput").ap()

    din("x_prompt", (NPS, SEQ, D)); din("x_sample", (NSS, TS, D))
    din("p_prompt", (DEPTH, NPS, SEQ, 256)); din("p_sample", (DEPTH, NSS, TS, 256))
    din("cache_mla_ckv", (2, NSS, 2048, 256)); din("cache_mla_krope", (2, NSS, 2048, 64))
    din("cache_sb_k", (1, NSS, 2048, 16, 64)); din("cache_sb_v", (1, NSS, 2048, 16, 64))
    din("cache_swa_k", (1, NSS, 128, 4, 64)); din("cache_swa_v", (1, NSS, 128, 4, 64))
    din("state_ffn_conv", (DEPTH, NSS, 2, DFF))
    din("g_mix", (DEPTH, D)); din("g_ffn", (DEPTH, D)); din("g_ple", (DEPTH, D)); din("g_final", (D,))
    din("w_mla_dq", (2, D, 384)); din("g_mla_q", (2, 384)); din("w_mla_uq", (2, 384, 3072))
    din("w_mla_dkv", (2, D, 320)); din("g_mla_kv", (2, 256))
    din("w_mla_uk", (2, 256, 16, 128)); din("w_mla_uv", (2, 256, 16, 128)); din("w_mla_o", (2, 2048, D))
    din("w_sb_qkv", (1, D, 3072)); din("w_sb_o", (1, D, D))
    din("w_swa_qkv", (1, D, 1536)); din("b_swa_qkv", (1, 1536)); din("swa_sinks", (1, 16)); din("w_swa_o", (1, D, D))
    din("w_ffn_in", (DEPTH, D, 2 * DFF)); din("ffn_conv_w", (DEPTH, 3, DFF)); din("ffn_conv_b", (DEPTH, DFF))
    din("w_ffn_out", (DEPTH, DFF, D)); din("w_ple_gate", (DEPTH, D, D)); din("w_ple_proj", (DEPTH, 256, D))
    din("c_ident", (128, 128)); din("c_cost", (17 * 128, 32)); din("c_sint", (17 * 128, 32))
    din("c_cos2", (128, 2560)); din("c_sin2", (128, 2560)); din("c_msk", (128, 1408))
    dout("y_prompt", (NPS, SEQ, D)); dout("y_sample", (NSS, TS, D))
    dout("mla_ckv_p", (2, NPS, SEQ, 256)); dout("mla_krope_p", (2, NPS, SEQ, 64))
    dout("mla_ckv_s", (2, NSS, TS, 256)); dout("mla_krope_s", (2, NSS, TS, 64))
    dout("sb_k_p", (1, NPS, SEQ, 16, 64)); dout("sb_v_p", (1, NPS, SEQ, 16, 64))
    dout("sb_k_s", (1, NSS, TS, 16, 64)); dout("sb_v_s", (1, NSS, TS, 16, 64))
    dout("swa_k_p", (1, NPS, 128, 4, 64)); dout("swa_v_p", (1, NPS, 128, 4, 64))
    dout("swa_k_s", (1, NSS, 128, 4, 64)); dout("swa_v_s", (1, NSS, 128, 4, 64))
    dout("ffn_conv_p", (DEPTH, NPS, 2, DFF)); dout("ffn_conv_s", (DEPTH, NSS, 2, DFF))

    es = ExitStack()
    with es:
        def sbt(name, shape, dt):
            return es.enter_context(nc.sbuf_tensor(name, shape, dt))

        P = Prog(nc)
        xt = sbt("xres", [128, 16, D], F32)
        X = [Buf(xt[:, i, :]) for i in range(16)]
        ident_f = Buf(sbt("identf", [128, 128], F32)[:])
        ident = Buf(sbt("identb", [128, 128], BF16)[:])
        cost = Buf(sbt("cost", [128, 17, 32], F32)[:])
        sint = Buf(sbt("sint", [128, 17, 32], F32)[:])
        msk = Buf(sbt("msk", [128, 1408], BF16)[:])
        gb = Buf(sbt("gb", [128, D], F32)[:])
        ssb = Buf(sbt("ssb", [128, 16], F32)[:])
        rstd = Buf(sbt("rstd", [128, 16], F32)[:])
        sm1 = Buf(sbt("sm1", [128, 8], F32)[:])
        own_ckvT = Buf(sbt("ownck", [128, 2, 256], BF16)[:]).v
        own_krT = Buf(sbt("ownkr", [64, 256], BF16)[:]).v
        AW = 130 * GRAN
        arena = Arena(sbt("arena", [128, AW], F32)[:], AW)
        PS = [Buf(es.enter_context(nc.psum_tensor(f"ps{i}", [128, 512], F32))[:]) for i in range(8)]
        sems = {e: es.enter_context(nc.semaphore("s_" + e)) for e in ENGS}
        dsems = {e: [es.enter_context(nc.semaphore(f"d_{e}{i}")) for i in range(NDSEM)] for e in ["sp", "pool", "act"]}
        block = es.enter_context(nc.Block())

        tri01 = msk[:, 0:128]
        negtri = msk[:, 128:256]
        ones = msk[:, 256:384]
        zeros = msk[:, 384:896]
        msk_s = msk[:, 896:1408]

        P.dma("sp", ident_f.v, dr["c_ident"])
        P.op("dve", "tensor_copy", out=ident.v, in_=ident_f.v)
        P.dma("sp", cost.v, dr["c_cost"].rearrange("(t p) f -> p t f", p=128))
        P.dma("sp", sint.v, dr["c_sint"].rearrange("(t p) f -> p t f", p=128))
        P.dma("pool", msk.v, dr["c_msk"])

        def wview(w2d):
            return w2d.rearrange("(c p) n -> p c n", p=128)

        def psb(i):
            return PS[i].v.bitcast(BF16)

        def rstd_from_ms(ntl, col0):
            P.op("act", "activation", out=rstd[:, col0:col0 + ntl], in_=ssb[:, col0:col0 + ntl], func=AF.Ln, bias=EPS, scale=1.0)
            P.op("act", "activation", out=rstd[:, col0:col0 + ntl], in_=rstd[:, col0:col0 + ntl], func=AF.Exp, scale=-0.5)

        def norm_T(tiles, g_ap, hT, junk, hb2):
            P.dma("sp", gb.v, g_ap.partition_broadcast(128))
            n = len(tiles)
            for k, t in enumerate(tiles):
                P.op("act", "activation", out=hb2[k % 2], in_=X[t].v, func=AF.Square, scale=1.0 / 32.0, accum_out=ssb[:, k:k + 1])
            rstd_from_ms(n, 0)
            for k, t in enumerate(tiles):
                hb = hb2[k % 2]
                P.op("dve", "scalar_tensor_tensor", out=hb, in0=X[t].v, scalar=rstd[:, k:k + 1], in1=gb.v, op0=ALU.mult, op1=ALU.mult)
                bank = 6 + (k % 2)
                pv = psb(bank).rearrange("p (c t) -> p c t", c=8)
                for c in range(8):
                    P.tr(pv[:, c, :], hb[:, c * 128:(c + 1) * 128], ident.v)
                P.op("act", "copy", out=hT[:, :, k * 128:(k + 1) * 128], in_=pv)

        def add_to_x(t, dh, ps):
            P.op("dve", "tensor_tensor", out=X[t][:, dh * 512:(dh + 1) * 512], in0=X[t][:, dh * 512:(dh + 1) * 512], in1=ps, op=ALU.add)

        def ple(layer, tiles, p_src):
            arena.reset()
            nt = len(tiles)
            TH = min(nt, 8)
            hT = arena.alloc([128, 8, TH * 128], BF16)
            junk = arena.alloc([128, D], BF16)
            hb2 = [arena.alloc([128, D], BF16) for _ in range(2)]
            wg = arena.alloc([128, 8, D], BF16)
            wp = arena.alloc([128, 2, D], BF16)
            pb2 = [arena.alloc([128, 256], BF16) for _ in range(2)]
            pT2 = [arena.alloc([128, 2, 128], BF16) for _ in range(2)]
            sg2 = [arena.alloc([128, 512], F32) for _ in range(2)]
            tt2 = [arena.alloc([128, 512], F32) for _ in range(2)]
            P.dma("pool", wg, wview(dr["w_ple_gate"][layer]))
            P.dma("pool", wp, wview(dr["w_ple_proj"][layer]))
            for h0 in range(0, nt, TH):
                sub = tiles[h0:h0 + TH]
                norm_T(sub, dr["g_ple"][layer], hT, junk, hb2)
                for k, t in enumerate(sub):
                    pb = pb2[k % 2]
                    pT = pT2[k % 2]
                    P.dma("pool", pb, p_src(t))
                    pv = psb(5).rearrange("p (c t) -> p c t", c=8)
                    for c in range(2):
                        P.tr(pv[:, c, :], pb[:, c * 128:(c + 1) * 128], ident.v)
                    P.op("act", "copy", out=pT, in_=pv[:, 0:2, :])
                    for dh in range(2):
                        pa = PS[dh].v
                        pbk = PS[2 + dh].v
                        for c in range(8):
                            P.mm(pa, hT[:, c, k * 128:(k + 1) * 128], wg[:, c, dh * 512:(dh + 1) * 512], start=(c == 0), stop=(c == 7))
                        for c in range(2):
                            P.mm(pbk, pT[:, c, :], wp[:, c, dh * 512:(dh + 1) * 512], start=(c == 0), stop=(c == 1))
                        sg = sg2[dh]
                        tt = tt2[dh]
                        P.op("act", "activation", out=sg, in_=pa, func=AF.Sigmoid)
                        P.op("dve", "tensor_tensor", out=tt, in0=sg, in1=pbk, op=ALU.mult)
                        P.op("dve", "tensor_tensor", out=X[t][:, dh * 512:(dh + 1) * 512], in0=X[t][:, dh * 512:(dh + 1) * 512], in1=tt, op=ALU.add)

        def ffn(layer, tiles, nseg, L, state_src, conv_dst):
            arena.reset()
            nt = len(tiles)
            TH = min(nt, 8)
            NB = nseg * L
            hT = arena.alloc([128, 8, TH * 128], BF16)
            junk = None
            hb2 = [arena.alloc([128, D], BF16) for _ in range(2)]
            actT = arena.alloc([128, NJ, TH * 128], BF16)
            wi2 = [arena.alloc([128, 8, 512], BF16) for _ in range(2)]
            wo2 = [arena.alloc([128, NJ, 256], BF16) for _ in range(2)]
            G2 = [arena.alloc([128, nseg, L + 2], F32) for _ in range(2)]
            t12 = [arena.alloc([128, nseg, L], F32) for _ in range(2)]
            ge2 = [arena.alloc([128, nseg, L], F32) for _ in range(2)]
            carry = arena.alloc([128, NJ, nseg, 2], F32)
            cw = arena.alloc([128, NJ, 3], F32)
            cb = arena.alloc([128, NJ], F32)
            for r in range(3):
                P.dma("sp", cw[:, :, r], dr["ffn_conv_w"][layer, r].rearrange("(j p) -> p j", p=128), nc_ok=True)
            P.dma("sp", cb, dr["ffn_conv_b"][layer].rearrange("(j p) -> p j", p=128), nc_ok=True)
            if state_src is None:
                P.op("dve", "memset", out=carry, constant=0.0) if False else P.emit("dve", lambda e, a=carry.ap: e.memset(a, 0.0), [], carry.bufs)
            else:
                for sg in range(nseg):
                    for r in range(2):
                        P.dma("sp", carry[:, :, sg, r], state_src[sg, r].rearrange("(j p) -> p j", p=128), nc_ok=True)
            win = wview(dr["w_ffn_in"][layer])
            wov = dr["w_ffn_out"][layer].rearrange("(j p) n -> p j n", p=128)
            uc = 0
            for h0 in range(0, nt, TH):
                sub = tiles[h0:h0 + TH]
                ntok = len(sub) * 128
                norm_T(sub, dr["g_ffn"][layer], hT, junk, hb2)
                nblk = ntok // NB
                work = []
                for grp in range(NJ // 2):
                    for u in range(2):
                        for blk in range(nblk):
                            work.append((grp, u, blk))

                def stage_a(idx, grp, u, blk):
                    wi = wi2[grp % 2]
                    if u == 0 and blk == 0:
                        P.dma("pool", wi[:, :, 0:256], win[:, :, grp * 256:(grp + 1) * 256])
                        P.dma("pool", wi[:, :, 256:512], win[:, :, DFF + grp * 256:DFF + (grp + 1) * 256])
                    j = grp * 2 + u
                    cs = slice(blk * NB, (blk + 1) * NB)
                    pg = PS[idx % 2].v[:, 0:NB]
                    pu = PS[2 + idx % 2].v[:, 0:NB]
                    G = G2[idx % 2]
                    t1 = t12[idx % 2]
                    for c in range(8):
                        P.mm(pg, wi[:, c, u * 128:(u + 1) * 128], hT[:, c, cs], start=(c == 0), stop=(c == 7))
                    for c in range(8):
                        P.mm(pu, wi[:, c, 256 + u * 128:256 + (u + 1) * 128], hT[:, c, cs], start=(c == 0), stop=(c == 7))
                    P.op("act", "copy", out=G[:, :, 0:2], in_=carry[:, j])
                    P.op("act", "copy", out=G[:, :, 2:L + 2], in_=pg.rearrange("p (s l) -> p s l", s=nseg))
                    P.op("dve", "tensor_copy", out=carry[:, j], in_=G[:, :, L:L + 2])
                    P.op("act", "activation", out=t1, in_=G[:, :, 2:L + 2], func=AF.Identity, scale=cw[:, j, 2:3], bias=cb[:, j:j + 1])
                    P.op("dve", "scalar_tensor_tensor", out=t1, in0=G[:, :, 1:L + 1], scalar=cw[:, j, 1:2], in1=t1, op0=ALU.mult, op1=ALU.add)
                    P.op("dve", "scalar_tensor_tensor", out=t1, in0=G[:, :, 0:L], scalar=cw[:, j, 0:1], in1=t1, op0=ALU.mult, op1=ALU.add)

                def stage_b(idx, grp, u, blk):
                    j = grp * 2 + u
                    cs = slice(blk * NB, (blk + 1) * NB)
                    pu = PS[2 + idx % 2].v[:, 0:NB]
                    t1 = t12[idx % 2]
                    ge = ge2[idx % 2]
                    P.op("act", "activation", out=ge, in_=t1, func=AF.Gelu)
                    P.op("dve", "tensor_tensor", out=actT[:, j, cs].rearrange("p (s l) -> p s l", s=nseg), in0=ge, in1=pu.rearrange("p (s l) -> p s l", s=nseg), op=ALU.mult)

                for i, wk_ in enumerate(work):
                    stage_a(uc + i, *wk_)
                    if i > 0:
                        stage_b(uc + i - 1, *work[i - 1])
                stage_b(uc + len(work) - 1, *work[-1])
                uc += len(work)
                for dq in range(4):
                    wo = wo2[dq % 2]
                    P.dma("pool", wo, wov[:, :, dq * 256:(dq + 1) * 256])
                    for k, t in enumerate(sub):
                        po = PS[4 + (dq * 8 + k) % 2].v[:, 0:256]
                        for j in range(NJ):
                            P.mm(po, actT[:, j, k * 128:(k + 1) * 128], wo[:, j, :], start=(j == 0), stop=(j == NJ - 1))
                        xs = X[t][:, dq * 256:(dq + 1) * 256]
                        P.op("dve", "tensor_tensor", out=xs, in0=xs, in1=po, op=ALU.add)
            for sg in range(nseg):
                for r in range(2):
                    P.dma("sp", conv_dst[sg, r].rearrange("(j p) -> p j", p=128), carry[:, :, sg, r], nc_ok=True)

        def softmax_group(chunks, N, scale, ndv, E3, rec, out_fn, sink=None, zb=0):
            den = PS[2].v[:, 0:N]
            O = [PS[3 + d].v[:, 0:N] for d in range(ndv)]
            zr = zeros[:, 0:N]
            P.mm(den, zeros[:, 0:128], zr, start=True, stop=False)
            for d in range(ndv):
                P.mm(O[d], zeros[:, 0:128], zr, start=True, stop=False)
            nch = len(chunks)

            def stage_z(ci):
                ch = chunks[ci]
                nk, c0, c1 = ch["nk"], ch["c0"], ch["c1"]
                z = PS[(zb + ci) % 2].v[0:nk, c0:c1]
                E = E3[ci % 3][0:nk, c0:c1]
                kl = ch["kl"]
                for i, (l, r) in enumerate(kl):
                    P.mm(z, l, r, start=(i == 0), stop=(i == len(kl) - 1))
                P.op("act", "activation", out=E, in_=z, func=AF.Exp, scale=scale)
                for (p0, p1, a, b) in ch["fix"]:
                    P.emit("dve", lambda e, ap=E3[ci % 3][p0:p1, a:b].ap: e.memset(ap, 0.0), [], E.bufs)

            def stage_v(ci):
                ch = chunks[ci]
                nk, c0, c1 = ch["nk"], ch["c0"], ch["c1"]
                E = E3[ci % 3][0:nk, c0:c1]
                last = (ci == nch - 1)
                P.mm(den[:, c0:c1], ones[0:nk, :], E, start=False, stop=last, skip_group_check=True)
                for d in range(ndv):
                    P.mm(O[d][:, c0:c1], ch["vl"][d], E, start=False, stop=last, skip_group_check=True)

            for ci in range(nch):
                stage_z(ci)
                if ci > 0:
                    stage_v(ci - 1)
            stage_v(nch - 1)
            if sink is not None:
                P.op("act", "activation", out=rec[:, 0:N], in_=den, func=AF.Ln, bias=sink, scale=1.0)
            else:
                P.op("act", "activation", out=rec[:, 0:N], in_=den, func=AF.Ln)
            P.op("act", "activation", out=rec[:, 0:N], in_=rec[:, 0:N], func=AF.Exp, scale=-1.0)
            out_fn(O, rec[:, 0:N])

        def mla(layer, tiles, sample):
            jl = layer // 3
            arena.reset()
            nt = len(tiles)
            T = nt * 128
            NK = 2112 if sample else 2048
            TW = 512 if sample else 2048
            tc0 = 2048 if sample else 0
            ckvT = arena.alloc([128, 2, NK], BF16)
            krT = arena.alloc([64, NK], BF16)
            ckvt = arena.alloc([128, 17 if sample else 16, 256], BF16)
            cqT = arena.alloc([128, 3, T], BF16)
            cos2 = arena.alloc([64, TW], BF16)
            sin2 = arena.alloc([64, TW], BF16)
            wukT = arena.alloc([128, 16, 256], BF16)
            wuv = arena.alloc([128, 2, 2048], BF16)
            mark = arena.off
            wo = arena.alloc([128, 16, D], BF16)
            OVT = arena.alloc([128, 16, 512 if not sample else T], BF16)
            wq4 = [arena.alloc([128, 3, 256], BF16) for _ in range(3)]
            E3 = [arena.alloc([128, 512], BF16) for _ in range(3)]
            rec = arena.alloc([128, 512], F32)
            qn2 = [arena.alloc([128, 512], BF16) for _ in range(2)]
            qr2 = [arena.alloc([64, 512], BF16) for _ in range(2)]
            qlat2 = [arena.alloc([128, 2, 512], BF16) for _ in range(2)]
            olat = arena.alloc([128, 2, 512], BF16)
            tq2 = [arena.alloc([64, 512], F32) for _ in range(2)]
            krc = arena.alloc([128, 16, 64], BF16) if sample else None
            arena.off = mark
            TH = min(nt, 8)
            hT = arena.alloc([128, 8, TH * 128], BF16)
            hb2 = [arena.alloc([128, D], BF16) for _ in range(2)]
            wdq = arena.alloc([128, 8, 384], BF16)
            wdkv = arena.alloc([128, 8, 320], BF16)
            gq = arena.alloc([128, 384], F32)
            gkv = arena.alloc([128, 256], F32)
            wuk = arena.alloc([128, 2, 2048], BF16)
            ckvf2 = [arena.alloc([128, 256], F32) for _ in range(2)]
            krf2 = [arena.alloc([128, 64], F32) for _ in range(2)]
            krt = arena.alloc([128, 4, 32], F32)
            krb2 = [arena.alloc([128, 64], BF16) for _ in range(2)]
            cqb2 = [arena.alloc([128, 384], BF16) for _ in range(2)]
            ckb_s = arena.alloc([128, 256], BF16)
            sqj = arena.alloc([128, 640], BF16)
            P.dma("pool", cos2, dr["c_cos2"][0:64, tc0:tc0 + TW])
            P.dma("pool", sin2, dr["c_sin2"][0:64, tc0:tc0 + TW])
            P.dma("pool", wdq, wview(dr["w_mla_dq"][jl]))
            P.dma("pool", wdkv, wview(dr["w_mla_dkv"][jl]))
            P.dma("sp", gq, dr["g_mla_q"][jl].partition_broadcast(128))
            P.dma("sp", gkv, dr["g_mla_kv"][jl].partition_broadcast(128))
            P.dma("pool", wuk, wview(dr["w_mla_uk"][jl].rearrange("l h n -> l (h n)")))
            P.dma("pool", wuv, wview(dr["w_mla_uv"][jl].rearrange("l h n -> l (h n)")))
            for h in range(16):
                pv = psb(5).rearrange("p (c t) -> p c t", c=8)
                for lc in range(2):
                    P.tr(pv[:, lc, :], wuk[:, lc, h * 128:(h + 1) * 128], ident.v)
                P.op("act", "copy", out=wukT[:, h, :].rearrange("p (c t) -> p c t", c=2), in_=pv[:, 0:2, :])
            for h0 in range(0, nt, TH):
                sub = tiles[h0:h0 + TH]
                norm_T(sub, dr["g_mix"][layer], hT, None, hb2)
                for k, t in enumerate(sub):
                    gt = h0 + k
                    ptile = 16 if sample else gt
                    hs = [hT[:, c, k * 128:(k + 1) * 128] for c in range(8)]
                    pkv = PS[0].v[:, 0:320]
                    pcq = PS[1].v[:, 0:384]
                    for c in range(8):
                        P.mm(pkv, hs[c], wdkv[:, c, :], start=(c == 0), stop=(c == 7))
                    for c in range(8):
                        P.mm(pcq, hs[c], wdq[:, c, :], start=(c == 0), stop=(c == 7))
                    P.op("act", "activation", out=sqj[:, 0:256], in_=pkv[:, 0:256], func=AF.Square, scale=1.0 / 16.0, accum_out=ssb[:, 8:9])
                    P.op("act", "activation", out=sqj[:, 256:640], in_=pcq, func=AF.Square, scale=384 ** -0.5, accum_out=ssb[:, 9:10])
                    rstd_from_ms(2, 8)
                    ckvf = ckvf2[k % 2]
                    krf = krf2[k % 2]
                    krb = krb2[k % 2]
                    cqb = cqb2[k % 2]
                    P.op("dve", "scalar_tensor_tensor", out=ckvf, in0=pkv[:, 0:256], scalar=rstd[:, 8:9], in1=gkv, op0=ALU.mult, op1=ALU.mult)
                    P.op("dve", "scalar_tensor_tensor", out=cqb, in0=pcq, scalar=rstd[:, 9:10], in1=gq, op0=ALU.mult, op1=ALU.mult)
                    x1 = pkv[:, 256:288]
                    x2 = pkv[:, 288:320]
                    cs_, sn_ = cost[:, ptile, :], sint[:, ptile, :]
                    P.op("dve", "tensor_tensor", out=krt[:, 0, :], in0=x1, in1=cs_, op=ALU.mult)
                    P.op("dve", "tensor_tensor", out=krt[:, 1, :], in0=x2, in1=sn_, op=ALU.mult)
                    P.op("dve", "tensor_tensor", out=krt[:, 2, :], in0=x2, in1=cs_, op=ALU.mult)
                    P.op("dve", "tensor_tensor", out=krt[:, 3, :], in0=x1, in1=sn_, op=ALU.mult)
                    P.op("dve", "tensor_tensor", out=krf[:, 0:32], in0=krt[:, 0, :], in1=krt[:, 1, :], op=ALU.subtract)
                    P.op("dve", "tensor_tensor", out=krf[:, 32:64], in0=krt[:, 2, :], in1=krt[:, 3, :], op=ALU.add)
                    if not sample:
                        si = tiles_seq
                        P.dma("sp", dr["mla_ckv_p"][jl, si, gt * 128:(gt + 1) * 128, :], ckvf)
                        P.dma("sp", dr["mla_krope_p"][jl, si, gt * 128:(gt + 1) * 128, :], krf)
                        ckb = ckvt[:, gt, :]
                    else:
                        P.dma("sp", dr["mla_ckv_s"][jl, 2 * gt:2 * gt + 2].rearrange("s t l -> (s t) l"), ckvf)
                        P.dma("sp", dr["mla_krope_s"][jl, 2 * gt:2 * gt + 2].rearrange("s t l -> (s t) l"), krf)
                        ckb = ckb_s
                    P.op("act", "copy", out=ckb, in_=ckvf)
                    P.op("act", "copy", out=krb, in_=krf)
                    pv = psb(5).rearrange("p (c t) -> p c t", c=8)
                    P.tr(pv[:, 0, :], ckb[:, 0:128], ident.v)
                    P.tr(pv[:, 1, :], ckb[:, 128:256], ident.v)
                    P.tr(pv[0:64, 2, :], krb, ident.v)
                    for c in range(3):
                        P.tr(pv[:, 3 + c, :], cqb[:, c * 128:(c + 1) * 128], ident.v)
                    if not sample:
                        kc0 = gt * 128
                        P.op("act", "copy", out=ckvT[:, :, kc0:kc0 + 128], in_=pv[:, 0:2, :])
                        P.op("act", "copy", out=krT[:, kc0:kc0 + 128], in_=pv[0:64, 2, :])
                    else:
                        P.op("act", "copy", out=own_ckvT[:, :, gt * 128:(gt + 1) * 128], in_=pv[:, 0:2, :])
                        P.op("act", "copy", out=own_krT[:, gt * 128:(gt + 1) * 128], in_=pv[0:64, 2, :])
                    P.op("act", "copy", out=cqT[:, :, gt * 128:(gt + 1) * 128], in_=pv[:, 3:6, :])
            P.mark("mla1")
            P.dma("pool", wo, wview(dr["w_mla_o"][jl]))
            uqv = dr["w_mla_uq"][jl].rearrange("(c p) n -> p c n", p=128)
            wqi = [0]

            def load_wq(h):
                wq = wq4[wqi[0] % 3]
                wqi[0] += 1
                P.dma("pool", wq[:, :, 0:192], uqv[:, :, h * 192:(h + 1) * 192])
                P.dma("pool", wq[:, :, 192:224], uqv[:, :, h * 192 + 160:h * 192 + 192])
                P.dma("pool", wq[:, :, 224:256], uqv[:, :, h * 192 + 128:h * 192 + 160])
                return wq

            def qproj(h, qcols, ocols):
                wq = load_wq(h)
                pq = PS[5].v[:, ocols]
                pr = PS[6].v[0:64, ocols]
                pt = PS[7].v[0:64, ocols]
                for c in range(3):
                    P.mm(pq, wq[:, c, 0:128], cqT[:, c, qcols], start=(c == 0), stop=(c == 2))
                for c in range(3):
                    P.mm(pr, wq[:, c, 128:192], cqT[:, c, qcols], start=(c == 0), stop=(c == 2))
                for c in range(3):
                    P.mm(pt, wq[:, c, 192:256], cqT[:, c, qcols], start=(c == 0), stop=(c == 2))

            def qfinish(N, tcols, b=0):
                qn, qr = qn2[b], qr2[b]
                P.op("act", "copy", out=qn[:, 0:N], in_=PS[5].v[:, 0:N])
                P.op("dve", "tensor_tensor", out=tq2[0][:, 0:N], in0=PS[6].v[0:64, 0:N], in1=cos2[:, tcols], op=ALU.mult)
                P.op("dve", "tensor_tensor", out=tq2[1][:, 0:N], in0=PS[7].v[0:64, 0:N], in1=sin2[:, tcols], op=ALU.mult)
                P.op("dve", "tensor_tensor", out=qr[:, 0:N], in0=tq2[0][:, 0:N], in1=tq2[1][:, 0:N], op=ALU.add)

            def qlat_from_qn(hlist, N, b=0):
                qn, qlat = qn2[b], qlat2[b]
                w = N // len(hlist)
                for lc in range(2):
                    pl = PS[5 + lc].v
                    for i, h in enumerate(hlist):
                        P.mm(pl[:, i * w:(i + 1) * w], wukT[:, h, lc * 128:(lc + 1) * 128], qn[:, i * w:(i + 1) * w], start=True, stop=True)
                    P.op("act", "copy", out=qlat[:, lc, 0:N], in_=pl[:, 0:N])

            if not sample:
                items = [(qb, h) for qb in range(T // 512) for h in range(16)]

                def prep(i):
                    qb, h = items[i]
                    qc = slice(qb * 512, (qb + 1) * 512)
                    qproj(h, qc, slice(0, 512))
                    qfinish(512, qc, i % 2)
                    qlat_from_qn([h], 512, i % 2)

                def attend(i):
                    qb, h = items[i]
                    qlat, qr = qlat2[i % 2], qr2[i % 2]
                    chunks = []
                    for kc in range(4 * qb + 4):
                        j = kc - 4 * qb
                        c0 = 128 * j if j >= 0 else 0
                        kcs = slice(kc * 128, (kc + 1) * 128)
                        chunks.append(dict(
                            kl=[(ckvT[:, 0, kcs], qlat[:, 0, c0:512]), (ckvT[:, 1, kcs], qlat[:, 1, c0:512]), (krT[:, kcs], qr[:, c0:512])],
                            nk=128, c0=c0, c1=512,
                            fix=[(64, 128, c0, c0 + 64)] if j >= 0 else [],
                            vl=[ckvt[:, kc, 0:128], ckvt[:, kc, 128:256]]))

                    def out_fn(O, r, h=h):
                        for lc in range(2):
                            P.op("dve", "tensor_tensor", out=olat[:, lc, :], in0=O[lc], in1=r, op=ALU.mult)
                        pv_ = PS[7].v
                        for lc in range(2):
                            P.mm(pv_, wuv[:, lc, h * 128:(h + 1) * 128], olat[:, lc, :], start=(lc == 0), stop=(lc == 1))
                        P.op("act", "copy", out=OVT[:, h, :], in_=pv_)
                    softmax_group(chunks, 512, MLA_SCALE, 2, E3, rec, out_fn, zb=h)

                prep(0)
                for i, (qb, h) in enumerate(items):
                    if i + 1 < len(items):
                        prep(i + 1)
                    attend(i)
                    if h == 15:
                        for k4 in range(4):
                            t = tiles[qb * 4 + k4]
                            for dh in range(2):
                                po = PS[5 + dh].v
                                for hh in range(16):
                                    P.mm(po, OVT[:, hh, k4 * 128:(k4 + 1) * 128], wo[:, hh, dh * 512:(dh + 1) * 512], start=(hh == 0), stop=(hh == 15))
                                add_to_x(t, dh, po)
            else:
                for s in range(NSS):
                    P.dma("pool", ckvt[:, 0:16, :], dr["cache_mla_ckv"][jl, s].rearrange("(t p) l -> p t l", p=128))
                    P.dma("pool", krc, dr["cache_mla_krope"][jl, s].rearrange("(t p) f -> p t f", p=128))
                    for kt in range(16):
                        pv = psb(5 + kt % 2).rearrange("p (c t) -> p c t", c=8)
                        P.tr(pv[:, 0, :], ckvt[:, kt, 0:128], ident.v)
                        P.tr(pv[:, 1, :], ckvt[:, kt, 128:256], ident.v)
                        P.tr(pv[0:64, 2, :], krc[:, kt, :], ident.v)
                        P.op("act", "copy", out=ckvT[:, :, kt * 128:(kt + 1) * 128], in_=pv[:, 0:2, :])
                        P.op("act", "copy", out=krT[:, kt * 128:(kt + 1) * 128], in_=pv[0:64, 2, :])
                    oc = slice(s * 64, (s + 1) * 64)
                    P.op("act", "copy", out=ckvT[:, :, 2048:2112], in_=own_ckvT[:, :, oc])
                    P.op("act", "copy", out=krT[:, 2048:2112], in_=own_krT[:, oc])
                    pv = psb(5).rearrange("p (c t) -> p c t", c=8)
                    for lc in range(2):
                        P.tr(pv[0:64, lc, :], ckvT[:, lc, 2048:2112], ident.v)
                    P.op("act", "copy", out=ckvt[0:64, 16, :].rearrange("p (c t) -> p c t", c=2), in_=pv[0:64, 0:2, :])
                    for g in range(2):
                        hl = list(range(8 * g, 8 * g + 8))
                        for i, h in enumerate(hl):
                            qproj(h, oc, slice(i * 64, (i + 1) * 64))
                        qfinish(512, slice(0, 512))
                        qlat_from_qn(hl, 512)
                        chunks = []
                        for kc in range(17):
                            nk = 128 if kc < 16 else 64
                            kcs = slice(kc * 128, kc * 128 + nk)
                            chunks.append(dict(
                                kl=[(ckvT[:, 0, kcs], qlat2[0][:, 0, :]), (ckvT[:, 1, kcs], qlat2[0][:, 1, :]), (krT[:, kcs], qr2[0][:, :])],
                                nk=nk, c0=0, c1=512, fix=[],
                                vl=[ckvt[0:nk, kc, 0:128], ckvt[0:nk, kc, 128:256]]))

                        def out_fn(O, r, hl=hl, s=s, g=g):
                            for lc in range(2):
                                P.op("dve", "tensor_tensor", out=olat[:, lc, :], in0=O[lc], in1=r, op=ALU.mult)
                            pv_ = PS[7].v
                            for i, h in enumerate(hl):
                                for lc in range(2):
                                    P.mm(pv_[:, i * 64:(i + 1) * 64], wuv[:, lc, h * 128:(h + 1) * 128], olat[:, lc, i * 64:(i + 1) * 64], start=(lc == 0), stop=(lc == 1))
                            P.op("act", "copy", out=OVT[:, 8 * g:8 * g + 8, s * 64:(s + 1) * 64], in_=pv_.rearrange("p (h q) -> p h q", h=8))
                        softmax_group(chunks, 512, MLA_SCALE, 2, E3, rec, out_fn, zb=g)
                for k, t in enumerate(tiles):
                    for dh in range(2):
                        po = PS[5 + dh].v
                        for h in range(16):
                            P.mm(po, OVT[:, h, k * 128:(k + 1) * 128], wo[:, h, dh * 512:(dh + 1) * 512], start=(h == 0), stop=(h == 15))
                        add_to_x(t, dh, po)

        tiles_seq = 0

        def swa(layer, tiles, sample):
            arena.reset()
            nt = len(tiles)
            T = nt * 128
            TW = 512 if sample else 2048
            tc0 = 2048 if sample else 0
            BW = 256 if sample else 512
            KX = 128 if sample else 0
            wo = arena.alloc([128, 8, D], BF16)
            cos2 = arena.alloc([128, TW], BF16)
            sin2 = arena.alloc([128, TW], BF16)
            bq = arena.alloc([128, 8], F32)
            bqr = arena.alloc([128, 8], F32)
            bk = arena.alloc([128, 4], F32)
            bkr = arena.alloc([128, 4], F32)
            bvs = arena.alloc([128, 256], F32)
            bv = arena.alloc([128, 512], F32)
            sk = arena.alloc([128, 16], F32)
            kT = arena.alloc([128, 4, T + KX], BF16)
            Vd = arena.alloc([128, nt + (1 if sample else 0), 512], BF16)
            qT = arena.alloc([128, 8, BW], BF16)
            OT = arena.alloc([128, 8, BW], BF16)
            hT = arena.alloc([128, 8, BW], BF16)
            hb2 = [arena.alloc([128, D], BF16) for _ in range(2)]
            E3 = [arena.alloc([128, 512], BF16) for _ in range(3)]
            rec = arena.alloc([128, 512], F32)
            ta = [arena.alloc([128, 512], F32) for _ in range(2)]
            kf = arena.alloc([128, 4, 128], F32)
            vf = arena.alloc([128, 256], F32)
            kto = arena.alloc([128, 2, 256], F32) if sample else arena.alloc([128, 1, 256], F32)
            vown = arena.alloc([64, 512], BF16) if sample else None
            knat_s = arena.alloc([128, 8, 512], BF16) if sample else None
            mark = arena.off
            wk = arena.alloc([128, 8, 4, 2, 64], BF16)
            wkr = arena.alloc([128, 8, 4, 4, 32], BF16)
            wv = arena.alloc([128, 8, 4, 2, 64], BF16)
            arena.off = mark
            wqh = arena.alloc([128, 8, 512], BF16)
            wqrh = arena.alloc([128, 8, 8, 2, 32], BF16)
            W = dr["w_swa_qkv"][0]
            wv_ = wview(W)
            w5 = W.rearrange("(c p) (h two f) -> p c h two f", p=128, two=2, f=32)
            w4 = W.rearrange("(c p) (h d) -> p c h d", p=128, d=64)
            P.dma("pool", wo, wview(dr["w_swa_o"][0]))
            P.dma("pool", cos2, dr["c_cos2"][:, tc0:tc0 + TW])
            P.dma("pool", sin2, dr["c_sin2"][:, tc0:tc0 + TW])
            B = dr["b_swa_qkv"][0]
            P.dma("sp", bq, B[0:1024].rearrange("(c p) -> p c", p=128), nc_ok=True)
            b3 = B.rearrange("(h two f) -> two f h", two=2, f=32)
            for hh in range(2):
                for two in range(2):
                    r0 = hh * 64 + two * 32
                    P.dma("sp", bqr[r0:r0 + 32, :], b3[1 - two, :, hh:16:2], nc_ok=True)
                    P.dma("sp", bk[r0:r0 + 32, :], b3[two, :, 16:20], nc_ok=True)
                    P.dma("sp", bkr[r0:r0 + 32, :], b3[1 - two, :, 16:20], nc_ok=True)
            P.dma("sp", bvs, B[1280:1536].partition_broadcast(128))
            bv4 = bv.rearrange("p (h d f) -> p h d f", h=4, d=2)
            for dup in range(2):
                P.op("dve", "tensor_copy", out=bv4[:, :, dup, :], in_=bvs.rearrange("p (h f) -> p h f", h=4))
            P.dma("sp", sk, dr["swa_sinks"][0].partition_broadcast(128))
            P.op("act", "activation", out=sk, in_=sk, func=AF.Exp)

            def load_kv_w():
                knat = knat_s if sample else qT
                P.dma("pool", knat, wv_[:, :, 1024:1536])
                kn4 = knat[:, :, 0:256].rearrange("p c (h d) -> p c h d", h=4)
                vn4 = knat[:, :, 256:512].rearrange("p c (h d) -> p c h d", h=4)
                kn5 = knat[:, :, 0:256].rearrange("p c (h two f) -> p c h two f", h=4, two=2)
                for dup in range(2):
                    P.op("pool", "tensor_copy", out=wk[:, :, :, dup, :], in_=kn4)
                    P.op("pool", "tensor_copy", out=wv[:, :, :, dup, :], in_=vn4)
                    for two in range(2):
                        P.op("pool", "tensor_copy", out=wkr[:, :, :, dup * 2 + two, :], in_=kn5[:, :, :, 1 - two, :])

            def load_q_w(half):
                P.dma("pool", wqh, wv_[:, :, half * 512:(half + 1) * 512])
                q5 = wqh.rearrange("p c (h two f) -> p c h two f", h=8, two=2)
                for two in range(2):
                    P.op("pool", "tensor_copy", out=wqrh[:, :, :, two, :], in_=q5[:, :, :, 1 - two, :])

            def rope_evac(out, pz, pzr, b, br, tcols, n, f32out=None):
                P.op("act", "activation", out=ta[0][:, 0:n], in_=pz, func=AF.Identity, bias=b, scale=1.0)
                P.op("act", "activation", out=ta[1][:, 0:n], in_=pzr, func=AF.Identity, bias=br, scale=1.0)
                P.op("dve", "tensor_tensor", out=ta[0][:, 0:n], in0=ta[0][:, 0:n], in1=cos2[:, tcols], op=ALU.mult)
                P.op("dve", "tensor_tensor", out=ta[1][:, 0:n], in0=ta[1][:, 0:n], in1=sin2[:, tcols], op=ALU.mult)
                P.op("dve", "tensor_tensor", out=out, in0=ta[0][:, 0:n], in1=ta[1][:, 0:n], op=ALU.add)
                if f32out is not None:
                    P.op("dve", "tensor_tensor", out=f32out[:, 0:n], in0=ta[0][:, 0:n], in1=ta[1][:, 0:n], op=ALU.add)

            def project_block(t0, btiles):
                n = len(btiles) * 128
                tcl = slice(t0, t0 + n) if not sample else slice(0, n)
                lastb = (t0 + n == T)
                load_kv_w()
                for kvh in range(4):
                    pz = PS[5].v[:, 0:n]
                    pzr = PS[6].v[:, 0:n]
                    for c in range(8):
                        P.mm(pz, wk[:, c, kvh].rearrange("p a b -> p (a b)"), hT[:, c, 0:n], start=(c == 0), stop=(c == 7))
                    for c in range(8):
                        P.mm(pzr, wkr[:, c, kvh].rearrange("p a b -> p (a b)"), hT[:, c, 0:n], start=(c == 0), stop=(c == 7))
                    want32 = sample or lastb
                    rope_evac(kT[:, kvh, t0:t0 + n], pz, pzr, bk[:, kvh:kvh + 1], bkr[:, kvh:kvh + 1], tcl, n, f32out=rec if want32 else None)
                    if want32:
                        cols = [(n - 128, 0)] if not sample else [(k2 * 128, k2) for k2 in range(n // 128)]
                        for (cc, oi) in cols:
                            pt_ = PS[7].v[:, 0:128]
                            P.tr(pt_, rec[:, cc:cc + 128], ident_f.v)
                            P.op("act", "copy", out=kto[:, oi, kvh * 64:(kvh + 1) * 64], in_=pt_[:, 0:64])
                if sample:
                    for k2 in range(n // 128):
                        for s2 in range(2):
                            P.dma("sp", dr["swa_k_s"][0, 2 * k2 + s2, 64:128].rearrange("t h d -> t (h d)"), kto[s2 * 64:(s2 + 1) * 64, k2, :])
                elif lastb:
                    P.dma("sp", dr["swa_k_p"][0, tiles_seq].rearrange("t h d -> t (h d)"), kto[:, 0, :])
                for k, t in enumerate(btiles):
                    gt = t0 // 128 + k
                    pvv = PS[7].v
                    for c in range(8):
                        P.mm(pvv, hT[:, c, k * 128:(k + 1) * 128], wv[:, c].rearrange("p a b d -> p (a b d)"), start=(c == 0), stop=(c == 7))
                    P.op("dve", "tensor_tensor", out=rec, in0=pvv, in1=bv, op=ALU.add)
                    P.op("act", "copy", out=Vd[:, gt, :], in_=rec)
                    if sample or gt == nt - 1:
                        P.op("act", "copy", out=vf.rearrange("p (h f) -> p h f", h=4), in_=rec.rearrange("p (h d f) -> p h d f", h=4, d=2)[:, :, 0, :])
                        if not sample:
                            P.dma("sp", dr["swa_v_p"][0, tiles_seq].rearrange("t h d -> t (h d)"), vf)
                        else:
                            for s2 in range(2):
                                P.dma("sp", dr["swa_v_s"][0, 2 * gt + s2, 64:128].rearrange("t h d -> t (h d)"), vf[s2 * 64:(s2 + 1) * 64, :])
                for half in range(2):
                    load_q_w(half)
                    for p4 in range(4):
                        pr = half * 4 + p4
                        pz = PS[5].v[:, 0:n]
                        pzr = PS[6].v[:, 0:n]
                        for c in range(8):
                            P.mm(pz, wqh[:, c, p4 * 128:(p4 + 1) * 128], hT[:, c, 0:n], start=(c == 0), stop=(c == 7))
                        for c in range(8):
                            P.mm(pzr, wqrh[:, c, 2 * p4:2 * p4 + 2].rearrange("p a b d -> p (a b d)"), hT[:, c, 0:n], start=(c == 0), stop=(c == 7))
                        rope_evac(qT[:, pr, 0:n], pz, pzr, bq[:, pr:pr + 1], bqr[:, pr:pr + 1], tcl, n)

            def wo_apply(tlist):
                for k4, t in enumerate(tlist):
                    for dh in range(2):
                        po = PS[5 + dh].v
                        for pr in range(8):
                            P.mm(po, OT[:, pr, k4 * 128:(k4 + 1) * 128], wo[:, pr, dh * 512:(dh + 1) * 512], start=(pr == 0), stop=(pr == 7))
                        add_to_x(t, dh, po)

            if not sample:
                for qb in range(nt // 4):
                    sub = tiles[qb * 4:qb * 4 + 4]
                    norm_T(sub, dr["g_mix"][layer], hT, None, hb2)
                    project_block(qb * 512, sub)
                    for h in range(16):
                        kvh, half, pr = h // 4, h % 2, h // 2
                        r0 = 64 * half
                        chunks = []
                        for kc in range(max(4 * qb - 1, 0), 4 * qb + 4):
                            base = 128 * (kc - 4 * qb)
                            c0, c1 = max(0, base), min(512, base + 256)
                            fix = []
                            a, b_ = max(c0, base + 192), min(c1, base + 256)
                            if b_ > a:
                                fix.append((0, 64, a, b_))
                            a, b_ = max(c0, base), min(c1, base + 64)
                            if b_ > a:
                                fix.append((64, 128, a, b_))
                            chunks.append(dict(kl=[(kT[r0:r0 + 64, kvh, kc * 128:(kc + 1) * 128], qT[r0:r0 + 64, pr, c0:c1])],
                                               nk=128, c0=c0, c1=c1, fix=fix,
                                               vl=[Vd[:, kc, kvh * 128:(kvh + 1) * 128]]))

                        def out_fn(O, r, r0=r0, pr=pr):
                            P.op("dve", "tensor_tensor", out=OT[r0:r0 + 64, pr, :], in0=O[0][r0:r0 + 64, :], in1=r[r0:r0 + 64, :], op=ALU.mult)
                        softmax_group(chunks, 512, SWA_SCALE, 1, E3, rec, out_fn, sink=sk[:, h:h + 1], zb=h)
                    wo_apply(sub)
            else:
                norm_T(tiles, dr["g_mix"][layer], hT, None, hb2)
                project_block(0, tiles)
                ck = dr["cache_swa_k"][0]
                cv = dr["cache_swa_v"][0]
                for s in range(NSS):
                    P.dma("sp", dr["swa_k_s"][0, s, 0:64], ck[s, 64:128])
                    P.dma("sp", dr["swa_v_s"][0, s, 0:64], cv[s, 64:128])
                    kcb = E3[2].rearrange("p (h d f) -> p h d f", h=4, d=2)
                    for dup in range(2):
                        P.dma("pool", kcb[:, :, dup, :], ck[s])
                        P.dma("pool", Vd[:, nt, :].rearrange("p (h d f) -> p h d f", h=4, d=2)[:, :, dup, :], cv[s])
                    for kvh in range(4):
                        pv = psb(7)
                        P.tr(pv[:, 0:128], E3[2][:, kvh * 128:(kvh + 1) * 128], ident.v)
                        P.op("act", "copy", out=kT[:, kvh, T:T + 128], in_=pv[:, 0:128])
                    P.dma("sp", vown, Vd[(s % 2) * 64:(s % 2) * 64 + 64, s // 2, :])
                    oc = slice(s * 64, (s + 1) * 64)
                    for h in range(16):
                        kvh, half, pr = h // 4, h % 2, h // 2
                        r0 = 64 * half
                        chunks = [
                            dict(kl=[(kT[r0:r0 + 64, kvh, T:T + 128], qT[r0:r0 + 64, pr, oc])], nk=128, c0=0, c1=64, fix=[],
                                 vl=[Vd[:, nt, kvh * 128:(kvh + 1) * 128]]),
                            dict(kl=[(kT[r0:r0 + 64, kvh, oc], qT[r0:r0 + 64, pr, oc])], nk=64, c0=0, c1=64, fix=[],
                                 vl=[vown[:, kvh * 128:(kvh + 1) * 128]]),
                        ]

                        def out_fn(O, r, r0=r0, pr=pr, oc=oc):
                            P.op("dve", "tensor_tensor", out=OT[r0:r0 + 64, pr, oc], in0=O[0][r0:r0 + 64, :], in1=r[r0:r0 + 64, :], op=ALU.mult)
                        softmax_group(chunks, 64, SWA_SCALE, 1, E3[0:2] + [E3[1]], rec, out_fn, sink=sk[:, h:h + 1], zb=h)
                wo_apply(tiles)

        def sb(layer, tiles, sample):
            arena.reset()
            nt = len(tiles)
            T = nt * 128
            hT = arena.alloc([128, 8, T], BF16)
            junk = None
            hb2 = [arena.alloc([128, D], BF16) for _ in range(2)]
            kT = arena.alloc([128, 4, 2048 + 128], BF16)
            Vg = arena.alloc([128, 17, 512], BF16)
            qT = arena.alloc([128, 4, 512 if not sample else T], BF16)
            OT = arena.alloc([128, 4, 512 if not sample else T], BF16)
            w2 = [arena.alloc([128, 8, 512], BF16) for _ in range(2)]
            wo = arena.alloc([128, 4, D], BF16)
            ef2 = [arena.alloc([128, 512], F32) for _ in range(3)]
            sp2 = [arena.alloc([128, 512], BF16) for _ in range(3)]
            tf2 = [arena.alloc([128, 512], F32) for _ in range(3)]
            A2 = [arena.alloc([128, 512], BF16) for _ in range(3)]
            R = arena.alloc([128, 512], F32)
            of2 = [arena.alloc([128, 512], F32) for _ in range(2)]
            ob2 = [arena.alloc([128, 512], BF16) for _ in range(2)]
            vown = arena.alloc([64, 512], BF16) if sample else None
            kown = arena.alloc([128, 4, 256], BF16) if sample else None
            vownt = arena.alloc([128, 2, 512], BF16) if sample else None
            ZB = [0, 1, 5]
            RB = [2, 7, 6]
            Wv = wview(dr["w_sb_qkv"][0])
            norm_T(tiles, dr["g_mix"][layer], hT, junk, hb2)
            wi = [0]

            def load_w(col0):
                w = w2[wi[0] % 2]
                wi[0] += 1
                P.dma("pool", w, Wv[:, :, col0:col0 + 512])
                return w

            def proj_tile(w, k):
                pz = PS[5 + k % 2].v
                for c in range(8):
                    P.mm(pz, hT[:, c, k * 128:(k + 1) * 128], w[:, c, :], start=(c == 0), stop=(c == 7))
                return pz

            def to_T(dst, src_bf):
                pv = psb(7)
                for pr in range(4):
                    P.tr(pv[:, pr * 128:(pr + 1) * 128], src_bf[:, pr * 128:(pr + 1) * 128], ident.v)
                P.op("act", "copy", out=dst, in_=pv[:, 0:512].rearrange("p (a t) -> p a t", a=4))

            def unit_s1(d):
                z, nk, c0, c1, ui, mask = d["z"], d["nk"], d["c0"], d["c1"], d["ui"], d["mask"]
                d["zfn"]()
                zz = z[0:nk, c0:c1]
                ef = ef2[ui % 3][0:nk, c0:c1]
                sp = sp2[ui % 3][0:nk, c0:c1]
                P.op("act", "activation", out=ef, in_=zz, func=AF.Exp)
                P.op("act", "activation", out=sp, in_=ef, func=AF.Ln, bias=1.0, scale=1.0)
                if mask is not None:
                    mo, mv = mask
                    P.op("dve", "tensor_tensor", out=sp2[ui % 3][0:nk, mo], in0=sp2[ui % 3][0:nk, mo], in1=mv, op=ALU.mult)

            def unit_s1b(d):
                z, nk, c0, c1, ui = d["z"], d["nk"], d["c0"], d["c1"], d["ui"]
                zz = z[0:nk, c0:c1]
                sp = sp2[ui % 3][0:nk, c0:c1]
                P.mm(zz, negtri[0:nk, 0:nk], sp, start=False, stop=True, skip_group_check=True)
                rs = PS[RB[ui % 3]].v
                P.mm(rs[:, c0:c1], ones[0:nk, :], sp, start=True, stop=True)

            def unit_s2(d):
                z, nk, c0, c1, ui, mask = d["z"], d["nk"], d["c0"], d["c1"], d["ui"], d["mask"]
                zz = z[0:nk, c0:c1]
                tf = tf2[ui % 3][0:nk, c0:c1]
                A = A2[ui % 3][0:nk, c0:c1]
                rs = PS[RB[ui % 3]].v
                P.op("dve", "tensor_tensor", out=tf, in0=zz, in1=R[0:nk, c0:c1], op=ALU.subtract)
                P.op("act", "activation", out=A, in_=tf, func=AF.Exp)
                if mask is not None:
                    mo, mv = mask
                    P.op("dve", "tensor_tensor", out=A2[ui % 3][0:nk, mo], in0=A2[ui % 3][0:nk, mo], in1=mv, op=ALU.mult)
                P.op("dve", "tensor_tensor", out=R[:, c0:c1], in0=R[:, c0:c1], in1=rs[:, c0:c1], op=ALU.add)
                for (oap, vlhs, aap) in d["vl"](A2[ui % 3]):
                    P.mm(oap, vlhs, aap, start=False, stop=d["last"], skip_group_check=True)

            def run_units(units):
                n = len(units)
                for i in range(n + 2):
                    if i < n:
                        unit_s1(units[i])
                        unit_s1b(units[i])
                    if i >= 2:
                        unit_s2(units[i - 2])

            for g in range(2):
                P.dma("pool", wo, wview(dr["w_sb_o"][0])[:, 4 * g:4 * g + 4, :])
                for which in range(2):
                    w = load_w(1024 * (1 + which) + 512 * g)
                    for k, t in enumerate(tiles):
                        pz = proj_tile(w, k)
                        of = of2[k % 2]
                        ob = ob2[k % 2]
                        P.op("act", "copy", out=of, in_=pz)
                        name = ("sb_k_" if which == 0 else "sb_v_") + ("s" if sample else "p")
                        if not sample:
                            dst = dr[name][0, tiles_seq, k * 128:(k + 1) * 128, 8 * g:8 * g + 8].rearrange("t h d -> t (h d)")
                            P.dma("sp", dst, of)
                        else:
                            for s2 in range(2):
                                dst = dr[name][0, 2 * k + s2, :, 8 * g:8 * g + 8].rearrange("t h d -> t (h d)")
                                P.dma("sp", dst, of[s2 * 64:(s2 + 1) * 64, :])
                        if which == 0:
                            P.op("dve", "tensor_copy", out=ob, in_=of)
                            if not sample:
                                to_T(kT[:, :, k * 128:(k + 1) * 128], ob)
                            else:
                                to_T(kown[:, :, k * 128:(k + 1) * 128], ob)
                        else:
                            if not sample:
                                P.op("dve", "tensor_copy", out=Vg[:, k, :], in_=of)
                            else:
                                P.op("dve", "tensor_copy", out=vownt[:, k, :], in_=of)
                P.mark("sb_a")
                wq_ = load_w(512 * g)
                if not sample:
                    for qb in range(T // 512):
                        for k4 in range(4):
                            k = qb * 4 + k4
                            pz = proj_tile(wq_, k)
                            ob = ob2[k % 2]
                            P.op("act", "activation", out=ob, in_=pz, func=AF.Copy, scale=SB_SCALE)
                            to_T(qT[:, :, k4 * 128:(k4 + 1) * 128], ob)
                        ui = 0
                        for hh in range(8):
                            pr, half = hh // 2, hh % 2
                            r0 = 64 * half
                            P.emit("dve", lambda e, a=R.ap: e.memset(a, 0.0), [], R.bufs)
                            O = PS[3 + hh % 2].v
                            P.mm(O, zeros[:, 0:128], zeros, start=True, stop=False)
                            units = []
                            for kc in range(4 * qb + 3, -1, -1):
                                j = kc - 4 * qb
                                c0 = 128 * j if j >= 0 else 0
                                z = PS[ZB[ui % 3]].v
                                units.append(dict(
                                    z=z, nk=128, c0=c0, c1=512, ui=ui, last=(kc == 0),
                                    mask=(slice(c0, c0 + 128), tri01) if j >= 0 else None,
                                    zfn=lambda z=z, c0=c0, kc=kc, r0=r0, pr=pr: P.mm(z[:, c0:512], kT[r0:r0 + 64, pr, kc * 128:(kc + 1) * 128], qT[r0:r0 + 64, pr, c0:512], start=True, stop=True),
                                    vl=lambda Ab, O=O, kc=kc, pr=pr, c0=c0: [(O[:, c0:512], Vg[:, kc, pr * 128:(pr + 1) * 128], Ab[:, c0:512])]))
                                ui += 1
                            run_units(units)
                            P.op("act", "copy", out=OT[r0:r0 + 64, pr, :], in_=O[r0:r0 + 64, :])
                        for k4 in range(4):
                            t = tiles[qb * 4 + k4]
                            for dh in range(2):
                                po = PS[5 + dh].v
                                for pr in range(4):
                                    P.mm(po, OT[:, pr, k4 * 128:(k4 + 1) * 128], wo[:, pr, dh * 512:(dh + 1) * 512], start=(pr == 0), stop=(pr == 3))
                                add_to_x(t, dh, po)
                else:
                    for k, t in enumerate(tiles):
                        pz = proj_tile(wq_, k)
                        ob = A2[k % 2]
                        P.op("act", "activation", out=ob, in_=pz, func=AF.Copy, scale=SB_SCALE)
                        to_T(qT[:, :, k * 128:(k + 1) * 128], ob)
                    ck = dr["cache_sb_k"][0]
                    cv = dr["cache_sb_v"][0]
                    ui = 0
                    P.mark("sb_b")
                    for s in range(NSS):
                        P.dma("pool", Vg[:, 0:16, :], ck[s, :, 8 * g:8 * g + 8].rearrange("(t p) h d -> p t (h d)", p=128))
                        for kt in range(16):
                            to_T(kT[:, :, kt * 128:(kt + 1) * 128], Vg[:, kt, :])
                        P.dma("pool", Vg[:, 0:16, :], cv[s, :, 8 * g:8 * g + 8].rearrange("(t p) h d -> p t (h d)", p=128))
                        P.dma("sp", vown, vownt[(s % 2) * 64:(s % 2) * 64 + 64, s // 2, :])
                        oc = slice((s % 2) * 64, (s % 2) * 64 + 64)
                        qc = slice(s * 64, (s + 1) * 64)
                        P.mark("sb_c")
                        P.emit("dve", lambda e, a=R.ap: e.memset(a, 0.0), [], R.bufs)
                        O = PS[3 + s % 2].v
                        P.mm(O, zeros[:, 0:128], zeros, start=True, stop=False)
                        units = []
                        for kc in range(16, -1, -1):
                            nk = 64 if kc == 16 else 128
                            for half in range(2):
                                r0 = 64 * half
                                cb0 = half * 256
                                z = PS[ZB[ui % 3]].v

                                def zfn(z=z, nk=nk, cb0=cb0, r0=r0, kc=kc, qc=qc):
                                    P.mm(z[0:nk, cb0:cb0 + 256], zeros[:, 0:nk], zeros[:, 0:256], start=True, stop=False)
                                    for pr in range(4):
                                        ksrc = kown[r0:r0 + 64, pr, qc] if kc == 16 else kT[r0:r0 + 64, pr, kc * 128:(kc + 1) * 128]
                                        P.mm(z[0:nk, cb0 + pr * 64:cb0 + (pr + 1) * 64], ksrc, qT[r0:r0 + 64, pr, qc], start=False, stop=(pr == 3), skip_group_check=True)

                                def vl(Ab, O=O, kc=kc, nk=nk, cb0=cb0):
                                    res = []
                                    for pr in range(4):
                                        vsrc = vown[:, pr * 128:(pr + 1) * 128] if kc == 16 else Vg[:, kc, pr * 128:(pr + 1) * 128]
                                        res.append((O[:, cb0 + pr * 64:cb0 + (pr + 1) * 64], vsrc, Ab[0:nk, cb0 + pr * 64:cb0 + (pr + 1) * 64]))
                                    return res
                                units.append(dict(z=z, nk=nk, c0=cb0, c1=cb0 + 256, ui=ui, last=(kc == 0), zfn=zfn, vl=vl,
                                                  mask=(slice(cb0, cb0 + 256), msk_s[0:64, 0:256]) if kc == 16 else None))
                                ui += 1
                        run_units(units)
                        for hh in range(8):
                            pr, half = hh // 2, hh % 2
                            r0 = 64 * half
                            P.op("act", "copy", out=OT[r0:r0 + 64, pr, qc], in_=O[r0:r0 + 64, half * 256 + pr * 64:half * 256 + (pr + 1) * 64])
                    for k, t in enumerate(tiles):
                        for dh in range(2):
                            po = PS[5 + dh].v
                            for pr in range(4):
                                P.mm(po, OT[:, pr, k * 128:(k + 1) * 128], wo[:, pr, dh * 512:(dh + 1) * 512], start=(pr == 0), stop=(pr == 3))
                            add_to_x(t, dh, po)

        def final_norm(tiles, dst_fn):
            arena.reset()
            yb2 = [arena.alloc([128, D], F32) for _ in range(2)]
            junk = arena.alloc([128, D], BF16)
            P.dma("sp", gb.v, dr["g_final"].partition_broadcast(128))
            for k, t in enumerate(tiles):
                P.op("act", "activation", out=junk, in_=X[t].v, func=AF.Square, scale=1.0 / 32.0, accum_out=ssb[:, k:k + 1])
            rstd_from_ms(len(tiles), 0)
            for k, t in enumerate(tiles):
                yb = yb2[k % 2]
                P.op("dve", "scalar_tensor_tensor", out=yb, in0=X[t].v, scalar=rstd[:, k:k + 1], in1=gb.v, op0=ALU.mult, op1=ALU.mult)
                P.dma("sp", dst_fn(k), yb)

        def run_pass(sample, si):
            nonlocal tiles_seq
            tiles_seq = si
            if not sample:
                tiles = list(range(16))
                for t in tiles:
                    P.dma("sp", X[t].v, dr["x_prompt"][si, t * 128:(t + 1) * 128, :])
            else:
                tiles = [0, 1]
                for t in tiles:
                    P.dma("sp", X[t].v, dr["x_sample"][2 * t:2 * t + 2].rearrange("s t d -> (s t) d"))
            for layer in range(CFG["depth"]):
                m = layer % 3
                if m == 0:
                    mla(layer, tiles, sample)
                elif m == 1:
                    sb(layer, tiles, sample)
                else:
                    swa(layer, tiles, sample)
                P.mark("mix%d" % layer)
                if not sample:
                    ffn(layer, tiles, 1, 512, None, dr["ffn_conv_p"][layer, si:si + 1])
                else:
                    ffn(layer, tiles, 4, 64, dr["state_ffn_conv"][layer], dr["ffn_conv_s"][layer])
                P.mark("ffn%d" % layer)
                if not sample:
                    ple(layer, tiles, lambda t, layer=layer: dr["p_prompt"][layer, si, t * 128:(t + 1) * 128, :])
                else:
                    ple(layer, tiles, lambda t, layer=layer: dr["p_sample"][layer, 2 * t:2 * t + 2].rearrange("s t f -> (s t) f"))
                P.mark("ple%d" % layer)
            if not sample:
                final_norm(tiles, lambda k: dr["y_prompt"][si, k * 128:(k + 1) * 128, :])
            else:
                final_norm(tiles, lambda k: dr["y_sample"][2 * k:2 * k + 2].rearrange("s t d -> (s t) d"))

        if CFG["sample"]:
            run_pass(True, 0)
            P.stopped = False
        for si in range(CFG["nprompt"]):
            run_pass(False, si)
            P.stopped = False
        P.finalize(block, sems, dsems)
    return nc


def _consts():
    c = {}
    c["c_ident"] = np.eye(128, dtype=np.float32)
    half = 32
    inv = (10000.0 ** (-np.arange(half, dtype=np.float32) / half)).astype(np.float32)
    pos_t = np.concatenate([np.arange(2048), 2048 + (np.arange(128) % 64)]).astype(np.float32)
    ang = pos_t[:, None] * inv[None, :]
    c["c_cost"] = np.cos(ang).astype(np.float32)
    c["c_sint"] = np.sin(ang).astype(np.float32)
    pos_f = np.concatenate([np.arange(2048), 2048 + (np.arange(512) % 64)]).astype(np.float32)
    angf = (inv[:, None] * pos_f[None, :]).astype(np.float32)
    cf, sf = np.cos(angf).astype(np.float32), np.sin(angf).astype(np.float32)
    c["c_cos2"] = np.concatenate([cf, cf, cf, cf], 0)
    c["c_sin2"] = np.concatenate([-sf, sf, -sf, sf], 0)
    k = np.arange(128)[:, None]
    q = np.arange(128)[None, :]
    tri01 = (k < q).astype(np.float32)
    negtri = -(k >= q).astype(np.float32)
    ones = np.ones((128, 128), np.float32)
    zeros = np.zeros((128, 512), np.float32)
    m64 = np.zeros((128, 64), np.float32)
    m64[0:64, :] = (np.arange(64)[:, None] < np.arange(64)[None, :])
    msk_s = np.tile(m64, (1, 8))
    c["c_msk"] = np.concatenate([tri01, negtri, ones, zeros, msk_s], 1).astype(np.float32)
    return c


_NC = None
OUT_NAMES = ["y_prompt", "y_sample", "mla_ckv_p", "mla_krope_p", "mla_ckv_s", "mla_krope_s",
             "sb_k_p", "sb_v_p", "sb_k_s", "sb_v_s", "swa_k_p", "swa_v_p", "swa_k_s", "swa_v_s",
             "ffn_conv_p", "ffn_conv_s"]
BATCH_AXIS = {"x_prompt": 0, "x_sample": 0, "p_prompt": 1, "p_sample": 1, "cache_mla_ckv": 1, "cache_mla_krope": 1,
              "cache_sb_k": 1, "cache_sb_v": 1, "cache_swa_k": 1, "cache_swa_v": 1, "state_ffn_conv": 1}


def kernel(**inputs):
    global _NC
    if _NC is None:
        _NC = build_program()
    nc = _NC
    consts = _consts()
    in_maps = []
    for c in range(NCORES):
        m = dict(consts)
        for name, arr in inputs.items():
            a = np.asarray(arr, dtype=np.float32)
            if name in BATCH_AXIS:
                ax = BATCH_AXIS[name]
                sl = [slice(None)] * a.ndim
                sl[ax] = slice(4 * c, 4 * c + 4)
                a = np.ascontiguousarray(a[tuple(sl)])
            m[name] = a
        in_maps.append(m)
    res = run_bass_kernel_spmd(nc, in_maps, core_ids=list(range(NCORES)))
    outs = []
    for name in OUT_NAMES:
        ax = 0 if name in ("y_prompt", "y_sample") else 1
        outs.append(np.concatenate([np.asarray(r[name]) for r in res.results], axis=ax).astype(np.float32))
    return tuple(outs)
```

```python
from contextlib import ExitStack
import numpy as np
import concourse.bass as bass
import concourse.mybir as mybir
from concourse.bass_utils import run_bass_kernel_spmd

F32 = mybir.dt.float32
BF16 = mybir.dt.bfloat16
AF = mybir.ActivationFunctionType
ALU = mybir.AluOpType

NCORES = 8
NPS = 4
NSS = 4
SEQ = 2048
TS = 64
D = 1024
DEPTH = 4
DFF = 2816
NJ = 22
EPS = 1e-6
MLA_SCALE = 192 ** -0.5
SB_SCALE = 0.125
SWA_SCALE = 0.125

SAME_ENG_SYNC = True
NDSEM = 24
ENGS = ["pe", "act", "dve", "pool", "sp"]
GRAN = 256


class Buf:
    __slots__ = ("ap", "lw", "rd", "rdd")

    def __init__(self, ap):
        self.ap = ap
        self.lw = None
        self.rd = {}
        self.rdd = []

    def __getitem__(self, k):
        return V([self], self.ap[k])

    @property
    def v(self):
        return V([self], self.ap)


class V:
    __slots__ = ("bufs", "ap")

    def __init__(self, bufs, ap):
        self.bufs = bufs
        self.ap = ap

    def __getitem__(self, k):
        return V(self.bufs, self.ap[k])

    def rearrange(self, pattern_, **kw):
        return V(self.bufs, self.ap.rearrange(pattern_, **kw))

    def bitcast(self, dt):
        return V(self.bufs, self.ap.bitcast(dt))


class Ins:
    __slots__ = ("fn", "deps", "dma", "sig", "sigval", "slot", "dval", "prevd")

    def __init__(self, fn, deps, dma):
        self.fn = fn
        self.deps = deps
        self.dma = dma
        self.sig = False
        self.sigval = 0
        self.slot = -1
        self.dval = 0
        self.prevd = 0


CFG = {"sample": True, "nprompt": NPS, "depth": DEPTH, "stop": None, "sb_dummy": 4}


class Prog:
    def __init__(self, nc):
        self.nc = nc
        self.ins = {e: [] for e in ENGS}
        self.stopped = False

    def mark(self, label):
        if CFG["stop"] == label:
            self.stopped = True

    def emit(self, eng, fn, reads=(), writes=(), dma=False):
        if self.stopped:
            return None
        lst = self.ins[eng]
        idx = len(lst)
        node = (eng, idx)
        deps = set()
        for b in reads:
            if b.lw is not None:
                deps.add(b.lw)
        for b in writes:
            if b.lw is not None:
                deps.add(b.lw)
            for e, i in b.rd.items():
                deps.add((e, i))
            for n in b.rdd:
                deps.add(n)
        for b in reads:
            if dma:
                b.rdd.append(node)
            elif b.rd.get(eng, -1) < idx:
                b.rd[eng] = idx
        for b in writes:
            b.lw = node
            b.rd = {}
            b.rdd = []
        deps.discard(node)
        lst.append(Ins(fn, deps, dma))
        return node

    def op(self, eng, method, **kw):
        reads, writes, real = [], [], {}
        for k, v in kw.items():
            if isinstance(v, V):
                (writes if k in ("out", "accum_out") else reads).extend(v.bufs)
                real[k] = v.ap
            else:
                real[k] = v
        return self.emit(eng, lambda e: getattr(e, method)(**real), reads, writes)

    def mm(self, out, lhsT, rhs, start=True, stop=True, **kw):
        o, l, r = out.ap, lhsT.ap, rhs.ap
        return self.emit("pe", lambda e: e.matmul(o, lhsT=l, rhs=r, start=start, stop=stop, **kw),
                         lhsT.bufs + rhs.bufs, out.bufs)

    def tr(self, out, in_, ident):
        o, i, d = out.ap, in_.ap, ident.ap
        return self.emit("pe", lambda e: e.transpose(o, i, d), in_.bufs + ident.bufs, out.bufs)

    def dma(self, q, out, in_, nc_ok=False):
        reads, writes = [], []
        o, i = out, in_
        if isinstance(out, V):
            writes = out.bufs
            o = out.ap
        if isinstance(in_, V):
            reads = in_.bufs
            i = in_.ap
        if nc_ok:
            fn = lambda e: e.dma_start(out=o, in_=i, allow_slow_non_contiguous=True)
        else:
            fn = lambda e: e.dma_start(out=o, in_=i)
        return self.emit(q, fn, reads, writes, dma=True)

    def finalize(self, block, sems, dsems):
        ins = self.ins
        for e in ENGS:
            for x in ins[e]:
                for (f, j) in x.deps:
                    y = ins[f][j]
                    if y.dma:
                        continue
                    if f == e and (e == "pe" or not SAME_ENG_SYNC):
                        continue
                    y.sig = True
        final_dma = {}
        for e in ENGS:
            c = 0
            nd = 0
            last = [0] * NDSEM
            for x in ins[e]:
                if x.dma:
                    x.slot = nd % NDSEM
                    x.prevd = last[x.slot]
                    last[x.slot] += 16
                    x.dval = last[x.slot]
                    nd += 1
                elif x.sig:
                    c += 1
                    x.sigval = c
            final_dma[e] = last

        def run(e, eng):
            waited = {}
            for x in ins[e]:
                need = {}
                for (f, j) in x.deps:
                    y = ins[f][j]
                    if y.dma:
                        key = ("d", f, y.slot)
                        val = y.dval
                    else:
                        if f == e and (e == "pe" or not SAME_ENG_SYNC):
                            continue
                        key = ("c", f)
                        val = y.sigval
                    if need.get(key, 0) < val:
                        need[key] = val
                if x.dma and x.prevd > 0:
                    key = ("d", e, x.slot)
                    if need.get(key, 0) < x.prevd:
                        need[key] = x.prevd
                for key, val in need.items():
                    if waited.get(key, 0) >= val:
                        continue
                    waited[key] = val
                    sem = sems[key[1]] if key[0] == "c" else dsems[key[1]][key[2]]
                    eng.wait_ge(sem, val)
                r = x.fn(eng)
                if x.dma:
                    r.then_inc(dsems[e][x.slot], 16)
                elif x.sig:
                    r.then_inc(sems[e], 1)
            if e == "sp":
                for q in ENGS:
                    for s, val in enumerate(final_dma[q]):
                        if val > 0 and waited.get(("d", q, s), 0) < val:
                            eng.wait_ge(dsems[q][s], val)

        @block.tensor
        def _(eng):
            run("pe", eng)

        @block.scalar
        def _(eng):
            run("act", eng)

        @block.vector
        def _(eng):
            run("dve", eng)

        @block.gpsimd
        def _(eng):
            run("pool", eng)

        @block.sync
        def _(eng):
            run("sp", eng)


class Arena:
    def __init__(self, ap, nwords):
        self.ap = ap
        self.n = nwords // GRAN
        self.bufs = [Buf(ap[:, i * GRAN:(i + 1) * GRAN]) for i in range(self.n)]
        self.off = 0

    def reset(self):
        self.off = 0

    def alloc(self, shape, dt):
        free = int(np.prod(shape[1:]))
        words = free if dt == F32 else (free + 1) // 2
        g = (words + GRAN - 1) // GRAN
        g0 = self.off
        assert g0 + g <= self.n, ("arena overflow", g0, g, self.n)
        self.off += g
        ap = self.ap[0:shape[0], g0 * GRAN:g0 * GRAN + words]
        if dt != F32:
            ap = ap.bitcast(dt)[:, 0:free]
        if len(shape) > 2:
            names = "abcdefg"[:len(shape) - 1]
            pat = "p (" + " ".join(names) + ") -> p " + " ".join(names)
            ap = ap.rearrange(pat, **{n: shape[i + 1] for i, n in enumerate(names[:-1])})
        return V(self.bufs[g0:g0 + g], ap)


def build_program():
    nc = bass.Bass("TRN2", target_bir_lowering=False)
    dr = {}

    def din(name, shape):
        dr[name] = nc.dram_tensor(name, list(shape), F32, kind="ExternalInput").ap()

    def dout(name, shape):
        dr[name] = nc.dram_tensor(name, list(shape), F32, kind="ExternalOutput").ap()

    din("x_prompt", (NPS, SEQ, D)); din("x_sample", (NSS, TS, D))
    din("p_prompt", (DEPTH, NPS, SEQ, 256)); din("p_sample", (DEPTH, NSS, TS, 256))
    din("cache_mla_ckv", (2, NSS, 2048, 256)); din("cache_mla_krope", (2, NSS, 2048, 64))
    din("cache_sb_k", (1, NSS, 2048, 16, 64)); din("cache_sb_v", (1, NSS, 2048, 16, 64))
    din("cache_swa_k", (1, NSS, 128, 4, 64)); din("cache_swa_v", (1, NSS, 128, 4, 64))
    din("state_ffn_conv", (DEPTH, NSS, 2, DFF))
    din("g_mix", (DEPTH, D)); din("g_ffn", (DEPTH, D)); din("g_ple", (DEPTH, D)); din("g_final", (D,))
    din("w_mla_dq", (2, D, 384)); din("g_mla_q", (2, 384)); din("w_mla_uq", (2, 384, 3072))
    din("w_mla_dkv", (2, D, 320)); din("g_mla_kv", (2, 256))
    din("w_mla_uk", (2, 256, 16, 128)); din("w_mla_uv", (2, 256, 16, 128)); din("w_mla_o", (2, 2048, D))
    din("w_sb_qkv", (1, D, 3072)); din("w_sb_o", (1, D, D))
    din("w_swa_qkv", (1, D, 1536)); din("b_swa_qkv", (1, 1536)); din("swa_sinks", (1, 16)); din("w_swa_o", (1, D, D))
    din("w_ffn_in", (DEPTH, D, 2 * DFF)); din("ffn_conv_w", (DEPTH, 3, DFF)); din("ffn_conv_b", (DEPTH, DFF))
    din("w_ffn_out", (DEPTH, DFF, D)); din("w_ple_gate", (DEPTH, D, D)); din("w_ple_proj", (DEPTH, 256, D))
    din("c_ident", (128, 128)); din("c_cost", (17 * 128, 32)); din("c_sint", (17 * 128, 32))
    din("c_cos2", (128, 2560)); din("c_sin2", (128, 2560)); din("c_msk", (128, 1408))
    dout("y_prompt", (NPS, SEQ, D)); dout("y_sample", (NSS, TS, D))
    dout("mla_ckv_p", (2, NPS, SEQ, 256)); dout("mla_krope_p", (2, NPS, SEQ, 64))
    dout("mla_ckv_s", (2, NSS, TS, 256)); dout("mla_krope_s", (2, NSS, TS, 64))
    dout("sb_k_p", (1, NPS, SEQ, 16, 64)); dout("sb_v_p", (1, NPS, SEQ, 16, 64))
    dout("sb_k_s", (1, NSS, TS, 16, 64)); dout("sb_v_s", (1, NSS, TS, 16, 64))
    dout("swa_k_p", (1, NPS, 128, 4, 64)); dout("swa_v_p", (1, NPS, 128, 4, 64))
    dout("swa_k_s", (1, NSS, 128, 4, 64)); dout("swa_v_s", (1, NSS, 128, 4, 64))
    dout("ffn_conv_p", (DEPTH, NPS, 2, DFF)); dout("ffn_conv_s", (DEPTH, NSS, 2, DFF))

    es = ExitStack()
    with es:
        def sbt(name, shape, dt):
            return es.enter_context(nc.sbuf_tensor(name, shape, dt))

        P = Prog(nc)
        xt = sbt("xres", [128, 16, D], F32)
        X = [Buf(xt[:, i, :]) for i in range(16)]
        ident_f = Buf(sbt("identf", [128, 128], F32)[:])
        ident = Buf(sbt("identb", [128, 128], BF16)[:])
        cost = Buf(sbt("cost", [128, 17, 32], F32)[:])
        sint = Buf(sbt("sint", [128, 17, 32], F32)[:])
        msk = Buf(sbt("msk", [128, 1408], BF16)[:])
        gb = Buf(sbt("gb", [128, D], F32)[:])
        ssb = Buf(sbt("ssb", [128, 16], F32)[:])
        rstd = Buf(sbt("rstd", [128, 16], F32)[:])
        sm1 = Buf(sbt("sm1", [128, 8], F32)[:])
        own_ckvT = Buf(sbt("ownck", [128, 2, 256], BF16)[:]).v
        own_krT = Buf(sbt("ownkr", [64, 256], BF16)[:]).v
        AW = 130 * GRAN
        arena = Arena(sbt("arena", [128, AW], F32)[:], AW)
        PS = [Buf(es.enter_context(nc.psum_tensor(f"ps{i}", [128, 512], F32))[:]) for i in range(8)]
        sems = {e: es.enter_context(nc.semaphore("s_" + e)) for e in ENGS}
        dsems = {e: [es.enter_context(nc.semaphore(f"d_{e}{i}")) for i in range(NDSEM)] for e in ["sp", "pool", "act"]}
        block = es.enter_context(nc.Block())

        tri01 = msk[:, 0:128]
        negtri = msk[:, 128:256]
        ones = msk[:, 256:384]
        zeros = msk[:, 384:896]
        msk_s = msk[:, 896:1408]

        P.dma("sp", ident_f.v, dr["c_ident"])
        P.op("dve", "tensor_copy", out=ident.v, in_=ident_f.v)
        P.dma("sp", cost.v, dr["c_cost"].rearrange("(t p) f -> p t f", p=128))
        P.dma("sp", sint.v, dr["c_sint"].rearrange("(t p) f -> p t f", p=128))
        P.dma("pool", msk.v, dr["c_msk"])

        def wview(w2d):
            return w2d.rearrange("(c p) n -> p c n", p=128)

        def psb(i):
            return PS[i].v.bitcast(BF16)

        def rstd_from_ms(ntl, col0):
            P.op("act", "activation", out=rstd[:, col0:col0 + ntl], in_=ssb[:, col0:col0 + ntl], func=AF.Ln, bias=EPS, scale=1.0)
            P.op("act", "activation", out=rstd[:, col0:col0 + ntl], in_=rstd[:, col0:col0 + ntl], func=AF.Exp, scale=-0.5)

        def norm_T(tiles, g_ap, hT, junk, hb2):
            P.dma("sp", gb.v, g_ap.partition_broadcast(128))
            n = len(tiles)
            for k, t in enumerate(tiles):
                P.op("act", "activation", out=hb2[k % 2], in_=X[t].v, func=AF.Square, scale=1.0 / 32.0, accum_out=ssb[:, k:k + 1])
            rstd_from_ms(n, 0)
            for k, t in enumerate(tiles):
                hb = hb2[k % 2]
                P.op("dve", "scalar_tensor_tensor", out=hb, in0=X[t].v, scalar=rstd[:, k:k + 1], in1=gb.v, op0=ALU.mult, op1=ALU.mult)
                bank = 6 + (k % 2)
                pv = psb(bank).rearrange("p (c t) -> p c t", c=8)
                for c in range(8):
                    P.tr(pv[:, c, :], hb[:, c * 128:(c + 1) * 128], ident.v)
                P.op("act", "copy", out=hT[:, :, k * 128:(k + 1) * 128], in_=pv)

        def add_to_x(t, dh, ps):
            P.op("dve", "tensor_tensor", out=X[t][:, dh * 512:(dh + 1) * 512], in0=X[t][:, dh * 512:(dh + 1) * 512], in1=ps, op=ALU.add)

        def ple(layer, tiles, p_src):
            arena.reset()
            nt = len(tiles)
            TH = min(nt, 8)
            hT = arena.alloc([128, 8, TH * 128], BF16)
            junk = arena.alloc([128, D], BF16)
            hb2 = [arena.alloc([128, D], BF16) for _ in range(2)]
            wg = arena.alloc([128, 8, D], BF16)
            wp = arena.alloc([128, 2, D], BF16)
            pb2 = [arena.alloc([128, 256], BF16) for _ in range(2)]
            pT2 = [arena.alloc([128, 2, 128], BF16) for _ in range(2)]
            sg2 = [arena.alloc([128, 512], F32) for _ in range(2)]
            tt2 = [arena.alloc([128, 512], F32) for _ in range(2)]
            P.dma("pool", wg, wview(dr["w_ple_gate"][layer]))
            P.dma("pool", wp, wview(dr["w_ple_proj"][layer]))
            for h0 in range(0, nt, TH):
                sub = tiles[h0:h0 + TH]
                norm_T(sub, dr["g_ple"][layer], hT, junk, hb2)
                for k, t in enumerate(sub):
                    pb = pb2[k % 2]
                    pT = pT2[k % 2]
                    P.dma("pool", pb, p_src(t))
                    pv = psb(5).rearrange("p (c t) -> p c t", c=8)
                    for c in range(2):
                        P.tr(pv[:, c, :], pb[:, c * 128:(c + 1) * 128], ident.v)
                    P.op("act", "copy", out=pT, in_=pv[:, 0:2, :])
                    for dh in range(2):
                        pa = PS[dh].v
                        pbk = PS[2 + dh].v
                        for c in range(8):
                            P.mm(pa, hT[:, c, k * 128:(k + 1) * 128], wg[:, c, dh * 512:(dh + 1) * 512], start=(c == 0), stop=(c == 7))
                        for c in range(2):
                            P.mm(pbk, pT[:, c, :], wp[:, c, dh * 512:(dh + 1) * 512], start=(c == 0), stop=(c == 1))
                        sg = sg2[dh]
                        tt = tt2[dh]
                        P.op("act", "activation", out=sg, in_=pa, func=AF.Sigmoid)
                        P.op("dve", "tensor_tensor", out=tt, in0=sg, in1=pbk, op=ALU.mult)
                        P.op("dve", "tensor_tensor", out=X[t][:, dh * 512:(dh + 1) * 512], in0=X[t][:, dh * 512:(dh + 1) * 512], in1=tt, op=ALU.add)

        def ffn(layer, tiles, nseg, L, state_src, conv_dst):
            arena.reset()
            nt = len(tiles)
            TH = min(nt, 8)
            NB = nseg * L
            hT = arena.alloc([128, 8, TH * 128], BF16)
            junk = None
            hb2 = [arena.alloc([128, D], BF16) for _ in range(2)]
            actT = arena.alloc([128, NJ, TH * 128], BF16)
            wi2 = [arena.alloc([128, 8, 512], BF16) for _ in range(2)]
            wo2 = [arena.alloc([128, NJ, 256], BF16) for _ in range(2)]
            G2 = [arena.alloc([128, nseg, L + 2], F32) for _ in range(2)]
            t12 = [arena.alloc([128, nseg, L], F32) for _ in range(2)]
            ge2 = [arena.alloc([128, nseg, L], F32) for _ in range(2)]
            carry = arena.alloc([128, NJ, nseg, 2], F32)
            cw = arena.alloc([128, NJ, 3], F32)
            cb = arena.alloc([128, NJ], F32)
            for r in range(3):
                P.dma("sp", cw[:, :, r], dr["ffn_conv_w"][layer, r].rearrange("(j p) -> p j", p=128), nc_ok=True)
            P.dma("sp", cb, dr["ffn_conv_b"][layer].rearrange("(j p) -> p j", p=128), nc_ok=True)
            if state_src is None:
                P.op("dve", "memset", out=carry, constant=0.0) if False else P.emit("dve", lambda e, a=carry.ap: e.memset(a, 0.0), [], carry.bufs)
            else:
                for sg in range(nseg):
                    for r in range(2):
                        P.dma("sp", carry[:, :, sg, r], state_src[sg, r].rearrange("(j p) -> p j", p=128), nc_ok=True)
            win = wview(dr["w_ffn_in"][layer])
            wov = dr["w_ffn_out"][layer].rearrange("(j p) n -> p j n", p=128)
            uc = 0
            for h0 in range(0, nt, TH):
                sub = tiles[h0:h0 + TH]
                ntok = len(sub) * 128
                norm_T(sub, dr["g_ffn"][layer], hT, junk, hb2)
                nblk = ntok // NB
                work = []
                for grp in range(NJ // 2):
                    for u in range(2):
                        for blk in range(nblk):
                            work.append((grp, u, blk))

                def stage_a(idx, grp, u, blk):
                    wi = wi2[grp % 2]
                    if u == 0 and blk == 0:
                        P.dma("pool", wi[:, :, 0:256], win[:, :, grp * 256:(grp + 1) * 256])
                        P.dma("pool", wi[:, :, 256:512], win[:, :, DFF + grp * 256:DFF + (grp + 1) * 256])
                    j = grp * 2 + u
                    cs = slice(blk * NB, (blk + 1) * NB)
                    pg = PS[idx % 2].v[:, 0:NB]
                    pu = PS[2 + idx % 2].v[:, 0:NB]
                    G = G2[idx % 2]
                    t1 = t12[idx % 2]
                    for c in range(8):
                        P.mm(pg, wi[:, c, u * 128:(u + 1) * 128], hT[:, c, cs], start=(c == 0), stop=(c == 7))
                    for c in range(8):
                        P.mm(pu, wi[:, c, 256 + u * 128:256 + (u + 1) * 128], hT[:, c, cs], start=(c == 0), stop=(c == 7))
                    P.op("act", "copy", out=G[:, :, 0:2], in_=carry[:, j])
                    P.op("act", "copy", out=G[:, :, 2:L + 2], in_=pg.rearrange("p (s l) -> p s l", s=nseg))
                    P.op("dve", "tensor_copy", out=carry[:, j], in_=G[:, :, L:L + 2])
                    P.op("act", "activation", out=t1, in_=G[:, :, 2:L + 2], func=AF.Identity, scale=cw[:, j, 2:3], bias=cb[:, j:j + 1])
                    P.op("dve", "scalar_tensor_tensor", out=t1, in0=G[:, :, 1:L + 1], scalar=cw[:, j, 1:2], in1=t1, op0=ALU.mult, op1=ALU.add)
                    P.op("dve", "scalar_tensor_tensor", out=t1, in0=G[:, :, 0:L], scalar=cw[:, j, 0:1], in1=t1, op0=ALU.mult, op1=ALU.add)

                def stage_b(idx, grp, u, blk):
                    j = grp * 2 + u
                    cs = slice(blk * NB, (blk + 1) * NB)
                    pu = PS[2 + idx % 2].v[:, 0:NB]
                    t1 = t12[idx % 2]
                    ge = ge2[idx % 2]
                    P.op("act", "activation", out=ge, in_=t1, func=AF.Gelu)
                    P.op("dve", "tensor_tensor", out=actT[:, j, cs].rearrange("p (s l) -> p s l", s=nseg), in0=ge, in1=pu.rearrange("p (s l) -> p s l", s=nseg), op=ALU.mult)

                for i, wk_ in enumerate(work):
                    stage_a(uc + i, *wk_)
                    if i > 0:
                        stage_b(uc + i - 1, *work[i - 1])
                stage_b(uc + len(work) - 1, *work[-1])
                uc += len(work)
                for dq in range(4):
                    wo = wo2[dq % 2]
                    P.dma("pool", wo, wov[:, :, dq * 256:(dq + 1) * 256])
                    for k, t in enumerate(sub):
                        po = PS[4 + (dq * 8 + k) % 2].v[:, 0:256]
                        for j in range(NJ):
                            P.mm(po, actT[:, j, k * 128:(k + 1) * 128], wo[:, j, :], start=(j == 0), stop=(j == NJ - 1))
                        xs = X[t][:, dq * 256:(dq + 1) * 256]
                        P.op("dve", "tensor_tensor", out=xs, in0=xs, in1=po, op=ALU.add)
            for sg in range(nseg):
                for r in range(2):
                    P.dma("sp", conv_dst[sg, r].rearrange("(j p) -> p j", p=128), carry[:, :, sg, r], nc_ok=True)

        def softmax_group(chunks, N, scale, ndv, E3, rec, out_fn, sink=None, zb=0):
            den = PS[2].v[:, 0:N]
            O = [PS[3 + d].v[:, 0:N] for d in range(ndv)]
            zr = zeros[:, 0:N]
            P.mm(den, zeros[:, 0:128], zr, start=True, stop=False)
            for d in range(ndv):
                P.mm(O[d], zeros[:, 0:128], zr, start=True, stop=False)
            nch = len(chunks)

            def stage_z(ci):
                ch = chunks[ci]
                nk, c0, c1 = ch["nk"], ch["c0"], ch["c1"]
                z = PS[(zb + ci) % 2].v[0:nk, c0:c1]
                E = E3[ci % 3][0:nk, c0:c1]
                kl = ch["kl"]
                for i, (l, r) in enumerate(kl):
                    P.mm(z, l, r, start=(i == 0), stop=(i == len(kl) - 1))
                P.op("act", "activation", out=E, in_=z, func=AF.Exp, scale=scale)
                for (p0, p1, a, b) in ch["fix"]:
                    P.emit("dve", lambda e, ap=E3[ci % 3][p0:p1, a:b].ap: e.memset(ap, 0.0), [], E.bufs)

            def stage_v(ci):
                ch = chunks[ci]
                nk, c0, c1 = ch["nk"], ch["c0"], ch["c1"]
                E = E3[ci % 3][0:nk, c0:c1]
                last = (ci == nch - 1)
                P.mm(den[:, c0:c1], ones[0:nk, :], E, start=False, stop=last, skip_group_check=True)
                for d in range(ndv):
                    P.mm(O[d][:, c0:c1], ch["vl"][d], E, start=False, stop=last, skip_group_check=True)

            for ci in range(nch):
                stage_z(ci)
                if ci > 0:
                    stage_v(ci - 1)
            stage_v(nch - 1)
            if sink is not None:
                P.op("act", "activation", out=rec[:, 0:N], in_=den, func=AF.Ln, bias=sink, scale=1.0)
            else:
                P.op("act", "activation", out=rec[:, 0:N], in_=den, func=AF.Ln)
            P.op("act", "activation", out=rec[:, 0:N], in_=rec[:, 0:N], func=AF.Exp, scale=-1.0)
            out_fn(O, rec[:, 0:N])

        def mla(layer, tiles, sample):
            jl = layer // 3
            arena.reset()
            nt = len(tiles)
            T = nt * 128
            NK = 2112 if sample else 2048
            TW = 512 if sample else 2048
            tc0 = 2048 if sample else 0
            ckvT = arena.alloc([128, 2, NK], BF16)
            krT = arena.alloc([64, NK], BF16)
            ckvt = arena.alloc([128, 17 if sample else 16, 256], BF16)
            cqT = arena.alloc([128, 3, T], BF16)
            cos2 = arena.alloc([64, TW], BF16)
            sin2 = arena.alloc([64, TW], BF16)
            wukT = arena.alloc([128, 16, 256], BF16)
            wuv = arena.alloc([128, 2, 2048], BF16)
            mark = arena.off
            wo = arena.alloc([128, 16, D], BF16)
            OVT = arena.alloc([128, 16, 512 if not sample else T], BF16)
            wq4 = [arena.alloc([128, 3, 256], BF16) for _ in range(3)]
            E3 = [arena.alloc([128, 512], BF16) for _ in range(3)]
            rec = arena.alloc([128, 512], F32)
            qn2 = [arena.alloc([128, 512], BF16) for _ in range(2)]
            qr2 = [arena.alloc([64, 512], BF16) for _ in range(2)]
            qlat2 = [arena.alloc([128, 2, 512], BF16) for _ in range(2)]
            olat = arena.alloc([128, 2, 512], BF16)
            tq2 = [arena.alloc([64, 512], F32) for _ in range(2)]
            krc = arena.alloc([128, 16, 64], BF16) if sample else None
            arena.off = mark
            TH = min(nt, 8)
            hT = arena.alloc([128, 8, TH * 128], BF16)
            hb2 = [arena.alloc([128, D], BF16) for _ in range(2)]
            wdq = arena.alloc([128, 8, 384], BF16)
            wdkv = arena.alloc([128, 8, 320], BF16)
            gq = arena.alloc([128, 384], F32)
            gkv = arena.alloc([128, 256], F32)
            wuk = arena.alloc([128, 2, 2048], BF16)
            ckvf2 = [arena.alloc([128, 256], F32) for _ in range(2)]
            krf2 = [arena.alloc([128, 64], F32) for _ in range(2)]
            krt = arena.alloc([128, 4, 32], F32)
            krb2 = [arena.alloc([128, 64], BF16) for _ in range(2)]
            cqb2 = [arena.alloc([128, 384], BF16) for _ in range(2)]
            ckb_s = arena.alloc([128, 256], BF16)
            sqj = arena.alloc([128, 640], BF16)
            P.dma("pool", cos2, dr["c_cos2"][0:64, tc0:tc0 + TW])
            P.dma("pool", sin2, dr["c_sin2"][0:64, tc0:tc0 + TW])
            P.dma("pool", wdq, wview(dr["w_mla_dq"][jl]))
            P.dma("pool", wdkv, wview(dr["w_mla_dkv"][jl]))
            P.dma("sp", gq, dr["g_mla_q"][jl].partition_broadcast(128))
            P.dma("sp", gkv, dr["g_mla_kv"][jl].partition_broadcast(128))
            P.dma("pool", wuk, wview(dr["w_mla_uk"][jl].rearrange("l h n -> l (h n)")))
            P.dma("pool", wuv, wview(dr["w_mla_uv"][jl].rearrange("l h n -> l (h n)")))
            for h in range(16):
                pv = psb(5).rearrange("p (c t) -> p c t", c=8)
                for lc in range(2):
                    P.tr(pv[:, lc, :], wuk[:, lc, h * 128:(h + 1) * 128], ident.v)
                P.op("act", "copy", out=wukT[:, h, :].rearrange("p (c t) -> p c t", c=2), in_=pv[:, 0:2, :])
            for h0 in range(0, nt, TH):
                sub = tiles[h0:h0 + TH]
                norm_T(sub, dr["g_mix"][layer], hT, None, hb2)
                for k, t in enumerate(sub):
                    gt = h0 + k
                    ptile = 16 if sample else gt
                    hs = [hT[:, c, k * 128:(k + 1) * 128] for c in range(8)]
                    pkv = PS[0].v[:, 0:320]
                    pcq = PS[1].v[:, 0:384]
                    for c in range(8):
                        P.mm(pkv, hs[c], wdkv[:, c, :], start=(c == 0), stop=(c == 7))
                    for c in range(8):
                        P.mm(pcq, hs[c], wdq[:, c, :], start=(c == 0), stop=(c == 7))
                    P.op("act", "activation", out=sqj[:, 0:256], in_=pkv[:, 0:256], func=AF.Square, scale=1.0 / 16.0, accum_out=ssb[:, 8:9])
                    P.op("act", "activation", out=sqj[:, 256:640], in_=pcq, func=AF.Square, scale=384 ** -0.5, accum_out=ssb[:, 9:10])
                    rstd_from_ms(2, 8)
                    ckvf = ckvf2[k % 2]
                    krf = krf2[k % 2]
                    krb = krb2[k % 2]
                    cqb = cqb2[k % 2]
                    P.op("dve", "scalar_tensor_tensor", out=ckvf, in0=pkv[:, 0:256], scalar=rstd[:, 8:9], in1=gkv, op0=ALU.mult, op1=ALU.mult)
                    P.op("dve", "scalar_tensor_tensor", out=cqb, in0=pcq, scalar=rstd[:, 9:10], in1=gq, op0=ALU.mult, op1=ALU.mult)
                    x1 = pkv[:, 256:288]
                    x2 = pkv[:, 288:320]
                    cs_, sn_ = cost[:, ptile, :], sint[:, ptile, :]
                    P.op("dve", "tensor_tensor", out=krt[:, 0, :], in0=x1, in1=cs_, op=ALU.mult)
                    P.op("dve", "tensor_tensor", out=krt[:, 1, :], in0=x2, in1=sn_, op=ALU.mult)
                    P.op("dve", "tensor_tensor", out=krt[:, 2, :], in0=x2, in1=cs_, op=ALU.mult)
                    P.op("dve", "tensor_tensor", out=krt[:, 3, :], in0=x1, in1=sn_, op=ALU.mult)
                    P.op("dve", "tensor_tensor", out=krf[:, 0:32], in0=krt[:, 0, :], in1=krt[:, 1, :], op=ALU.subtract)
                    P.op("dve", "tensor_tensor", out=krf[:, 32:64], in0=krt[:, 2, :], in1=krt[:, 3, :], op=ALU.add)
                    if not sample:
                        si = tiles_seq
                        P.dma("sp", dr["mla_ckv_p"][jl, si, gt * 128:(gt + 1) * 128, :], ckvf)
                        P.dma("sp", dr["mla_krope_p"][jl, si, gt * 128:(gt + 1) * 128, :], krf)
                        ckb = ckvt[:, gt, :]
                    else:
                        P.dma("sp", dr["mla_ckv_s"][jl, 2 * gt:2 * gt + 2].rearrange("s t l -> (s t) l"), ckvf)
                        P.dma("sp", dr["mla_krope_s"][jl, 2 * gt:2 * gt + 2].rearrange("s t l -> (s t) l"), krf)
                        ckb = ckb_s
                    P.op("act", "copy", out=ckb, in_=ckvf)
                    P.op("act", "copy", out=krb, in_=krf)
                    pv = psb(5).rearrange("p (c t) -> p c t", c=8)
                    P.tr(pv[:, 0, :], ckb[:, 0:128], ident.v)
                    P.tr(pv[:, 1, :], ckb[:, 128:256], ident.v)
                    P.tr(pv[0:64, 2, :], krb, ident.v)
                    for c in range(3):
                        P.tr(pv[:, 3 + c, :], cqb[:, c * 128:(c + 1) * 128], ident.v)
                    if not sample:
                        kc0 = gt * 128
                        P.op("act", "copy", out=ckvT[:, :, kc0:kc0 + 128], in_=pv[:, 0:2, :])
                        P.op("act", "copy", out=krT[:, kc0:kc0 + 128], in_=pv[0:64, 2, :])
                    else:
                        P.op("act", "copy", out=own_ckvT[:, :, gt * 128:(gt + 1) * 128], in_=pv[:, 0:2, :])
                        P.op("act", "copy", out=own_krT[:, gt * 128:(gt + 1) * 128], in_=pv[0:64, 2, :])
                    P.op("act", "copy", out=cqT[:, :, gt * 128:(gt + 1) * 128], in_=pv[:, 3:6, :])
            P.mark("mla1")
            P.dma("pool", wo, wview(dr["w_mla_o"][jl]))
            uqv = dr["w_mla_uq"][jl].rearrange("(c p) n -> p c n", p=128)
            wqi = [0]

            def load_wq(h):
                wq = wq4[wqi[0] % 3]
                wqi[0] += 1
                P.dma("pool", wq[:, :, 0:192], uqv[:, :, h * 192:(h + 1) * 192])
                P.dma("pool", wq[:, :, 192:224], uqv[:, :, h * 192 + 160:h * 192 + 192])
                P.dma("pool", wq[:, :, 224:256], uqv[:, :, h * 192 + 128:h * 192 + 160])
                return wq

            def qproj(h, qcols, ocols):
                wq = load_wq(h)
                pq = PS[5].v[:, ocols]
                pr = PS[6].v[0:64, ocols]
                pt = PS[7].v[0:64, ocols]
                for c in range(3):
                    P.mm(pq, wq[:, c, 0:128], cqT[:, c, qcols], start=(c == 0), stop=(c == 2))
                for c in range(3):
                    P.mm(pr, wq[:, c, 128:192], cqT[:, c, qcols], start=(c == 0), stop=(c == 2))
                for c in range(3):
                    P.mm(pt, wq[:, c, 192:256], cqT[:, c, qcols], start=(c == 0), stop=(c == 2))

            def qfinish(N, tcols, b=0):
                qn, qr = qn2[b], qr2[b]
                P.op("act", "copy", out=qn[:, 0:N], in_=PS[5].v[:, 0:N])
                P.op("dve", "tensor_tensor", out=tq2[0][:, 0:N], in0=PS[6].v[0:64, 0:N], in1=cos2[:, tcols], op=ALU.mult)
                P.op("dve", "tensor_tensor", out=tq2[1][:, 0:N], in0=PS[7].v[0:64, 0:N], in1=sin2[:, tcols], op=ALU.mult)
                P.op("dve", "tensor_tensor", out=qr[:, 0:N], in0=tq2[0][:, 0:N], in1=tq2[1][:, 0:N], op=ALU.add)

            def qlat_from_qn(hlist, N, b=0):
                qn, qlat = qn2[b], qlat2[b]
                w = N // len(hlist)
                for lc in range(2):
                    pl = PS[5 + lc].v
                    for i, h in enumerate(hlist):
                        P.mm(pl[:, i * w:(i + 1) * w], wukT[:, h, lc * 128:(lc + 1) * 128], qn[:, i * w:(i + 1) * w], start=True, stop=True)
                    P.op("act", "copy", out=qlat[:, lc, 0:N], in_=pl[:, 0:N])

            if not sample:
                items = [(qb, h) for qb in range(T // 512) for h in range(16)]

                def prep(i):
                    qb, h = items[i]
                    qc = slice(qb * 512, (qb + 1) * 512)
                    qproj(h, qc, slice(0, 512))
                    qfinish(512, qc, i % 2)
                    qlat_from_qn([h], 512, i % 2)

                def attend(i):
                    qb, h = items[i]
                    qlat, qr = qlat2[i % 2], qr2[i % 2]
                    chunks = []
                    for kc in range(4 * qb + 4):
                        j = kc - 4 * qb
                        c0 = 128 * j if j >= 0 else 0
                        kcs = slice(kc * 128, (kc + 1) * 128)
                        chunks.append(dict(
                            kl=[(ckvT[:, 0, kcs], qlat[:, 0, c0:512]), (ckvT[:, 1, kcs], qlat[:, 1, c0:512]), (krT[:, kcs], qr[:, c0:512])],
                            nk=128, c0=c0, c1=512,
                            fix=[(64, 128, c0, c0 + 64)] if j >= 0 else [],
                            vl=[ckvt[:, kc, 0:128], ckvt[:, kc, 128:256]]))

                    def out_fn(O, r, h=h):
                        for lc in range(2):
                            P.op("dve", "tensor_tensor", out=olat[:, lc, :], in0=O[lc], in1=r, op=ALU.mult)
                        pv_ = PS[7].v
                        for lc in range(2):
                            P.mm(pv_, wuv[:, lc, h * 128:(h + 1) * 128], olat[:, lc, :], start=(lc == 0), stop=(lc == 1))
                        P.op("act", "copy", out=OVT[:, h, :], in_=pv_)
                    softmax_group(chunks, 512, MLA_SCALE, 2, E3, rec, out_fn, zb=h)

                prep(0)
                for i, (qb, h) in enumerate(items):
                    if i + 1 < len(items):
                        prep(i + 1)
                    attend(i)
                    if h == 15:
                        for k4 in range(4):
                            t = tiles[qb * 4 + k4]
                            for dh in range(2):
                                po = PS[5 + dh].v
                                for hh in range(16):
                                    P.mm(po, OVT[:, hh, k4 * 128:(k4 + 1) * 128], wo[:, hh, dh * 512:(dh + 1) * 512], start=(hh == 0), stop=(hh == 15))
                                add_to_x(t, dh, po)
            else:
                for s in range(NSS):
                    P.dma("pool", ckvt[:, 0:16, :], dr["cache_mla_ckv"][jl, s].rearrange("(t p) l -> p t l", p=128))
                    P.dma("pool", krc, dr["cache_mla_krope"][jl, s].rearrange("(t p) f -> p t f", p=128))
                    for kt in range(16):
                        pv = psb(5 + kt % 2).rearrange("p (c t) -> p c t", c=8)
                        P.tr(pv[:, 0, :], ckvt[:, kt, 0:128], ident.v)
                        P.tr(pv[:, 1, :], ckvt[:, kt, 128:256], ident.v)
                        P.tr(pv[0:64, 2, :], krc[:, kt, :], ident.v)
                        P.op("act", "copy", out=ckvT[:, :, kt * 128:(kt + 1) * 128], in_=pv[:, 0:2, :])
                        P.op("act", "copy", out=krT[:, kt * 128:(kt + 1) * 128], in_=pv[0:64, 2, :])
                    oc = slice(s * 64, (s + 1) * 64)
                    P.op("act", "copy", out=ckvT[:, :, 2048:2112], in_=own_ckvT[:, :, oc])
                    P.op("act", "copy", out=krT[:, 2048:2112], in_=own_krT[:, oc])
                    pv = psb(5).rearrange("p (c t) -> p c t", c=8)
                    for lc in range(2):
                        P.tr(pv[0:64, lc, :], ckvT[:, lc, 2048:2112], ident.v)
                    P.op("act", "copy", out=ckvt[0:64, 16, :].rearrange("p (c t) -> p c t", c=2), in_=pv[0:64, 0:2, :])
                    for g in range(2):
                        hl = list(range(8 * g, 8 * g + 8))
                        for i, h in enumerate(hl):
                            qproj(h, oc, slice(i * 64, (i + 1) * 64))
                        qfinish(512, slice(0, 512))
                        qlat_from_qn(hl, 512)
                        chunks = []
                        for kc in range(17):
                            nk = 128 if kc < 16 else 64
                            kcs = slice(kc * 128, kc * 128 + nk)
                            chunks.append(dict(
                                kl=[(ckvT[:, 0, kcs], qlat2[0][:, 0, :]), (ckvT[:, 1, kcs], qlat2[0][:, 1, :]), (krT[:, kcs], qr2[0][:, :])],
                                nk=nk, c0=0, c1=512, fix=[],
                                vl=[ckvt[0:nk, kc, 0:128], ckvt[0:nk, kc, 128:256]]))

                        def out_fn(O, r, hl=hl, s=s, g=g):
                            for lc in range(2):
                                P.op("dve", "tensor_tensor", out=olat[:, lc, :], in0=O[lc], in1=r, op=ALU.mult)
                            pv_ = PS[7].v
                            for i, h in enumerate(hl):
                                for lc in range(2):
                                    P.mm(pv_[:, i * 64:(i + 1) * 64], wuv[:, lc, h * 128:(h + 1) * 128], olat[:, lc, i * 64:(i + 1) * 64], start=(lc == 0), stop=(lc == 1))
                            P.op("act", "copy", out=OVT[:, 8 * g:8 * g + 8, s * 64:(s + 1) * 64], in_=pv_.rearrange("p (h q) -> p h q", h=8))
                        softmax_group(chunks, 512, MLA_SCALE, 2, E3, rec, out_fn, zb=g)
                for k, t in enumerate(tiles):
                    for dh in range(2):
                        po = PS[5 + dh].v
                        for h in range(16):
                            P.mm(po, OVT[:, h, k * 128:(k + 1) * 128], wo[:, h, dh * 512:(dh + 1) * 512], start=(h == 0), stop=(h == 15))
                        add_to_x(t, dh, po)

        tiles_seq = 0

        def swa(layer, tiles, sample):
            arena.reset()
            nt = len(tiles)
            T = nt * 128
            TW = 512 if sample else 2048
            tc0 = 2048 if sample else 0
            BW = 256 if sample else 512
            KX = 128 if sample else 0
            wo = arena.alloc([128, 8, D], BF16)
            cos2 = arena.alloc([128, TW], BF16)
            sin2 = arena.alloc([128, TW], BF16)
            bq = arena.alloc([128, 8], F32)
            bqr = arena.alloc([128, 8], F32)
            bk = arena.alloc([128, 4], F32)
            bkr = arena.alloc([128, 4], F32)
            bvs = arena.alloc([128, 256], F32)
            bv = arena.alloc([128, 512], F32)
            sk = arena.alloc([128, 16], F32)
            kT = arena.alloc([128, 4, T + KX], BF16)
            Vd = arena.alloc([128, nt + (1 if sample else 0), 512], BF16)
            qT = arena.alloc([128, 8, BW], BF16)
            OT = arena.alloc([128, 8, BW], BF16)
            hT = arena.alloc([128, 8, BW], BF16)
            hb2 = [arena.alloc([128, D], BF16) for _ in range(2)]
            E3 = [arena.alloc([128, 512], BF16) for _ in range(3)]
            rec = arena.alloc([128, 512], F32)
            ta = [arena.alloc([128, 512], F32) for _ in range(2)]
            kf = arena.alloc([128, 4, 128], F32)
            vf = arena.alloc([128, 256], F32)
            kto = arena.alloc([128, 2, 256], F32) if sample else arena.alloc([128, 1, 256], F32)
            vown = arena.alloc([64, 512], BF16) if sample else None
            knat_s = arena.alloc([128, 8, 512], BF16) if sample else None
            mark = arena.off
            wk = arena.alloc([128, 8, 4, 2, 64], BF16)
            wkr = arena.alloc([128, 8, 4, 4, 32], BF16)
            wv = arena.alloc([128, 8, 4, 2, 64], BF16)
            arena.off = mark
            wqh = arena.alloc([128, 8, 512], BF16)
            wqrh = arena.alloc([128, 8, 8, 2, 32], BF16)
            W = dr["w_swa_qkv"][0]
            wv_ = wview(W)
            w5 = W.rearrange("(c p) (h two f) -> p c h two f", p=128, two=2, f=32)
            w4 = W.rearrange("(c p) (h d) -> p c h d", p=128, d=64)
            P.dma("pool", wo, wview(dr["w_swa_o"][0]))
            P.dma("pool", cos2, dr["c_cos2"][:, tc0:tc0 + TW])
            P.dma("pool", sin2, dr["c_sin2"][:, tc0:tc0 + TW])
            B = dr["b_swa_qkv"][0]
            P.dma("sp", bq, B[0:1024].rearrange("(c p) -> p c", p=128), nc_ok=True)
            b3 = B.rearrange("(h two f) -> two f h", two=2, f=32)
            for hh in range(2):
                for two in range(2):
                    r0 = hh * 64 + two * 32
                    P.dma("sp", bqr[r0:r0 + 32, :], b3[1 - two, :, hh:16:2], nc_ok=True)
                    P.dma("sp", bk[r0:r0 + 32, :], b3[two, :, 16:20], nc_ok=True)
                    P.dma("sp", bkr[r0:r0 + 32, :], b3[1 - two, :, 16:20], nc_ok=True)
            P.dma("sp", bvs, B[1280:1536].partition_broadcast(128))
            bv4 = bv.rearrange("p (h d f) -> p h d f", h=4, d=2)
            for dup in range(2):
                P.op("dve", "tensor_copy", out=bv4[:, :, dup, :], in_=bvs.rearrange("p (h f) -> p h f", h=4))
            P.dma("sp", sk, dr["swa_sinks"][0].partition_broadcast(128))
            P.op("act", "activation", out=sk, in_=sk, func=AF.Exp)

            def load_kv_w():
                knat = knat_s if sample else qT
                P.dma("pool", knat, wv_[:, :, 1024:1536])
                kn4 = knat[:, :, 0:256].rearrange("p c (h d) -> p c h d", h=4)
                vn4 = knat[:, :, 256:512].rearrange("p c (h d) -> p c h d", h=4)
                kn5 = knat[:, :, 0:256].rearrange("p c (h two f) -> p c h two f", h=4, two=2)
                for dup in range(2):
                    P.op("dve", "tensor_copy", out=wk[:, :, :, dup, :], in_=kn4)
                    P.op("act", "copy", out=wv[:, :, :, dup, :], in_=vn4)
                    for two in range(2):
                        P.op("dve" if two == 0 else "act", "tensor_copy" if two == 0 else "copy", out=wkr[:, :, :, dup * 2 + two, :], in_=kn5[:, :, :, 1 - two, :])

            def load_q_w(half):
                P.dma("pool", wqh, wv_[:, :, half * 512:(half + 1) * 512])
                q5 = wqh.rearrange("p c (h two f) -> p c h two f", h=8, two=2)
                for two in range(2):
                    P.op("dve" if two == 0 else "act", "tensor_copy" if two == 0 else "copy", out=wqrh[:, :, :, two, :], in_=q5[:, :, :, 1 - two, :])

            def rope_evac(out, pz, pzr, b, br, tcols, n, f32out=None):
                P.op("act", "activation", out=ta[0][:, 0:n], in_=pz, func=AF.Identity, bias=b, scale=1.0)
                P.op("act", "activation", out=ta[1][:, 0:n], in_=pzr, func=AF.Identity, bias=br, scale=1.0)
                P.op("dve", "tensor_tensor", out=ta[0][:, 0:n], in0=ta[0][:, 0:n], in1=cos2[:, tcols], op=ALU.mult)
                P.op("dve", "tensor_tensor", out=ta[1][:, 0:n], in0=ta[1][:, 0:n], in1=sin2[:, tcols], op=ALU.mult)
                P.op("dve", "tensor_tensor", out=out, in0=ta[0][:, 0:n], in1=ta[1][:, 0:n], op=ALU.add)
                if f32out is not None:
                    P.op("dve", "tensor_tensor", out=f32out[:, 0:n], in0=ta[0][:, 0:n], in1=ta[1][:, 0:n], op=ALU.add)

            def project_block(t0, btiles):
                n = len(btiles) * 128
                tcl = slice(t0, t0 + n) if not sample else slice(0, n)
                lastb = (t0 + n == T)
                load_kv_w()
                for kvh in range(4):
                    pz = PS[5].v[:, 0:n]
                    pzr = PS[6].v[:, 0:n]
                    for c in range(8):
                        P.mm(pz, wk[:, c, kvh].rearrange("p a b -> p (a b)"), hT[:, c, 0:n], start=(c == 0), stop=(c == 7))
                    for c in range(8):
                        P.mm(pzr, wkr[:, c, kvh].rearrange("p a b -> p (a b)"), hT[:, c, 0:n], start=(c == 0), stop=(c == 7))
                    want32 = sample or lastb
                    rope_evac(kT[:, kvh, t0:t0 + n], pz, pzr, bk[:, kvh:kvh + 1], bkr[:, kvh:kvh + 1], tcl, n, f32out=rec if want32 else None)
                    if want32:
                        cols = [(n - 128, 0)] if not sample else [(k2 * 128, k2) for k2 in range(n // 128)]
                        for (cc, oi) in cols:
                            pt_ = PS[7].v[:, 0:128]
                            P.tr(pt_, rec[:, cc:cc + 128], ident_f.v)
                            P.op("act", "copy", out=kto[:, oi, kvh * 64:(kvh + 1) * 64], in_=pt_[:, 0:64])
                if sample:
                    for k2 in range(n // 128):
                        for s2 in range(2):
                            P.dma("sp", dr["swa_k_s"][0, 2 * k2 + s2, 64:128].rearrange("t h d -> t (h d)"), kto[s2 * 64:(s2 + 1) * 64, k2, :])
                elif lastb:
                    P.dma("sp", dr["swa_k_p"][0, tiles_seq].rearrange("t h d -> t (h d)"), kto[:, 0, :])
                for k, t in enumerate(btiles):
                    gt = t0 // 128 + k
                    pvv = PS[7].v
                    for c in range(8):
                        P.mm(pvv, hT[:, c, k * 128:(k + 1) * 128], wv[:, c].rearrange("p a b d -> p (a b d)"), start=(c == 0), stop=(c == 7))
                    P.op("dve", "tensor_tensor", out=rec, in0=pvv, in1=bv, op=ALU.add)
                    P.op("act", "copy", out=Vd[:, gt, :], in_=rec)
                    if sample or gt == nt - 1:
                        P.op("act", "copy", out=vf.rearrange("p (h f) -> p h f", h=4), in_=rec.rearrange("p (h d f) -> p h d f", h=4, d=2)[:, :, 0, :])
                        if not sample:
                            P.dma("sp", dr["swa_v_p"][0, tiles_seq].rearrange("t h d -> t (h d)"), vf)
                        else:
                            for s2 in range(2):
                                P.dma("sp", dr["swa_v_s"][0, 2 * gt + s2, 64:128].rearrange("t h d -> t (h d)"), vf[s2 * 64:(s2 + 1) * 64, :])
                for half in range(2):
                    load_q_w(half)
                    for p4 in range(4):
                        pr = half * 4 + p4
                        pz = PS[5].v[:, 0:n]
                        pzr = PS[6].v[:, 0:n]
                        for c in range(8):
                            P.mm(pz, wqh[:, c, p4 * 128:(p4 + 1) * 128], hT[:, c, 0:n], start=(c == 0), stop=(c == 7))
                        for c in range(8):
                            P.mm(pzr, wqrh[:, c, 2 * p4:2 * p4 + 2].rearrange("p a b d -> p (a b d)"), hT[:, c, 0:n], start=(c == 0), stop=(c == 7))
                        rope_evac(qT[:, pr, 0:n], pz, pzr, bq[:, pr:pr + 1], bqr[:, pr:pr + 1], tcl, n)

            def wo_apply(tlist):
                for k4, t in enumerate(tlist):
                    for dh in range(2):
                        po = PS[5 + dh].v
                        for pr in range(8):
                            P.mm(po, OT[:, pr, k4 * 128:(k4 + 1) * 128], wo[:, pr, dh * 512:(dh + 1) * 512], start=(pr == 0), stop=(pr == 7))
                        add_to_x(t, dh, po)

            if not sample:
                for qb in range(nt // 4):
                    sub = tiles[qb * 4:qb * 4 + 4]
                    norm_T(sub, dr["g_mix"][layer], hT, None, hb2)
                    project_block(qb * 512, sub)
                    for h in range(16):
                        kvh, half, pr = h // 4, h % 2, h // 2
                        r0 = 64 * half
                        chunks = []
                        for kc in range(max(4 * qb - 1, 0), 4 * qb + 4):
                            base = 128 * (kc - 4 * qb)
                            c0, c1 = max(0, base), min(512, base + 256)
                            fix = []
                            a, b_ = max(c0, base + 192), min(c1, base + 256)
                            if b_ > a:
                                fix.append((0, 64, a, b_))
                            a, b_ = max(c0, base), min(c1, base + 64)
                            if b_ > a:
                                fix.append((64, 128, a, b_))
                            chunks.append(dict(kl=[(kT[r0:r0 + 64, kvh, kc * 128:(kc + 1) * 128], qT[r0:r0 + 64, pr, c0:c1])],
                                               nk=128, c0=c0, c1=c1, fix=fix,
                                               vl=[Vd[:, kc, kvh * 128:(kvh + 1) * 128]]))

                        def out_fn(O, r, r0=r0, pr=pr):
                            P.op("dve", "tensor_tensor", out=OT[r0:r0 + 64, pr, :], in0=O[0][r0:r0 + 64, :], in1=r[r0:r0 + 64, :], op=ALU.mult)
                        softmax_group(chunks, 512, SWA_SCALE, 1, E3, rec, out_fn, sink=sk[:, h:h + 1], zb=h)
                    wo_apply(sub)
            else:
                norm_T(tiles, dr["g_mix"][layer], hT, None, hb2)
                project_block(0, tiles)
                ck = dr["cache_swa_k"][0]
                cv = dr["cache_swa_v"][0]
                for s in range(NSS):
                    P.dma("sp", dr["swa_k_s"][0, s, 0:64], ck[s, 64:128])
                    P.dma("sp", dr["swa_v_s"][0, s, 0:64], cv[s, 64:128])
                    kcb = E3[2].rearrange("p (h d f) -> p h d f", h=4, d=2)
                    for dup in range(2):
                        P.dma("pool", kcb[:, :, dup, :], ck[s])
                        P.dma("pool", Vd[:, nt, :].rearrange("p (h d f) -> p h d f", h=4, d=2)[:, :, dup, :], cv[s])
                    for kvh in range(4):
                        pv = psb(7)
                        P.tr(pv[:, 0:128], E3[2][:, kvh * 128:(kvh + 1) * 128], ident.v)
                        P.op("act", "copy", out=kT[:, kvh, T:T + 128], in_=pv[:, 0:128])
                    P.dma("sp", vown, Vd[(s % 2) * 64:(s % 2) * 64 + 64, s // 2, :])
                    oc = slice(s * 64, (s + 1) * 64)
                    for h in range(16):
                        kvh, half, pr = h // 4, h % 2, h // 2
                        r0 = 64 * half
                        chunks = [
                            dict(kl=[(kT[r0:r0 + 64, kvh, T:T + 128], qT[r0:r0 + 64, pr, oc])], nk=128, c0=0, c1=64, fix=[],
                                 vl=[Vd[:, nt, kvh * 128:(kvh + 1) * 128]]),
                            dict(kl=[(kT[r0:r0 + 64, kvh, oc], qT[r0:r0 + 64, pr, oc])], nk=64, c0=0, c1=64, fix=[],
                                 vl=[vown[:, kvh * 128:(kvh + 1) * 128]]),
                        ]

                        def out_fn(O, r, r0=r0, pr=pr, oc=oc):
                            P.op("dve", "tensor_tensor", out=OT[r0:r0 + 64, pr, oc], in0=O[0][r0:r0 + 64, :], in1=r[r0:r0 + 64, :], op=ALU.mult)
                        softmax_group(chunks, 64, SWA_SCALE, 1, E3[0:2] + [E3[1]], rec, out_fn, sink=sk[:, h:h + 1], zb=h)
                wo_apply(tiles)

        def sb(layer, tiles, sample):
            arena.reset()
            nt = len(tiles)
            T = nt * 128
            hT = arena.alloc([128, 8, T], BF16)
            junk = None
            hb2 = [arena.alloc([128, D], BF16) for _ in range(2)]
            kT = arena.alloc([128, 4, 2048 + 128], BF16)
            Vg = arena.alloc([128, 17, 512], BF16)
            qT = arena.alloc([128, 4, 512 if not sample else T], BF16)
            OT = arena.alloc([128, 4, 512 if not sample else T], BF16)
            w2 = [arena.alloc([128, 8, 512], BF16) for _ in range(2)]
            wo = arena.alloc([128, 4, D], BF16)
            ef2 = [arena.alloc([128, 512], F32) for _ in range(3)]
            sp2 = [arena.alloc([128, 512], BF16) for _ in range(3)]
            tf2 = [arena.alloc([128, 512], F32) for _ in range(3)]
            A2 = [arena.alloc([128, 512], BF16) for _ in range(3)]
            R = arena.alloc([128, 512], F32)
            of2 = [arena.alloc([128, 512], F32) for _ in range(2)]
            ob2 = [arena.alloc([128, 512], BF16) for _ in range(2)]
            vown = arena.alloc([64, 512], BF16) if sample else None
            kown = arena.alloc([128, 4, 256], BF16) if sample else None
            vownt = arena.alloc([128, 2, 512], BF16) if sample else None
            ZB = CFG.get("sb_zb", [0, 1, 5])
            RB = CFG.get("sb_rb", [2, 7, 6])
            Wv = wview(dr["w_sb_qkv"][0])
            norm_T(tiles, dr["g_mix"][layer], hT, junk, hb2)
            wi = [0]

            def load_w(col0):
                w = w2[wi[0] % 2]
                wi[0] += 1
                P.dma("pool", w, Wv[:, :, col0:col0 + 512])
                return w

            def proj_tile(w, k):
                pz = PS[5 + k % 2].v
                for c in range(8):
                    P.mm(pz, hT[:, c, k * 128:(k + 1) * 128], w[:, c, :], start=(c == 0), stop=(c == 7))
                return pz

            def to_T(dst, src_bf):
                pv = psb(7)
                for pr in range(4):
                    P.tr(pv[:, pr * 128:(pr + 1) * 128], src_bf[:, pr * 128:(pr + 1) * 128], ident.v)
                P.op("act", "copy", out=dst, in_=pv[:, 0:512].rearrange("p (a t) -> p a t", a=4))

            def unit_s1(d):
                z, nk, c0, c1, ui, mask = d["z"], d["nk"], d["c0"], d["c1"], d["ui"], d["mask"]
                d["zfn"]()
                for _ in range(CFG.get("sb_dummy", 0)):
                    P.mm(PS[d["dbank"]].v, zeros[:, 0:128], zeros, start=True, stop=True)
                zz = z[0:nk, c0:c1]
                ef = ef2[ui % 3][0:nk, c0:c1]
                sp = sp2[ui % 3][0:nk, c0:c1]
                P.op("act", "activation", out=ef, in_=zz, func=AF.Exp)
                P.op("act", "activation", out=sp, in_=ef, func=AF.Ln, bias=1.0, scale=1.0)
                if mask is not None:
                    mo, mv = mask
                    P.op("dve", "tensor_tensor", out=sp2[ui % 3][0:nk, mo], in0=sp2[ui % 3][0:nk, mo], in1=mv, op=ALU.mult)

            def unit_s1b(d):
                z, nk, c0, c1, ui = d["z"], d["nk"], d["c0"], d["c1"], d["ui"]
                zz = z[0:nk, c0:c1]
                sp = sp2[ui % 3][0:nk, c0:c1]
                P.mm(zz, negtri[0:nk, 0:nk], sp, start=False, stop=True, skip_group_check=True)
                rs = PS[RB[ui % 3]].v
                P.mm(rs[:, c0:c1], ones[0:nk, :], sp, start=True, stop=True)

            def unit_s2(d):
                z, nk, c0, c1, ui, mask = d["z"], d["nk"], d["c0"], d["c1"], d["ui"], d["mask"]
                zz = z[0:nk, c0:c1]
                tf = tf2[ui % 3][0:nk, c0:c1]
                A = A2[ui % 3][0:nk, c0:c1]
                rs = PS[RB[ui % 3]].v
                P.op("dve", "tensor_tensor", out=tf, in0=zz, in1=R[0:nk, c0:c1], op=ALU.subtract)
                P.op("act", "activation", out=A, in_=tf, func=AF.Exp)
                if mask is not None:
                    mo, mv = mask
                    P.op("dve", "tensor_tensor", out=A2[ui % 3][0:nk, mo], in0=A2[ui % 3][0:nk, mo], in1=mv, op=ALU.mult)
                P.op("dve", "tensor_tensor", out=R[:, c0:c1], in0=R[:, c0:c1], in1=rs[:, c0:c1], op=ALU.add)
                for (oap, vlhs, aap) in d["vl"](A2[ui % 3]):
                    P.mm(oap, vlhs, aap, start=False, stop=d["last"], skip_group_check=True)

            def run_units(units):
                n = len(units)
                for i in range(n + 2):
                    if CFG.get("sb_order", 0) == 1 and i >= 2:
                        unit_s2(units[i - 2])
                    if i < n:
                        unit_s1(units[i])
                        unit_s1b(units[i])
                    if CFG.get("sb_order", 0) == 0 and i >= 2:
                        unit_s2(units[i - 2])

            for g in range(2):
                P.dma("pool", wo, wview(dr["w_sb_o"][0])[:, 4 * g:4 * g + 4, :])
                for which in range(2):
                    w = load_w(1024 * (1 + which) + 512 * g)
                    for k, t in enumerate(tiles):
                        pz = proj_tile(w, k)
                        of = of2[k % 2]
                        ob = ob2[k % 2]
                        P.op("act", "copy", out=of, in_=pz)
                        name = ("sb_k_" if which == 0 else "sb_v_") + ("s" if sample else "p")
                        if not sample:
                            dst = dr[name][0, tiles_seq, k * 128:(k + 1) * 128, 8 * g:8 * g + 8].rearrange("t h d -> t (h d)")
                            P.dma("sp", dst, of)
                        else:
                            for s2 in range(2):
                                dst = dr[name][0, 2 * k + s2, :, 8 * g:8 * g + 8].rearrange("t h d -> t (h d)")
                                P.dma("sp", dst, of[s2 * 64:(s2 + 1) * 64, :])
                        if which == 0:
                            P.op("dve", "tensor_copy", out=ob, in_=of)
                            if not sample:
                                to_T(kT[:, :, k * 128:(k + 1) * 128], ob)
                            else:
                                to_T(kown[:, :, k * 128:(k + 1) * 128], ob)
                        else:
                            if not sample:
                                P.op("dve", "tensor_copy", out=Vg[:, k, :], in_=of)
                            else:
                                P.op("dve", "tensor_copy", out=vownt[:, k, :], in_=of)
                P.mark("sb_a")
                wq_ = load_w(512 * g)
                if not sample:
                    for qb in range(T // 512):
                        for k4 in range(4):
                            k = qb * 4 + k4
                            pz = proj_tile(wq_, k)
                            ob = ob2[k % 2]
                            P.op("act", "activation", out=ob, in_=pz, func=AF.Copy, scale=SB_SCALE)
                            to_T(qT[:, :, k4 * 128:(k4 + 1) * 128], ob)
                        ui = 0
                        for hh in range(8):
                            pr, half = hh // 2, hh % 2
                            r0 = 64 * half
                            P.emit("dve", lambda e, a=R.ap: e.memset(a, 0.0), [], R.bufs)
                            O = PS[3 + hh % 2].v
                            P.mm(O, zeros[:, 0:128], zeros, start=True, stop=False)
                            units = []
                            for kc in range(4 * qb + 3, -1, -1):
                                j = kc - 4 * qb
                                c0 = 128 * j if j >= 0 else 0
                                z = PS[ZB[ui % 3]].v
                                units.append(dict(
                                    z=z, nk=128, c0=c0, c1=512, ui=ui, last=(kc == 0), dbank=3 + (hh + 1) % 2,
                                    mask=(slice(c0, c0 + 128), tri01) if j >= 0 else None,
                                    zfn=lambda z=z, c0=c0, kc=kc, r0=r0, pr=pr: P.mm(z[:, c0:512], kT[r0:r0 + 64, pr, kc * 128:(kc + 1) * 128], qT[r0:r0 + 64, pr, c0:512], start=True, stop=True),
                                    vl=lambda Ab, O=O, kc=kc, pr=pr, c0=c0: [(O[:, c0:512], Vg[:, kc, pr * 128:(pr + 1) * 128], Ab[:, c0:512])]))
                                ui += 1
                            run_units(units)
                            P.op("act", "copy", out=OT[r0:r0 + 64, pr, :], in_=O[r0:r0 + 64, :])
                        for k4 in range(4):
                            t = tiles[qb * 4 + k4]
                            for dh in range(2):
                                po = PS[5 + dh].v
                                for pr in range(4):
                                    P.mm(po, OT[:, pr, k4 * 128:(k4 + 1) * 128], wo[:, pr, dh * 512:(dh + 1) * 512], start=(pr == 0), stop=(pr == 3))
                                add_to_x(t, dh, po)
                else:
                    for k, t in enumerate(tiles):
                        pz = proj_tile(wq_, k)
                        ob = A2[k % 2]
                        P.op("act", "activation", out=ob, in_=pz, func=AF.Copy, scale=SB_SCALE)
                        to_T(qT[:, :, k * 128:(k + 1) * 128], ob)
                    ck = dr["cache_sb_k"][0]
                    cv = dr["cache_sb_v"][0]
                    ui = 0
                    P.mark("sb_b")
                    for s in range(NSS):
                        P.dma("pool", Vg[:, 0:16, :], ck[s, :, 8 * g:8 * g + 8].rearrange("(t p) h d -> p t (h d)", p=128))
                        for kt in range(16):
                            to_T(kT[:, :, kt * 128:(kt + 1) * 128], Vg[:, kt, :])
                        P.dma("pool", Vg[:, 0:16, :], cv[s, :, 8 * g:8 * g + 8].rearrange("(t p) h d -> p t (h d)", p=128))
                        P.dma("sp", vown, vownt[(s % 2) * 64:(s % 2) * 64 + 64, s // 2, :])
                        oc = slice((s % 2) * 64, (s % 2) * 64 + 64)
                        qc = slice(s * 64, (s + 1) * 64)
                        P.mark("sb_c")
                        P.emit("dve", lambda e, a=R.ap: e.memset(a, 0.0), [], R.bufs)
                        O = PS[3 + s % 2].v
                        P.mm(O, zeros[:, 0:128], zeros, start=True, stop=False)
                        units = []
                        for kc in range(16, -1, -1):
                            nk = 64 if kc == 16 else 128
                            for half in range(2):
                                r0 = 64 * half
                                cb0 = half * 256
                                z = PS[ZB[ui % 3]].v

                                def zfn(z=z, nk=nk, cb0=cb0, r0=r0, kc=kc, qc=qc):
                                    P.mm(z[0:nk, cb0:cb0 + 256], zeros[:, 0:nk], zeros[:, 0:256], start=True, stop=False)
                                    for pr in range(4):
                                        ksrc = kown[r0:r0 + 64, pr, qc] if kc == 16 else kT[r0:r0 + 64, pr, kc * 128:(kc + 1) * 128]
                                        P.mm(z[0:nk, cb0 + pr * 64:cb0 + (pr + 1) * 64], ksrc, qT[r0:r0 + 64, pr, qc], start=False, stop=(pr == 3), skip_group_check=True)

                                def vl(Ab, O=O, kc=kc, nk=nk, cb0=cb0):
                                    res = []
                                    for pr in range(4):
                                        vsrc = vown[:, pr * 128:(pr + 1) * 128] if kc == 16 else Vg[:, kc, pr * 128:(pr + 1) * 128]
                                        res.append((O[:, cb0 + pr * 64:cb0 + (pr + 1) * 64], vsrc, Ab[0:nk, cb0 + pr * 64:cb0 + (pr + 1) * 64]))
                                    return res
                                units.append(dict(z=z, nk=nk, c0=cb0, c1=cb0 + 256, ui=ui, last=(kc == 0), zfn=zfn, vl=vl, dbank=3 + (s + 1) % 2,
                                                  mask=(slice(cb0, cb0 + 256), msk_s[0:64, 0:256]) if kc == 16 else None))
                                ui += 1
                        run_units(units)
                        for hh in range(8):
                            pr, half = hh // 2, hh % 2
                            r0 = 64 * half
                            P.op("act", "copy", out=OT[r0:r0 + 64, pr, qc], in_=O[r0:r0 + 64, half * 256 + pr * 64:half * 256 + (pr + 1) * 64])
                    for k, t in enumerate(tiles):
                        for dh in range(2):
                            po = PS[5 + dh].v
                            for pr in range(4):
                                P.mm(po, OT[:, pr, k * 128:(k + 1) * 128], wo[:, pr, dh * 512:(dh + 1) * 512], start=(pr == 0), stop=(pr == 3))
                            add_to_x(t, dh, po)

        def final_norm(tiles, dst_fn):
            arena.reset()
            yb2 = [arena.alloc([128, D], F32) for _ in range(2)]
            junk = arena.alloc([128, D], BF16)
            P.dma("sp", gb.v, dr["g_final"].partition_broadcast(128))
            for k, t in enumerate(tiles):
                P.op("act", "activation", out=junk, in_=X[t].v, func=AF.Square, scale=1.0 / 32.0, accum_out=ssb[:, k:k + 1])
            rstd_from_ms(len(tiles), 0)
            for k, t in enumerate(tiles):
                yb = yb2[k % 2]
                P.op("dve", "scalar_tensor_tensor", out=yb, in0=X[t].v, scalar=rstd[:, k:k + 1], in1=gb.v, op0=ALU.mult, op1=ALU.mult)
                P.dma("sp", dst_fn(k), yb)

        def run_pass(sample, si):
            nonlocal tiles_seq
            tiles_seq = si
            if not sample:
                tiles = list(range(16))
                for t in tiles:
                    P.dma("sp", X[t].v, dr["x_prompt"][si, t * 128:(t + 1) * 128, :])
            else:
                tiles = [0, 1]
                for t in tiles:
                    P.dma("sp", X[t].v, dr["x_sample"][2 * t:2 * t + 2].rearrange("s t d -> (s t) d"))
            for layer in range(CFG["depth"]):
                m = layer % 3
                if m == 0:
                    mla(layer, tiles, sample)
                elif m == 1:
                    sb(layer, tiles, sample)
                else:
                    swa(layer, tiles, sample)
                P.mark("mix%d" % layer)
                if not sample:
                    ffn(layer, tiles, 1, 512, None, dr["ffn_conv_p"][layer, si:si + 1])
                else:
                    ffn(layer, tiles, 4, 64, dr["state_ffn_conv"][layer], dr["ffn_conv_s"][layer])
                P.mark("ffn%d" % layer)
                if not sample:
                    ple(layer, tiles, lambda t, layer=layer: dr["p_prompt"][layer, si, t * 128:(t + 1) * 128, :])
                else:
                    ple(layer, tiles, lambda t, layer=layer: dr["p_sample"][layer, 2 * t:2 * t + 2].rearrange("s t f -> (s t) f"))
                P.mark("ple%d" % layer)
            if not sample:
                final_norm(tiles, lambda k: dr["y_prompt"][si, k * 128:(k + 1) * 128, :])
            else:
                final_norm(tiles, lambda k: dr["y_sample"][2 * k:2 * k + 2].rearrange("s t d -> (s t) d"))

        if CFG["sample"]:
            run_pass(True, 0)
            P.stopped = False
        for si in range(CFG["nprompt"]):
            run_pass(False, si)
            P.stopped = False
        P.finalize(block, sems, dsems)
    return nc


def _consts():
    c = {}
    c["c_ident"] = np.eye(128, dtype=np.float32)
    half = 32
    inv = (10000.0 ** (-np.arange(half, dtype=np.float32) / half)).astype(np.float32)
    pos_t = np.concatenate([np.arange(2048), 2048 + (np.arange(128) % 64)]).astype(np.float32)
    ang = pos_t[:, None] * inv[None, :]
    c["c_cost"] = np.cos(ang).astype(np.float32)
    c["c_sint"] = np.sin(ang).astype(np.float32)
    pos_f = np.concatenate([np.arange(2048), 2048 + (np.arange(512) % 64)]).astype(np.float32)
    angf = (inv[:, None] * pos_f[None, :]).astype(np.float32)
    cf, sf = np.cos(angf).astype(np.float32), np.sin(angf).astype(np.float32)
    c["c_cos2"] = np.concatenate([cf, cf, cf, cf], 0)
    c["c_sin2"] = np.concatenate([-sf, sf, -sf, sf], 0)
    k = np.arange(128)[:, None]
    q = np.arange(128)[None, :]
    tri01 = (k < q).astype(np.float32)
    negtri = -(k >= q).astype(np.float32)
    ones = np.ones((128, 128), np.float32)
    zeros = np.zeros((128, 512), np.float32)
    m64 = np.zeros((128, 64), np.float32)
    m64[0:64, :] = (np.arange(64)[:, None] < np.arange(64)[None, :])
    msk_s = np.tile(m64, (1, 8))
    c["c_msk"] = np.concatenate([tri01, negtri, ones, zeros, msk_s], 1).astype(np.float32)
    return c


_NC = None
OUT_NAMES = ["y_prompt", "y_sample", "mla_ckv_p", "mla_krope_p", "mla_ckv_s", "mla_krope_s",
             "sb_k_p", "sb_v_p", "sb_k_s", "sb_v_s", "swa_k_p", "swa_v_p", "swa_k_s", "swa_v_s",
             "ffn_conv_p", "ffn_conv_s"]
BATCH_AXIS = {"x_prompt": 0, "x_sample": 0, "p_prompt": 1, "p_sample": 1, "cache_mla_ckv": 1, "cache_mla_krope": 1,
              "cache_sb_k": 1, "cache_sb_v": 1, "cache_swa_k": 1, "cache_swa_v": 1, "state_ffn_conv": 1}


def kernel(**inputs):
    global _NC
    if _NC is None:
        _NC = build_program()
    nc = _NC
    consts = _consts()
    in_maps = []
    for c in range(NCORES):
        m = dict(consts)
        for name, arr in inputs.items():
            a = np.asarray(arr, dtype=np.float32)
            if name in BATCH_AXIS:
                ax = BATCH_AXIS[name]
                sl = [slice(None)] * a.ndim
                sl[ax] = slice(4 * c, 4 * c + 4)
                a = np.ascontiguousarray(a[tuple(sl)])
            m[name] = a
        in_maps.append(m)
    res = run_bass_kernel_spmd(nc, in_maps, core_ids=list(range(NCORES)))
    outs = []
    for name in OUT_NAMES:
        ax = 0 if name in ("y_prompt", "y_sample") else 1
        outs.append(np.concatenate([np.asarray(r[name]) for r in res.results], axis=ax).astype(np.float32))
    return tuple(outs)
```

```python
from contextlib import ExitStack
import numpy as np
import concourse.bass as bass
import concourse.mybir as mybir
from concourse.bass_utils import run_bass_kernel_spmd

F32 = mybir.dt.float32
BF16 = mybir.dt.bfloat16
AF = mybir.ActivationFunctionType
ALU = mybir.AluOpType

NCORES = 8
NPS = 4
NSS = 4
SEQ = 2048
TS = 64
D = 1024
DEPTH = 4
DFF = 2816
NJ = 22
EPS = 1e-6
MLA_SCALE = 192 ** -0.5
SB_SCALE = 0.125
SWA_SCALE = 0.125

SAME_ENG_SYNC = True
NDSEM = 24
ENGS = ["pe", "act", "dve", "pool", "sp"]
GRAN = 256


class Buf:
    __slots__ = ("ap", "lw", "rd", "rdd")

    def __init__(self, ap):
        self.ap = ap
        self.lw = None
        self.rd = {}
        self.rdd = []

    def __getitem__(self, k):
        return V([self], self.ap[k])

    @property
    def v(self):
        return V([self], self.ap)


class V:
    __slots__ = ("bufs", "ap")

    def __init__(self, bufs, ap):
        self.bufs = bufs
        self.ap = ap

    def __getitem__(self, k):
        return V(self.bufs, self.ap[k])

    def rearrange(self, pattern_, **kw):
        return V(self.bufs, self.ap.rearrange(pattern_, **kw))

    def bitcast(self, dt):
        return V(self.bufs, self.ap.bitcast(dt))


class Ins:
    __slots__ = ("fn", "deps", "dma", "sig", "sigval", "slot", "dval", "prevd")

    def __init__(self, fn, deps, dma):
        self.fn = fn
        self.deps = deps
        self.dma = dma
        self.sig = False
        self.sigval = 0
        self.slot = -1
        self.dval = 0
        self.prevd = 0


CFG = {"sample": True, "nprompt": NPS, "depth": DEPTH, "stop": None, "sb_dummy": 4}


class Prog:
    def __init__(self, nc):
        self.nc = nc
        self.ins = {e: [] for e in ENGS}
        self.stopped = False

    def mark(self, label):
        if CFG["stop"] == label:
            self.stopped = True

    def emit(self, eng, fn, reads=(), writes=(), dma=False):
        if self.stopped:
            return None
        lst = self.ins[eng]
        idx = len(lst)
        node = (eng, idx)
        deps = set()
        for b in reads:
            if b.lw is not None:
                deps.add(b.lw)
        for b in writes:
            if b.lw is not None:
                deps.add(b.lw)
            for e, i in b.rd.items():
                deps.add((e, i))
            for n in b.rdd:
                deps.add(n)
        for b in reads:
            if dma:
                b.rdd.append(node)
            elif b.rd.get(eng, -1) < idx:
                b.rd[eng] = idx
        for b in writes:
            b.lw = node
            b.rd = {}
            b.rdd = []
        deps.discard(node)
        lst.append(Ins(fn, deps, dma))
        return node

    def op(self, eng, method, **kw):
        reads, writes, real = [], [], {}
        for k, v in kw.items():
            if isinstance(v, V):
                (writes if k in ("out", "accum_out") else reads).extend(v.bufs)
                real[k] = v.ap
            else:
                real[k] = v
        return self.emit(eng, lambda e: getattr(e, method)(**real), reads, writes)

    def mm(self, out, lhsT, rhs, start=True, stop=True, **kw):
        o, l, r = out.ap, lhsT.ap, rhs.ap
        return self.emit("pe", lambda e: e.matmul(o, lhsT=l, rhs=r, start=start, stop=stop, **kw),
                         lhsT.bufs + rhs.bufs, out.bufs)

    def tr(self, out, in_, ident):
        o, i, d = out.ap, in_.ap, ident.ap
        return self.emit("pe", lambda e: e.transpose(o, i, d), in_.bufs + ident.bufs, out.bufs)

    def dma(self, q, out, in_, nc_ok=False):
        reads, writes = [], []
        o, i = out, in_
        if isinstance(out, V):
            writes = out.bufs
            o = out.ap
        if isinstance(in_, V):
            reads = in_.bufs
            i = in_.ap
        if nc_ok:
            fn = lambda e: e.dma_start(out=o, in_=i, allow_slow_non_contiguous=True)
        else:
            fn = lambda e: e.dma_start(out=o, in_=i)
        return self.emit(q, fn, reads, writes, dma=True)

    def finalize(self, block, sems, dsems):
        ins = self.ins
        for e in ENGS:
            for x in ins[e]:
                for (f, j) in x.deps:
                    y = ins[f][j]
                    if y.dma:
                        continue
                    if f == e and (e == "pe" or not SAME_ENG_SYNC):
                        continue
                    y.sig = True
        final_dma = {}
        for e in ENGS:
            c = 0
            nd = 0
            last = [0] * NDSEM
            for x in ins[e]:
                if x.dma:
                    x.slot = nd % NDSEM
                    x.prevd = last[x.slot]
                    last[x.slot] += 16
                    x.dval = last[x.slot]
                    nd += 1
                elif x.sig:
                    c += 1
                    x.sigval = c
            final_dma[e] = last

        def run(e, eng):
            waited = {}
            for x in ins[e]:
                need = {}
                for (f, j) in x.deps:
                    y = ins[f][j]
                    if y.dma:
                        key = ("d", f, y.slot)
                        val = y.dval
                    else:
                        if f == e and (e == "pe" or not SAME_ENG_SYNC):
                            continue
                        key = ("c", f)
                        val = y.sigval
                    if need.get(key, 0) < val:
                        need[key] = val
                if x.dma and x.prevd > 0:
                    key = ("d", e, x.slot)
                    if need.get(key, 0) < x.prevd:
                        need[key] = x.prevd
                for key, val in need.items():
                    if waited.get(key, 0) >= val:
                        continue
                    waited[key] = val
                    sem = sems[key[1]] if key[0] == "c" else dsems[key[1]][key[2]]
                    eng.wait_ge(sem, val)
                r = x.fn(eng)
                if x.dma:
                    r.then_inc(dsems[e][x.slot], 16)
                elif x.sig:
                    r.then_inc(sems[e], 1)
            if e == "sp":
                for q in ENGS:
                    for s, val in enumerate(final_dma[q]):
                        if val > 0 and waited.get(("d", q, s), 0) < val:
                            eng.wait_ge(dsems[q][s], val)

        @block.tensor
        def _(eng):
            run("pe", eng)

        @block.scalar
        def _(eng):
            run("act", eng)

        @block.vector
        def _(eng):
            run("dve", eng)

        @block.gpsimd
        def _(eng):
            run("pool", eng)

        @block.sync
        def _(eng):
            run("sp", eng)


class Arena:
    def __init__(self, ap, nwords):
        self.ap = ap
        self.n = nwords // GRAN
        self.bufs = [Buf(ap[:, i * GRAN:(i + 1) * GRAN]) for i in range(self.n)]
        self.off = 0

    def reset(self):
        self.off = 0

    def alloc(self, shape, dt):
        free = int(np.prod(shape[1:]))
        words = free if dt == F32 else (free + 1) // 2
        g = (words + GRAN - 1) // GRAN
        g0 = self.off
        assert g0 + g <= self.n, ("arena overflow", g0, g, self.n)
        self.off += g
        ap = self.ap[0:shape[0], g0 * GRAN:g0 * GRAN + words]
        if dt != F32:
            ap = ap.bitcast(dt)[:, 0:free]
        if len(shape) > 2:
            names = "abcdefg"[:len(shape) - 1]
            pat = "p (" + " ".join(names) + ") -> p " + " ".join(names)
            ap = ap.rearrange(pat, **{n: shape[i + 1] for i, n in enumerate(names[:-1])})
        return V(self.bufs[g0:g0 + g], ap)


def build_program():
    nc = bass.Bass("TRN2", target_bir_lowering=False)
    dr = {}

    def din(name, shape):
        dr[name] = nc.dram_tensor(name, list(shape), F32, kind="ExternalInput").ap()

    def dout(name, shape):
        dr[name] = nc.dram_tensor(name, list(shape), F32, kind="ExternalOutput").ap()

    din("x_prompt", (NPS, SEQ, D)); din("x_sample", (NSS, TS, D))
    din("p_prompt", (DEPTH, NPS, SEQ, 256)); din("p_sample", (DEPTH, NSS, TS, 256))
    din("cache_mla_ckv", (2, NSS, 2048, 256)); din("cache_mla_krope", (2, NSS, 2048, 64))
    din("cache_sb_k", (1, NSS, 2048, 16, 64)); din("cache_sb_v", (1, NSS, 2048, 16, 64))
    din("cache_swa_k", (1, NSS, 128, 4, 64)); din("cache_swa_v", (1, NSS, 128, 4, 64))
    din("state_ffn_conv", (DEPTH, NSS, 2, DFF))
    din("g_mix", (DEPTH, D)); din("g_ffn", (DEPTH, D)); din("g_ple", (DEPTH, D)); din("g_final", (D,))
    din("w_mla_dq", (2, D, 384)); din("g_mla_q", (2, 384)); din("w_mla_uq", (2, 384, 3072))
    din("w_mla_dkv", (2, D, 320)); din("g_mla_kv", (2, 256))
    din("w_mla_uk", (2, 256, 16, 128)); din("w_mla_uv", (2, 256, 16, 128)); din("w_mla_o", (2, 2048, D))
    din("w_sb_qkv", (1, D, 3072)); din("w_sb_o", (1, D, D))
    din("w_swa_qkv", (1, D, 1536)); din("b_swa_qkv", (1, 1536)); din("swa_sinks", (1, 16)); din("w_swa_o", (1, D, D))
    din("w_ffn_in", (DEPTH, D, 2 * DFF)); din("ffn_conv_w", (DEPTH, 3, DFF)); din("ffn_conv_b", (DEPTH, DFF))
    din("w_ffn_out", (DEPTH, DFF, D)); din("w_ple_gate", (DEPTH, D, D)); din("w_ple_proj", (DEPTH, 256, D))
    din("c_ident", (128, 128)); din("c_cost", (17 * 128, 32)); din("c_sint", (17 * 128, 32))
    din("c_cos2", (128, 2560)); din("c_sin2", (128, 2560)); din("c_msk", (128, 1408))
    dout("y_prompt", (NPS, SEQ, D)); dout("y_sample", (NSS, TS, D))
    dout("mla_ckv_p", (2, NPS, SEQ, 256)); dout("mla_krope_p", (2, NPS, SEQ, 64))
    dout("mla_ckv_s", (2, NSS, TS, 256)); dout("mla_krope_s", (2, NSS, TS, 64))
    dout("sb_k_p", (1, NPS, SEQ, 16, 64)); dout("sb_v_p", (1, NPS, SEQ, 16, 64))
    dout("sb_k_s", (1, NSS, TS, 16, 64)); dout("sb_v_s", (1, NSS, TS, 16, 64))
    dout("swa_k_p", (1, NPS, 128, 4, 64)); dout("swa_v_p", (1, NPS, 128, 4, 64))
    dout("swa_k_s", (1, NSS, 128, 4, 64)); dout("swa_v_s", (1, NSS, 128, 4, 64))
    dout("ffn_conv_p", (DEPTH, NPS, 2, DFF)); dout("ffn_conv_s", (DEPTH, NSS, 2, DFF))

    es = ExitStack()
    with es:
        def sbt(name, shape, dt):
            return es.enter_context(nc.sbuf_tensor(name, shape, dt))

        P = Prog(nc)
        xt = sbt("xres", [128, 16, D], F32)
        X = [Buf(xt[:, i, :]) for i in range(16)]
        ident_f = Buf(sbt("identf", [128, 128], F32)[:])
        ident = Buf(sbt("identb", [128, 128], BF16)[:])
        cost = Buf(sbt("cost", [128, 17, 32], F32)[:])
        sint = Buf(sbt("sint", [128, 17, 32], F32)[:])
        msk = Buf(sbt("msk", [128, 1408], BF16)[:])
        gb = Buf(sbt("gb", [128, D], F32)[:])
        ssb = Buf(sbt("ssb", [128, 16], F32)[:])
        rstd = Buf(sbt("rstd", [128, 16], F32)[:])
        sm1 = Buf(sbt("sm1", [128, 8], F32)[:])
        own_ckvT = Buf(sbt("ownck", [128, 2, 256], BF16)[:]).v
        own_krT = Buf(sbt("ownkr", [64, 256], BF16)[:]).v
        AW = 130 * GRAN
        arena = Arena(sbt("arena", [128, AW], F32)[:], AW)
        PS = [Buf(es.enter_context(nc.psum_tensor(f"ps{i}", [128, 512], F32))[:]) for i in range(8)]
        sems = {e: es.enter_context(nc.semaphore("s_" + e)) for e in ENGS}
        dsems = {e: [es.enter_context(nc.semaphore(f"d_{e}{i}")) for i in range(NDSEM)] for e in ["sp", "pool", "act"]}
        block = es.enter_context(nc.Block())

        tri01 = msk[:, 0:128]
        negtri = msk[:, 128:256]
        ones = msk[:, 256:384]
        zeros = msk[:, 384:896]
        msk_s = msk[:, 896:1408]

        P.dma("sp", ident_f.v, dr["c_ident"])
        P.op("dve", "tensor_copy", out=ident.v, in_=ident_f.v)
        P.dma("sp", cost.v, dr["c_cost"].rearrange("(t p) f -> p t f", p=128))
        P.dma("sp", sint.v, dr["c_sint"].rearrange("(t p) f -> p t f", p=128))
        P.dma("pool", msk.v, dr["c_msk"])

        def wview(w2d):
            return w2d.rearrange("(c p) n -> p c n", p=128)

        def psb(i):
            return PS[i].v.bitcast(BF16)

        def rstd_from_ms(ntl, col0):
            P.op("act", "activation", out=rstd[:, col0:col0 + ntl], in_=ssb[:, col0:col0 + ntl], func=AF.Ln, bias=EPS, scale=1.0)
            P.op("act", "activation", out=rstd[:, col0:col0 + ntl], in_=rstd[:, col0:col0 + ntl], func=AF.Exp, scale=-0.5)

        def norm_T(tiles, g_ap, hT, junk, hb2):
            P.dma("sp", gb.v, g_ap.partition_broadcast(128))
            n = len(tiles)
            for k, t in enumerate(tiles):
                P.op("act", "activation", out=hb2[k % 2], in_=X[t].v, func=AF.Square, scale=1.0 / 32.0, accum_out=ssb[:, k:k + 1])
            rstd_from_ms(n, 0)
            for k, t in enumerate(tiles):
                hb = hb2[k % 2]
                P.op("dve", "scalar_tensor_tensor", out=hb, in0=X[t].v, scalar=rstd[:, k:k + 1], in1=gb.v, op0=ALU.mult, op1=ALU.mult)
                bank = 6 + (k % 2)
                pv = psb(bank).rearrange("p (c t) -> p c t", c=8)
                for c in range(8):
                    P.tr(pv[:, c, :], hb[:, c * 128:(c + 1) * 128], ident.v)
                P.op("act", "copy", out=hT[:, :, k * 128:(k + 1) * 128], in_=pv)

        def add_to_x(t, dh, ps):
            P.op("dve", "tensor_tensor", out=X[t][:, dh * 512:(dh + 1) * 512], in0=X[t][:, dh * 512:(dh + 1) * 512], in1=ps, op=ALU.add)

        def ple(layer, tiles, p_src):
            arena.reset()
            nt = len(tiles)
            TH = min(nt, 8)
            hT = arena.alloc([128, 8, TH * 128], BF16)
            junk = arena.alloc([128, D], BF16)
            hb2 = [arena.alloc([128, D], BF16) for _ in range(2)]
            wg = arena.alloc([128, 8, D], BF16)
            wp = arena.alloc([128, 2, D], BF16)
            pb2 = [arena.alloc([128, 256], BF16) for _ in range(2)]
            pT2 = [arena.alloc([128, 2, 128], BF16) for _ in range(2)]
            sg2 = [arena.alloc([128, 512], F32) for _ in range(2)]
            tt2 = [arena.alloc([128, 512], F32) for _ in range(2)]
            P.dma("pool", wg, wview(dr["w_ple_gate"][layer]))
            P.dma("pool", wp, wview(dr["w_ple_proj"][layer]))
            for h0 in range(0, nt, TH):
                sub = tiles[h0:h0 + TH]
                norm_T(sub, dr["g_ple"][layer], hT, junk, hb2)
                for k, t in enumerate(sub):
                    pb = pb2[k % 2]
                    pT = pT2[k % 2]
                    P.dma("pool", pb, p_src(t))
                    pv = psb(5).rearrange("p (c t) -> p c t", c=8)
                    for c in range(2):
                        P.tr(pv[:, c, :], pb[:, c * 128:(c + 1) * 128], ident.v)
                    P.op("act", "copy", out=pT, in_=pv[:, 0:2, :])
                    for dh in range(2):
                        pa = PS[dh].v
                        pbk = PS[2 + dh].v
                        for c in range(8):
                            P.mm(pa, hT[:, c, k * 128:(k + 1) * 128], wg[:, c, dh * 512:(dh + 1) * 512], start=(c == 0), stop=(c == 7))
                        for c in range(2):
                            P.mm(pbk, pT[:, c, :], wp[:, c, dh * 512:(dh + 1) * 512], start=(c == 0), stop=(c == 1))
                        sg = sg2[dh]
                        tt = tt2[dh]
                        P.op("act", "activation", out=sg, in_=pa, func=AF.Sigmoid)
                        P.op("dve", "tensor_tensor", out=tt, in0=sg, in1=pbk, op=ALU.mult)
                        P.op("dve", "tensor_tensor", out=X[t][:, dh * 512:(dh + 1) * 512], in0=X[t][:, dh * 512:(dh + 1) * 512], in1=tt, op=ALU.add)

        def ffn(layer, tiles, nseg, L, state_src, conv_dst):
            arena.reset()
            nt = len(tiles)
            TH = min(nt, 8)
            NB = nseg * L
            hT = arena.alloc([128, 8, TH * 128], BF16)
            junk = None
            hb2 = [arena.alloc([128, D], BF16) for _ in range(2)]
            actT = arena.alloc([128, NJ, TH * 128], BF16)
            wi2 = [arena.alloc([128, 8, 512], BF16) for _ in range(2)]
            wo2 = [arena.alloc([128, NJ, 256], BF16) for _ in range(2)]
            G2 = [arena.alloc([128, nseg, L + 2], F32) for _ in range(2)]
            t12 = [arena.alloc([128, nseg, L], F32) for _ in range(2)]
            ge2 = [arena.alloc([128, nseg, L], F32) for _ in range(2)]
            carry = arena.alloc([128, NJ, nseg, 2], F32)
            cw = arena.alloc([128, NJ, 3], F32)
            cb = arena.alloc([128, NJ], F32)
            for r in range(3):
                P.dma("sp", cw[:, :, r], dr["ffn_conv_w"][layer, r].rearrange("(j p) -> p j", p=128), nc_ok=True)
            P.dma("sp", cb, dr["ffn_conv_b"][layer].rearrange("(j p) -> p j", p=128), nc_ok=True)
            if state_src is None:
                P.op("dve", "memset", out=carry, constant=0.0) if False else P.emit("dve", lambda e, a=carry.ap: e.memset(a, 0.0), [], carry.bufs)
            else:
                for sg in range(nseg):
                    for r in range(2):
                        P.dma("sp", carry[:, :, sg, r], state_src[sg, r].rearrange("(j p) -> p j", p=128), nc_ok=True)
            win = wview(dr["w_ffn_in"][layer])
            wov = dr["w_ffn_out"][layer].rearrange("(j p) n -> p j n", p=128)
            uc = 0
            for h0 in range(0, nt, TH):
                sub = tiles[h0:h0 + TH]
                ntok = len(sub) * 128
                norm_T(sub, dr["g_ffn"][layer], hT, junk, hb2)
                nblk = ntok // NB
                work = []
                for grp in range(NJ // 2):
                    for u in range(2):
                        for blk in range(nblk):
                            work.append((grp, u, blk))

                def stage_a(idx, grp, u, blk):
                    wi = wi2[grp % 2]
                    if u == 0 and blk == 0:
                        P.dma("pool", wi[:, :, 0:256], win[:, :, grp * 256:(grp + 1) * 256])
                        P.dma("pool", wi[:, :, 256:512], win[:, :, DFF + grp * 256:DFF + (grp + 1) * 256])
                    j = grp * 2 + u
                    cs = slice(blk * NB, (blk + 1) * NB)
                    pg = PS[idx % 2].v[:, 0:NB]
                    pu = PS[2 + idx % 2].v[:, 0:NB]
                    G = G2[idx % 2]
                    t1 = t12[idx % 2]
                    for c in range(8):
                        P.mm(pg, wi[:, c, u * 128:(u + 1) * 128], hT[:, c, cs], start=(c == 0), stop=(c == 7))
                    for c in range(8):
                        P.mm(pu, wi[:, c, 256 + u * 128:256 + (u + 1) * 128], hT[:, c, cs], start=(c == 0), stop=(c == 7))
                    P.op("act", "copy", out=G[:, :, 0:2], in_=carry[:, j])
                    P.op("act", "copy", out=G[:, :, 2:L + 2], in_=pg.rearrange("p (s l) -> p s l", s=nseg))
                    P.op("dve", "tensor_copy", out=carry[:, j], in_=G[:, :, L:L + 2])
                    P.op("act", "activation", out=t1, in_=G[:, :, 2:L + 2], func=AF.Identity, scale=cw[:, j, 2:3], bias=cb[:, j:j + 1])
                    P.op("dve", "scalar_tensor_tensor", out=t1, in0=G[:, :, 1:L + 1], scalar=cw[:, j, 1:2], in1=t1, op0=ALU.mult, op1=ALU.add)
                    P.op("dve", "scalar_tensor_tensor", out=t1, in0=G[:, :, 0:L], scalar=cw[:, j, 0:1], in1=t1, op0=ALU.mult, op1=ALU.add)

                def stage_b(idx, grp, u, blk):
                    j = grp * 2 + u
                    cs = slice(blk * NB, (blk + 1) * NB)
                    pu = PS[2 + idx % 2].v[:, 0:NB]
                    t1 = t12[idx % 2]
                    ge = ge2[idx % 2]
                    P.op("act", "activation", out=ge, in_=t1, func=AF.Gelu)
                    P.op("dve", "tensor_tensor", out=actT[:, j, cs].rearrange("p (s l) -> p s l", s=nseg), in0=ge, in1=pu.rearrange("p (s l) -> p s l", s=nseg), op=ALU.mult)

                for i, wk_ in enumerate(work):
                    stage_a(uc + i, *wk_)
                    if i > 0:
                        stage_b(uc + i - 1, *work[i - 1])
                stage_b(uc + len(work) - 1, *work[-1])
                uc += len(work)
                for dq in range(4):
                    wo = wo2[dq % 2]
                    P.dma("pool", wo, wov[:, :, dq * 256:(dq + 1) * 256])
                    for k, t in enumerate(sub):
                        po = PS[4 + (dq * 8 + k) % 2].v[:, 0:256]
                        for j in range(NJ):
                            P.mm(po, actT[:, j, k * 128:(k + 1) * 128], wo[:, j, :], start=(j == 0), stop=(j == NJ - 1))
                        xs = X[t][:, dq * 256:(dq + 1) * 256]
                        P.op("dve", "tensor_tensor", out=xs, in0=xs, in1=po, op=ALU.add)
            for sg in range(nseg):
                for r in range(2):
                    P.dma("sp", conv_dst[sg, r].rearrange("(j p) -> p j", p=128), carry[:, :, sg, r], nc_ok=True)

        def softmax_group(chunks, N, scale, ndv, E3, rec, out_fn, sink=None, zb=0):
            den = PS[2].v[:, 0:N]
            O = [PS[3 + d].v[:, 0:N] for d in range(ndv)]
            zr = zeros[:, 0:N]

            def prezero():
                P.mm(den, zeros[:, 0:128], zr, start=True, stop=False)
                for d in range(ndv):
                    P.mm(O[d], zeros[:, 0:128], zr, start=True, stop=False)
            nch = len(chunks)

            def stage_z(ci):
                ch = chunks[ci]
                nk, c0, c1 = ch["nk"], ch["c0"], ch["c1"]
                z = PS[(zb + ci) % 2].v[0:nk, c0:c1]
                E = E3[ci % 3][0:nk, c0:c1]
                kl = ch["kl"]
                for i, (l, r) in enumerate(kl):
                    P.mm(z, l, r, start=(i == 0), stop=(i == len(kl) - 1))
                P.op("act", "activation", out=E, in_=z, func=AF.Exp, scale=scale)
                for (p0, p1, a, b) in ch["fix"]:
                    P.emit("dve", lambda e, ap=E3[ci % 3][p0:p1, a:b].ap: e.memset(ap, 0.0), [], E.bufs)

            def stage_v(ci):
                ch = chunks[ci]
                nk, c0, c1 = ch["nk"], ch["c0"], ch["c1"]
                E = E3[ci % 3][0:nk, c0:c1]
                last = (ci == nch - 1)
                P.mm(den[:, c0:c1], ones[0:nk, :], E, start=False, stop=last, skip_group_check=True)
                for d in range(ndv):
                    P.mm(O[d][:, c0:c1], ch["vl"][d], E, start=False, stop=last, skip_group_check=True)

            nz0 = min(2, nch)
            for ci in range(nz0):
                stage_z(ci)
            prezero()
            for ci in range(1, nch):
                if ci >= nz0:
                    stage_z(ci)
                stage_v(ci - 1)
            stage_v(nch - 1)
            if sink is not None:
                P.op("act", "activation", out=rec[:, 0:N], in_=den, func=AF.Ln, bias=sink, scale=1.0)
            else:
                P.op("act", "activation", out=rec[:, 0:N], in_=den, func=AF.Ln)
            P.op("act", "activation", out=rec[:, 0:N], in_=rec[:, 0:N], func=AF.Exp, scale=-1.0)
            out_fn(O, rec[:, 0:N])

        def mla(layer, tiles, sample):
            jl = layer // 3
            arena.reset()
            nt = len(tiles)
            T = nt * 128
            NK = 2112 if sample else 2048
            TW = 512 if sample else 2048
            tc0 = 2048 if sample else 0
            ckvT = arena.alloc([128, 2, NK], BF16)
            krT = arena.alloc([64, NK], BF16)
            ckvt = arena.alloc([128, 17 if sample else 16, 256], BF16)
            cqT = arena.alloc([128, 3, T], BF16)
            cos2 = arena.alloc([64, TW], BF16)
            sin2 = arena.alloc([64, TW], BF16)
            wukT = arena.alloc([128, 16, 256], BF16)
            wuv = arena.alloc([128, 2, 2048], BF16)
            mark = arena.off
            wo = arena.alloc([128, 16, D], BF16)
            OVT = arena.alloc([128, 16, 512 if not sample else T], BF16)
            wq4 = [arena.alloc([128, 3, 256], BF16) for _ in range(3)]
            E3 = [arena.alloc([128, 512], BF16) for _ in range(3)]
            rec = arena.alloc([128, 512], F32)
            qn2 = [arena.alloc([128, 512], BF16) for _ in range(2)]
            qr2 = [arena.alloc([64, 512], BF16) for _ in range(2)]
            qlat2 = [arena.alloc([128, 2, 512], BF16) for _ in range(2)]
            olat = arena.alloc([128, 2, 512], BF16)
            tq2 = [arena.alloc([64, 512], F32) for _ in range(2)]
            krc = arena.alloc([128, 16, 64], BF16) if sample else None
            arena.off = mark
            TH = min(nt, 8)
            hT = arena.alloc([128, 8, TH * 128], BF16)
            hb2 = [arena.alloc([128, D], BF16) for _ in range(2)]
            wdq = arena.alloc([128, 8, 384], BF16)
            wdkv = arena.alloc([128, 8, 320], BF16)
            gq = arena.alloc([128, 384], F32)
            gkv = arena.alloc([128, 256], F32)
            wuk = arena.alloc([128, 2, 2048], BF16)
            ckvf2 = [arena.alloc([128, 256], F32) for _ in range(2)]
            krf2 = [arena.alloc([128, 64], F32) for _ in range(2)]
            krt = arena.alloc([128, 4, 32], F32)
            krb2 = [arena.alloc([128, 64], BF16) for _ in range(2)]
            cqb2 = [arena.alloc([128, 384], BF16) for _ in range(2)]
            ckb_s = arena.alloc([128, 256], BF16)
            sqj = arena.alloc([128, 640], BF16)
            P.dma("pool", cos2, dr["c_cos2"][0:64, tc0:tc0 + TW])
            P.dma("pool", sin2, dr["c_sin2"][0:64, tc0:tc0 + TW])
            P.dma("pool", wdq, wview(dr["w_mla_dq"][jl]))
            P.dma("pool", wdkv, wview(dr["w_mla_dkv"][jl]))
            P.dma("sp", gq, dr["g_mla_q"][jl].partition_broadcast(128))
            P.dma("sp", gkv, dr["g_mla_kv"][jl].partition_broadcast(128))
            P.dma("pool", wuk, wview(dr["w_mla_uk"][jl].rearrange("l h n -> l (h n)")))
            P.dma("pool", wuv, wview(dr["w_mla_uv"][jl].rearrange("l h n -> l (h n)")))
            for h in range(16):
                pv = psb(5).rearrange("p (c t) -> p c t", c=8)
                for lc in range(2):
                    P.tr(pv[:, lc, :], wuk[:, lc, h * 128:(h + 1) * 128], ident.v)
                P.op("act", "copy", out=wukT[:, h, :].rearrange("p (c t) -> p c t", c=2), in_=pv[:, 0:2, :])
            for h0 in range(0, nt, TH):
                sub = tiles[h0:h0 + TH]
                norm_T(sub, dr["g_mix"][layer], hT, None, hb2)
                for k, t in enumerate(sub):
                    gt = h0 + k
                    ptile = 16 if sample else gt
                    hs = [hT[:, c, k * 128:(k + 1) * 128] for c in range(8)]
                    pkv = PS[0].v[:, 0:320]
                    pcq = PS[1].v[:, 0:384]
                    for c in range(8):
                        P.mm(pkv, hs[c], wdkv[:, c, :], start=(c == 0), stop=(c == 7))
                    for c in range(8):
                        P.mm(pcq, hs[c], wdq[:, c, :], start=(c == 0), stop=(c == 7))
                    P.op("act", "activation", out=sqj[:, 0:256], in_=pkv[:, 0:256], func=AF.Square, scale=1.0 / 16.0, accum_out=ssb[:, 8:9])
                    P.op("act", "activation", out=sqj[:, 256:640], in_=pcq, func=AF.Square, scale=384 ** -0.5, accum_out=ssb[:, 9:10])
                    rstd_from_ms(2, 8)
                    ckvf = ckvf2[k % 2]
                    krf = krf2[k % 2]
                    krb = krb2[k % 2]
                    cqb = cqb2[k % 2]
                    P.op("dve", "scalar_tensor_tensor", out=ckvf, in0=pkv[:, 0:256], scalar=rstd[:, 8:9], in1=gkv, op0=ALU.mult, op1=ALU.mult)
                    P.op("dve", "scalar_tensor_tensor", out=cqb, in0=pcq, scalar=rstd[:, 9:10], in1=gq, op0=ALU.mult, op1=ALU.mult)
                    x1 = pkv[:, 256:288]
                    x2 = pkv[:, 288:320]
                    cs_, sn_ = cost[:, ptile, :], sint[:, ptile, :]
                    P.op("dve", "tensor_tensor", out=krt[:, 0, :], in0=x1, in1=cs_, op=ALU.mult)
                    P.op("dve", "tensor_tensor", out=krt[:, 1, :], in0=x2, in1=sn_, op=ALU.mult)
                    P.op("dve", "tensor_tensor", out=krt[:, 2, :], in0=x2, in1=cs_, op=ALU.mult)
                    P.op("dve", "tensor_tensor", out=krt[:, 3, :], in0=x1, in1=sn_, op=ALU.mult)
                    P.op("dve", "tensor_tensor", out=krf[:, 0:32], in0=krt[:, 0, :], in1=krt[:, 1, :], op=ALU.subtract)
                    P.op("dve", "tensor_tensor", out=krf[:, 32:64], in0=krt[:, 2, :], in1=krt[:, 3, :], op=ALU.add)
                    if not sample:
                        si = tiles_seq
                        P.dma("sp", dr["mla_ckv_p"][jl, si, gt * 128:(gt + 1) * 128, :], ckvf)
                        P.dma("sp", dr["mla_krope_p"][jl, si, gt * 128:(gt + 1) * 128, :], krf)
                        ckb = ckvt[:, gt, :]
                    else:
                        P.dma("sp", dr["mla_ckv_s"][jl, 2 * gt:2 * gt + 2].rearrange("s t l -> (s t) l"), ckvf)
                        P.dma("sp", dr["mla_krope_s"][jl, 2 * gt:2 * gt + 2].rearrange("s t l -> (s t) l"), krf)
                        ckb = ckb_s
                    P.op("act", "copy", out=ckb, in_=ckvf)
                    P.op("act", "copy", out=krb, in_=krf)
                    pv = psb(5).rearrange("p (c t) -> p c t", c=8)
                    P.tr(pv[:, 0, :], ckb[:, 0:128], ident.v)
                    P.tr(pv[:, 1, :], ckb[:, 128:256], ident.v)
                    P.tr(pv[0:64, 2, :], krb, ident.v)
                    for c in range(3):
                        P.tr(pv[:, 3 + c, :], cqb[:, c * 128:(c + 1) * 128], ident.v)
                    if not sample:
                        kc0 = gt * 128
                        P.op("act", "copy", out=ckvT[:, :, kc0:kc0 + 128], in_=pv[:, 0:2, :])
                        P.op("act", "copy", out=krT[:, kc0:kc0 + 128], in_=pv[0:64, 2, :])
                    else:
                        P.op("act", "copy", out=own_ckvT[:, :, gt * 128:(gt + 1) * 128], in_=pv[:, 0:2, :])
                        P.op("act", "copy", out=own_krT[:, gt * 128:(gt + 1) * 128], in_=pv[0:64, 2, :])
                    P.op("act", "copy", out=cqT[:, :, gt * 128:(gt + 1) * 128], in_=pv[:, 3:6, :])
            P.mark("mla1")
            P.dma("pool", wo, wview(dr["w_mla_o"][jl]))
            uqv = dr["w_mla_uq"][jl].rearrange("(c p) n -> p c n", p=128)
            wqi = [0]

            def load_wq(h):
                wq = wq4[wqi[0] % 3]
                wqi[0] += 1
                P.dma("pool", wq[:, :, 0:192], uqv[:, :, h * 192:(h + 1) * 192])
                P.dma("pool", wq[:, :, 192:224], uqv[:, :, h * 192 + 160:h * 192 + 192])
                P.dma("pool", wq[:, :, 224:256], uqv[:, :, h * 192 + 128:h * 192 + 160])
                return wq

            def qproj(h, qcols, ocols):
                wq = load_wq(h)
                pq = PS[5].v[:, ocols]
                pr = PS[6].v[0:64, ocols]
                pt = PS[7].v[0:64, ocols]
                for c in range(3):
                    P.mm(pq, wq[:, c, 0:128], cqT[:, c, qcols], start=(c == 0), stop=(c == 2))
                for c in range(3):
                    P.mm(pr, wq[:, c, 128:192], cqT[:, c, qcols], start=(c == 0), stop=(c == 2))
                for c in range(3):
                    P.mm(pt, wq[:, c, 192:256], cqT[:, c, qcols], start=(c == 0), stop=(c == 2))

            def qfinish(N, tcols, b=0):
                qn, qr = qn2[b], qr2[b]
                P.op("act", "copy", out=qn[:, 0:N], in_=PS[5].v[:, 0:N])
                P.op("dve", "tensor_tensor", out=tq2[0][:, 0:N], in0=PS[6].v[0:64, 0:N], in1=cos2[:, tcols], op=ALU.mult)
                P.op("dve", "tensor_tensor", out=tq2[1][:, 0:N], in0=PS[7].v[0:64, 0:N], in1=sin2[:, tcols], op=ALU.mult)
                P.op("dve", "tensor_tensor", out=qr[:, 0:N], in0=tq2[0][:, 0:N], in1=tq2[1][:, 0:N], op=ALU.add)

            def qlat_from_qn(hlist, N, b=0):
                qn, qlat = qn2[b], qlat2[b]
                w = N // len(hlist)
                for lc in range(2):
                    pl = PS[5 + lc].v
                    for i, h in enumerate(hlist):
                        P.mm(pl[:, i * w:(i + 1) * w], wukT[:, h, lc * 128:(lc + 1) * 128], qn[:, i * w:(i + 1) * w], start=True, stop=True)
                    P.op("act", "copy", out=qlat[:, lc, 0:N], in_=pl[:, 0:N])

            if not sample:
                items = [(qb, h) for qb in range(T // 512) for h in range(16)]

                def prep(i):
                    qb, h = items[i]
                    qc = slice(qb * 512, (qb + 1) * 512)
                    qproj(h, qc, slice(0, 512))
                    qfinish(512, qc, i % 2)
                    qlat_from_qn([h], 512, i % 2)

                def attend(i):
                    qb, h = items[i]
                    qlat, qr = qlat2[i % 2], qr2[i % 2]
                    chunks = []
                    for kc in range(4 * qb + 4):
                        j = kc - 4 * qb
                        c0 = 128 * j if j >= 0 else 0
                        kcs = slice(kc * 128, (kc + 1) * 128)
                        chunks.append(dict(
                            kl=[(ckvT[:, 0, kcs], qlat[:, 0, c0:512]), (ckvT[:, 1, kcs], qlat[:, 1, c0:512]), (krT[:, kcs], qr[:, c0:512])],
                            nk=128, c0=c0, c1=512,
                            fix=[(64, 128, c0, c0 + 64)] if j >= 0 else [],
                            vl=[ckvt[:, kc, 0:128], ckvt[:, kc, 128:256]]))

                    def out_fn(O, r, h=h):
                        for lc in range(2):
                            P.op("dve", "tensor_tensor", out=olat[:, lc, :], in0=O[lc], in1=r, op=ALU.mult)
                        pv_ = PS[7].v
                        for lc in range(2):
                            P.mm(pv_, wuv[:, lc, h * 128:(h + 1) * 128], olat[:, lc, :], start=(lc == 0), stop=(lc == 1))
                        P.op("act", "copy", out=OVT[:, h, :], in_=pv_)
                    softmax_group(chunks, 512, MLA_SCALE, 2, E3, rec, out_fn, zb=h)

                prep(0)
                for i, (qb, h) in enumerate(items):
                    if i + 1 < len(items):
                        prep(i + 1)
                    attend(i)
                    if h == 15:
                        for k4 in range(4):
                            t = tiles[qb * 4 + k4]
                            for dh in range(2):
                                po = PS[5 + dh].v
                                for hh in range(16):
                                    P.mm(po, OVT[:, hh, k4 * 128:(k4 + 1) * 128], wo[:, hh, dh * 512:(dh + 1) * 512], start=(hh == 0), stop=(hh == 15))
                                add_to_x(t, dh, po)
            else:
                for s in range(NSS):
                    P.dma("pool", ckvt[:, 0:16, :], dr["cache_mla_ckv"][jl, s].rearrange("(t p) l -> p t l", p=128))
                    P.dma("pool", krc, dr["cache_mla_krope"][jl, s].rearrange("(t p) f -> p t f", p=128))
                    for kt in range(16):
                        pv = psb(5 + kt % 2).rearrange("p (c t) -> p c t", c=8)
                        P.tr(pv[:, 0, :], ckvt[:, kt, 0:128], ident.v)
                        P.tr(pv[:, 1, :], ckvt[:, kt, 128:256], ident.v)
                        P.tr(pv[0:64, 2, :], krc[:, kt, :], ident.v)
                        P.op("act", "copy", out=ckvT[:, :, kt * 128:(kt + 1) * 128], in_=pv[:, 0:2, :])
                        P.op("act", "copy", out=krT[:, kt * 128:(kt + 1) * 128], in_=pv[0:64, 2, :])
                    oc = slice(s * 64, (s + 1) * 64)
                    P.op("act", "copy", out=ckvT[:, :, 2048:2112], in_=own_ckvT[:, :, oc])
                    P.op("act", "copy", out=krT[:, 2048:2112], in_=own_krT[:, oc])
                    pv = psb(5).rearrange("p (c t) -> p c t", c=8)
                    for lc in range(2):
                        P.tr(pv[0:64, lc, :], ckvT[:, lc, 2048:2112], ident.v)
                    P.op("act", "copy", out=ckvt[0:64, 16, :].rearrange("p (c t) -> p c t", c=2), in_=pv[0:64, 0:2, :])
                    for g in range(2):
                        hl = list(range(8 * g, 8 * g + 8))
                        for i, h in enumerate(hl):
                            qproj(h, oc, slice(i * 64, (i + 1) * 64))
                        qfinish(512, slice(0, 512))
                        qlat_from_qn(hl, 512)
                        chunks = []
                        for kc in range(17):
                            nk = 128 if kc < 16 else 64
                            kcs = slice(kc * 128, kc * 128 + nk)
                            chunks.append(dict(
                                kl=[(ckvT[:, 0, kcs], qlat2[0][:, 0, :]), (ckvT[:, 1, kcs], qlat2[0][:, 1, :]), (krT[:, kcs], qr2[0][:, :])],
                                nk=nk, c0=0, c1=512, fix=[],
                                vl=[ckvt[0:nk, kc, 0:128], ckvt[0:nk, kc, 128:256]]))

                        def out_fn(O, r, hl=hl, s=s, g=g):
                            for lc in range(2):
                                P.op("dve", "tensor_tensor", out=olat[:, lc, :], in0=O[lc], in1=r, op=ALU.mult)
                            pv_ = PS[7].v
                            for i, h in enumerate(hl):
                                for lc in range(2):
                                    P.mm(pv_[:, i * 64:(i + 1) * 64], wuv[:, lc, h * 128:(h + 1) * 128], olat[:, lc, i * 64:(i + 1) * 64], start=(lc == 0), stop=(lc == 1))
                            P.op("act", "copy", out=OVT[:, 8 * g:8 * g + 8, s * 64:(s + 1) * 64], in_=pv_.rearrange("p (h q) -> p h q", h=8))
                        softmax_group(chunks, 512, MLA_SCALE, 2, E3, rec, out_fn, zb=g)
                for k, t in enumerate(tiles):
                    for dh in range(2):
                        po = PS[5 + dh].v
                        for h in range(16):
                            P.mm(po, OVT[:, h, k * 128:(k + 1) * 128], wo[:, h, dh * 512:(dh + 1) * 512], start=(h == 0), stop=(h == 15))
                        add_to_x(t, dh, po)

        tiles_seq = 0

        def swa(layer, tiles, sample):
            arena.reset()
            nt = len(tiles)
            T = nt * 128
            TW = 512 if sample else 2048
            tc0 = 2048 if sample else 0
            BW = 256 if sample else 512
            KX = 128 if sample else 0
            wo = arena.alloc([128, 8, D], BF16)
            cos2 = arena.alloc([128, TW], BF16)
            sin2 = arena.alloc([128, TW], BF16)
            bq = arena.alloc([128, 8], F32)
            bqr = arena.alloc([128, 8], F32)
            bk = arena.alloc([128, 4], F32)
            bkr = arena.alloc([128, 4], F32)
            bvs = arena.alloc([128, 256], F32)
            bv = arena.alloc([128, 512], F32)
            sk = arena.alloc([128, 16], F32)
            kT = arena.alloc([128, 4, T + KX], BF16)
            Vd = arena.alloc([128, nt + (1 if sample else 0), 512], BF16)
            qT = arena.alloc([128, 8, BW], BF16)
            OT = arena.alloc([128, 8, BW], BF16)
            hT = arena.alloc([128, 8, BW], BF16)
            hb2 = [arena.alloc([128, D], BF16) for _ in range(2)]
            E3 = [arena.alloc([128, 512], BF16) for _ in range(3)]
            rec = arena.alloc([128, 512], F32)
            ta = [arena.alloc([128, 512], F32) for _ in range(2)]
            kf = arena.alloc([128, 4, 128], F32)
            vf = arena.alloc([128, 256], F32)
            kto = arena.alloc([128, 2, 256], F32) if sample else arena.alloc([128, 1, 256], F32)
            vown = arena.alloc([64, 512], BF16) if sample else None
            knat_s = arena.alloc([128, 8, 512], BF16) if sample else None
            mark = arena.off
            wk = arena.alloc([128, 8, 4, 2, 64], BF16)
            wkr = arena.alloc([128, 8, 4, 4, 32], BF16)
            wv = arena.alloc([128, 8, 4, 2, 64], BF16)
            arena.off = mark
            wqh = arena.alloc([128, 8, 512], BF16)
            wqrh = arena.alloc([128, 8, 8, 2, 32], BF16)
            W = dr["w_swa_qkv"][0]
            wv_ = wview(W)
            w5 = W.rearrange("(c p) (h two f) -> p c h two f", p=128, two=2, f=32)
            w4 = W.rearrange("(c p) (h d) -> p c h d", p=128, d=64)
            P.dma("pool", wo, wview(dr["w_swa_o"][0]))
            P.dma("pool", cos2, dr["c_cos2"][:, tc0:tc0 + TW])
            P.dma("pool", sin2, dr["c_sin2"][:, tc0:tc0 + TW])
            B = dr["b_swa_qkv"][0]
            P.dma("sp", bq, B[0:1024].rearrange("(c p) -> p c", p=128), nc_ok=True)
            b3 = B.rearrange("(h two f) -> two f h", two=2, f=32)
            for hh in range(2):
                for two in range(2):
                    r0 = hh * 64 + two * 32
                    P.dma("sp", bqr[r0:r0 + 32, :], b3[1 - two, :, hh:16:2], nc_ok=True)
                    P.dma("sp", bk[r0:r0 + 32, :], b3[two, :, 16:20], nc_ok=True)
                    P.dma("sp", bkr[r0:r0 + 32, :], b3[1 - two, :, 16:20], nc_ok=True)
            P.dma("sp", bvs, B[1280:1536].partition_broadcast(128))
            bv4 = bv.rearrange("p (h d f) -> p h d f", h=4, d=2)
            for dup in range(2):
                P.op("dve", "tensor_copy", out=bv4[:, :, dup, :], in_=bvs.rearrange("p (h f) -> p h f", h=4))
            P.dma("sp", sk, dr["swa_sinks"][0].partition_broadcast(128))
            P.op("act", "activation", out=sk, in_=sk, func=AF.Exp)

            def load_kv_w():
                knat = knat_s if sample else qT
                P.dma("pool", knat, wv_[:, :, 1024:1536])
                kn4 = knat[:, :, 0:256].rearrange("p c (h d) -> p c h d", h=4)
                vn4 = knat[:, :, 256:512].rearrange("p c (h d) -> p c h d", h=4)
                kn5 = knat[:, :, 0:256].rearrange("p c (h two f) -> p c h two f", h=4, two=2)
                for dup in range(2):
                    P.op("dve", "tensor_copy", out=wk[:, :, :, dup, :], in_=kn4)
                    P.op("act", "copy", out=wv[:, :, :, dup, :], in_=vn4)
                    for two in range(2):
                        P.op("dve" if two == 0 else "act", "tensor_copy" if two == 0 else "copy", out=wkr[:, :, :, dup * 2 + two, :], in_=kn5[:, :, :, 1 - two, :])

            def load_q_w(half):
                P.dma("pool", wqh, wv_[:, :, half * 512:(half + 1) * 512])
                q5 = wqh.rearrange("p c (h two f) -> p c h two f", h=8, two=2)
                for two in range(2):
                    P.op("dve" if two == 0 else "act", "tensor_copy" if two == 0 else "copy", out=wqrh[:, :, :, two, :], in_=q5[:, :, :, 1 - two, :])

            def rope_evac(out, pz, pzr, b, br, tcols, n, f32out=None):
                P.op("act", "activation", out=ta[0][:, 0:n], in_=pz, func=AF.Identity, bias=b, scale=1.0)
                P.op("act", "activation", out=ta[1][:, 0:n], in_=pzr, func=AF.Identity, bias=br, scale=1.0)
                P.op("dve", "tensor_tensor", out=ta[0][:, 0:n], in0=ta[0][:, 0:n], in1=cos2[:, tcols], op=ALU.mult)
                P.op("dve", "tensor_tensor", out=ta[1][:, 0:n], in0=ta[1][:, 0:n], in1=sin2[:, tcols], op=ALU.mult)
                P.op("dve", "tensor_tensor", out=out, in0=ta[0][:, 0:n], in1=ta[1][:, 0:n], op=ALU.add)
                if f32out is not None:
                    P.op("dve", "tensor_tensor", out=f32out[:, 0:n], in0=ta[0][:, 0:n], in1=ta[1][:, 0:n], op=ALU.add)

            def project_block(t0, btiles):
                n = len(btiles) * 128
                tcl = slice(t0, t0 + n) if not sample else slice(0, n)
                lastb = (t0 + n == T)
                load_kv_w()
                for kvh in range(4):
                    pz = PS[5].v[:, 0:n]
                    pzr = PS[6].v[:, 0:n]
                    for c in range(8):
                        P.mm(pz, wk[:, c, kvh].rearrange("p a b -> p (a b)"), hT[:, c, 0:n], start=(c == 0), stop=(c == 7))
                    for c in range(8):
                        P.mm(pzr, wkr[:, c, kvh].rearrange("p a b -> p (a b)"), hT[:, c, 0:n], start=(c == 0), stop=(c == 7))
                    want32 = sample or lastb
                    rope_evac(kT[:, kvh, t0:t0 + n], pz, pzr, bk[:, kvh:kvh + 1], bkr[:, kvh:kvh + 1], tcl, n, f32out=rec if want32 else None)
                    if want32:
                        cols = [(n - 128, 0)] if not sample else [(k2 * 128, k2) for k2 in range(n // 128)]
                        for (cc, oi) in cols:
                            pt_ = PS[7].v[:, 0:128]
                            P.tr(pt_, rec[:, cc:cc + 128], ident_f.v)
                            P.op("act", "copy", out=kto[:, oi, kvh * 64:(kvh + 1) * 64], in_=pt_[:, 0:64])
                if sample:
                    for k2 in range(n // 128):
                        for s2 in range(2):
                            P.dma("sp", dr["swa_k_s"][0, 2 * k2 + s2, 64:128].rearrange("t h d -> t (h d)"), kto[s2 * 64:(s2 + 1) * 64, k2, :])
                elif lastb:
                    P.dma("sp", dr["swa_k_p"][0, tiles_seq].rearrange("t h d -> t (h d)"), kto[:, 0, :])
                for k, t in enumerate(btiles):
                    gt = t0 // 128 + k
                    pvv = PS[7].v
                    for c in range(8):
                        P.mm(pvv, hT[:, c, k * 128:(k + 1) * 128], wv[:, c].rearrange("p a b d -> p (a b d)"), start=(c == 0), stop=(c == 7))
                    P.op("dve", "tensor_tensor", out=rec, in0=pvv, in1=bv, op=ALU.add)
                    P.op("act", "copy", out=Vd[:, gt, :], in_=rec)
                    if sample or gt == nt - 1:
                        P.op("act", "copy", out=vf.rearrange("p (h f) -> p h f", h=4), in_=rec.rearrange("p (h d f) -> p h d f", h=4, d=2)[:, :, 0, :])
                        if not sample:
                            P.dma("sp", dr["swa_v_p"][0, tiles_seq].rearrange("t h d -> t (h d)"), vf)
                        else:
                            for s2 in range(2):
                                P.dma("sp", dr["swa_v_s"][0, 2 * gt + s2, 64:128].rearrange("t h d -> t (h d)"), vf[s2 * 64:(s2 + 1) * 64, :])
                for half in range(2):
                    load_q_w(half)
                    for p4 in range(4):
                        pr = half * 4 + p4
                        pz = PS[5].v[:, 0:n]
                        pzr = PS[6].v[:, 0:n]
                        for c in range(8):
                            P.mm(pz, wqh[:, c, p4 * 128:(p4 + 1) * 128], hT[:, c, 0:n], start=(c == 0), stop=(c == 7))
                        for c in range(8):
                            P.mm(pzr, wqrh[:, c, 2 * p4:2 * p4 + 2].rearrange("p a b d -> p (a b d)"), hT[:, c, 0:n], start=(c == 0), stop=(c == 7))
                        rope_evac(qT[:, pr, 0:n], pz, pzr, bq[:, pr:pr + 1], bqr[:, pr:pr + 1], tcl, n)

            def wo_apply(tlist):
                for k4, t in enumerate(tlist):
                    for dh in range(2):
                        po = PS[5 + dh].v
                        for pr in range(8):
                            P.mm(po, OT[:, pr, k4 * 128:(k4 + 1) * 128], wo[:, pr, dh * 512:(dh + 1) * 512], start=(pr == 0), stop=(pr == 7))
                        add_to_x(t, dh, po)

            if not sample:
                for qb in range(nt // 4):
                    sub = tiles[qb * 4:qb * 4 + 4]
                    norm_T(sub, dr["g_mix"][layer], hT, None, hb2)
                    project_block(qb * 512, sub)
                    for h in range(16):
                        kvh, half, pr = h // 4, h % 2, h // 2
                        r0 = 64 * half
                        chunks = []
                        for kc in range(max(4 * qb - 1, 0), 4 * qb + 4):
                            base = 128 * (kc - 4 * qb)
                            c0, c1 = max(0, base), min(512, base + 256)
                            fix = []
                            a, b_ = max(c0, base + 192), min(c1, base + 256)
                            if b_ > a:
                                fix.append((0, 64, a, b_))
                            a, b_ = max(c0, base), min(c1, base + 64)
                            if b_ > a:
                                fix.append((64, 128, a, b_))
                            chunks.append(dict(kl=[(kT[r0:r0 + 64, kvh, kc * 128:(kc + 1) * 128], qT[r0:r0 + 64, pr, c0:c1])],
                                               nk=128, c0=c0, c1=c1, fix=fix,
                                               vl=[Vd[:, kc, kvh * 128:(kvh + 1) * 128]]))

                        def out_fn(O, r, r0=r0, pr=pr):
                            P.op("dve", "tensor_tensor", out=OT[r0:r0 + 64, pr, :], in0=O[0][r0:r0 + 64, :], in1=r[r0:r0 + 64, :], op=ALU.mult)
                        softmax_group(chunks, 512, SWA_SCALE, 1, E3, rec, out_fn, sink=sk[:, h:h + 1], zb=h)
                    wo_apply(sub)
            else:
                norm_T(tiles, dr["g_mix"][layer], hT, None, hb2)
                project_block(0, tiles)
                ck = dr["cache_swa_k"][0]
                cv = dr["cache_swa_v"][0]
                for s in range(NSS):
                    P.dma("sp", dr["swa_k_s"][0, s, 0:64], ck[s, 64:128])
                    P.dma("sp", dr["swa_v_s"][0, s, 0:64], cv[s, 64:128])
                    kcb = E3[2].rearrange("p (h d f) -> p h d f", h=4, d=2)
                    for dup in range(2):
                        P.dma("pool", kcb[:, :, dup, :], ck[s])
                        P.dma("pool", Vd[:, nt, :].rearrange("p (h d f) -> p h d f", h=4, d=2)[:, :, dup, :], cv[s])
                    for kvh in range(4):
                        pv = psb(7)
                        P.tr(pv[:, 0:128], E3[2][:, kvh * 128:(kvh + 1) * 128], ident.v)
                        P.op("act", "copy", out=kT[:, kvh, T:T + 128], in_=pv[:, 0:128])
                    P.dma("sp", vown, Vd[(s % 2) * 64:(s % 2) * 64 + 64, s // 2, :])
                    oc = slice(s * 64, (s + 1) * 64)
                    for h in range(16):
                        kvh, half, pr = h // 4, h % 2, h // 2
                        r0 = 64 * half
                        chunks = [
                            dict(kl=[(kT[r0:r0 + 64, kvh, T:T + 128], qT[r0:r0 + 64, pr, oc])], nk=128, c0=0, c1=64, fix=[],
                                 vl=[Vd[:, nt, kvh * 128:(kvh + 1) * 128]]),
                            dict(kl=[(kT[r0:r0 + 64, kvh, oc], qT[r0:r0 + 64, pr, oc])], nk=64, c0=0, c1=64, fix=[],
                                 vl=[vown[:, kvh * 128:(kvh + 1) * 128]]),
                        ]

                        def out_fn(O, r, r0=r0, pr=pr, oc=oc):
                            P.op("dve", "tensor_tensor", out=OT[r0:r0 + 64, pr, oc], in0=O[0][r0:r0 + 64, :], in1=r[r0:r0 + 64, :], op=ALU.mult)
                        softmax_group(chunks, 64, SWA_SCALE, 1, E3[0:2] + [E3[1]], rec, out_fn, sink=sk[:, h:h + 1], zb=h)
                wo_apply(tiles)

        def sb(layer, tiles, sample):
            arena.reset()
            nt = len(tiles)
            T = nt * 128
            hT = arena.alloc([128, 8, T], BF16)
            junk = None
            hb2 = [arena.alloc([128, D], BF16) for _ in range(2)]
            kT = arena.alloc([128, 4, 2048 + 128], BF16)
            Vg = arena.alloc([128, 17, 512], BF16)
            qT = arena.alloc([128, 4, 512 if not sample else T], BF16)
            OT = arena.alloc([128, 4, 512 if not sample else T], BF16)
            w2 = [arena.alloc([128, 8, 512], BF16) for _ in range(2)]
            wo = arena.alloc([128, 4, D], BF16)
            ef2 = [arena.alloc([128, 512], F32) for _ in range(3)]
            sp2 = [arena.alloc([128, 512], BF16) for _ in range(3)]
            tf2 = [arena.alloc([128, 512], F32) for _ in range(3)]
            A2 = [arena.alloc([128, 512], BF16) for _ in range(3)]
            R = arena.alloc([128, 512], F32)
            of2 = [arena.alloc([128, 512], F32) for _ in range(2)]
            ob2 = [arena.alloc([128, 512], BF16) for _ in range(2)]
            vown = arena.alloc([64, 512], BF16) if sample else None
            kown = arena.alloc([128, 4, 256], BF16) if sample else None
            vownt = arena.alloc([128, 2, 512], BF16) if sample else None
            ZB = CFG.get("sb_zb", [0, 1, 5])
            RB = CFG.get("sb_rb", [2, 7, 6])
            Wv = wview(dr["w_sb_qkv"][0])
            norm_T(tiles, dr["g_mix"][layer], hT, junk, hb2)
            wi = [0]

            def load_w(col0):
                w = w2[wi[0] % 2]
                wi[0] += 1
                P.dma("pool", w, Wv[:, :, col0:col0 + 512])
                return w

            def proj_tile(w, k):
                pz = PS[5 + k % 2].v
                for c in range(8):
                    P.mm(pz, hT[:, c, k * 128:(k + 1) * 128], w[:, c, :], start=(c == 0), stop=(c == 7))
                return pz

            def to_T(dst, src_bf):
                pv = psb(7)
                for pr in range(4):
                    P.tr(pv[:, pr * 128:(pr + 1) * 128], src_bf[:, pr * 128:(pr + 1) * 128], ident.v)
                P.op("act", "copy", out=dst, in_=pv[:, 0:512].rearrange("p (a t) -> p a t", a=4))

            def unit_s1(d):
                z, nk, c0, c1, ui, mask = d["z"], d["nk"], d["c0"], d["c1"], d["ui"], d["mask"]
                d["zfn"]()
                for _ in range(CFG.get("sb_dummy", 0)):
                    P.mm(PS[d["dbank"]].v, zeros[:, 0:128], zeros, start=True, stop=True)
                zz = z[0:nk, c0:c1]
                ef = ef2[ui % 3][0:nk, c0:c1]
                sp = sp2[ui % 3][0:nk, c0:c1]
                P.op("act", "activation", out=ef, in_=zz, func=AF.Exp)
                P.op("act", "activation", out=sp, in_=ef, func=AF.Ln, bias=1.0, scale=1.0)
                if mask is not None:
                    mo, mv = mask
                    P.op("dve", "tensor_tensor", out=sp2[ui % 3][0:nk, mo], in0=sp2[ui % 3][0:nk, mo], in1=mv, op=ALU.mult)

            def unit_s1b(d):
                z, nk, c0, c1, ui = d["z"], d["nk"], d["c0"], d["c1"], d["ui"]
                zz = z[0:nk, c0:c1]
                sp = sp2[ui % 3][0:nk, c0:c1]
                P.mm(zz, negtri[0:nk, 0:nk], sp, start=False, stop=True, skip_group_check=True)
                rs = PS[RB[ui % 3]].v
                P.mm(rs[:, c0:c1], ones[0:nk, :], sp, start=True, stop=True)

            def unit_s2(d):
                z, nk, c0, c1, ui, mask = d["z"], d["nk"], d["c0"], d["c1"], d["ui"], d["mask"]
                zz = z[0:nk, c0:c1]
                tf = tf2[ui % 3][0:nk, c0:c1]
                A = A2[ui % 3][0:nk, c0:c1]
                rs = PS[RB[ui % 3]].v
                P.op("dve", "tensor_tensor", out=tf, in0=zz, in1=R[0:nk, c0:c1], op=ALU.subtract)
                P.op("act", "activation", out=A, in_=tf, func=AF.Exp)
                if mask is not None:
                    mo, mv = mask
                    P.op("dve", "tensor_tensor", out=A2[ui % 3][0:nk, mo], in0=A2[ui % 3][0:nk, mo], in1=mv, op=ALU.mult)
                P.op("dve", "tensor_tensor", out=R[:, c0:c1], in0=R[:, c0:c1], in1=rs[:, c0:c1], op=ALU.add)
                for (oap, vlhs, aap) in d["vl"](A2[ui % 3]):
                    P.mm(oap, vlhs, aap, start=False, stop=d["last"], skip_group_check=True)

            def run_units(units):
                n = len(units)
                for i in range(n + 2):
                    if CFG.get("sb_order", 0) == 1 and i >= 2:
                        unit_s2(units[i - 2])
                    if i < n:
                        unit_s1(units[i])
                        unit_s1b(units[i])
                    if CFG.get("sb_order", 0) == 0 and i >= 2:
                        unit_s2(units[i - 2])

            for g in range(2):
                P.dma("pool", wo, wview(dr["w_sb_o"][0])[:, 4 * g:4 * g + 4, :])
                for which in range(2):
                    w = load_w(1024 * (1 + which) + 512 * g)
                    for k, t in enumerate(tiles):
                        pz = proj_tile(w, k)
                        of = of2[k % 2]
                        ob = ob2[k % 2]
                        P.op("act", "copy", out=of, in_=pz)
                        name = ("sb_k_" if which == 0 else "sb_v_") + ("s" if sample else "p")
                        if not sample:
                            dst = dr[name][0, tiles_seq, k * 128:(k + 1) * 128, 8 * g:8 * g + 8].rearrange("t h d -> t (h d)")
                            P.dma("sp", dst, of)
                        else:
                            for s2 in range(2):
                                dst = dr[name][0, 2 * k + s2, :, 8 * g:8 * g + 8].rearrange("t h d -> t (h d)")
                                P.dma("sp", dst, of[s2 * 64:(s2 + 1) * 64, :])
                        if which == 0:
                            P.op("dve", "tensor_copy", out=ob, in_=of)
                            if not sample:
                                to_T(kT[:, :, k * 128:(k + 1) * 128], ob)
                            else:
                                to_T(kown[:, :, k * 128:(k + 1) * 128], ob)
                        else:
                            if not sample:
                                P.op("dve", "tensor_copy", out=Vg[:, k, :], in_=of)
                            else:
                                P.op("dve", "tensor_copy", out=vownt[:, k, :], in_=of)
                P.mark("sb_a")
                wq_ = load_w(512 * g)
                if not sample:
                    for qb in range(T // 512):
                        for k4 in range(4):
                            k = qb * 4 + k4
                            pz = proj_tile(wq_, k)
                            ob = ob2[k % 2]
                            P.op("act", "activation", out=ob, in_=pz, func=AF.Copy, scale=SB_SCALE)
                            to_T(qT[:, :, k4 * 128:(k4 + 1) * 128], ob)
                        ui = 0
                        for hh in range(8):
                            pr, half = hh // 2, hh % 2
                            r0 = 64 * half
                            P.emit("dve", lambda e, a=R.ap: e.memset(a, 0.0), [], R.bufs)
                            O = PS[3 + hh % 2].v
                            P.mm(O, zeros[:, 0:128], zeros, start=True, stop=False)
                            units = []
                            for kc in range(4 * qb + 3, -1, -1):
                                j = kc - 4 * qb
                                c0 = 128 * j if j >= 0 else 0
                                z = PS[ZB[ui % 3]].v
                                units.append(dict(
                                    z=z, nk=128, c0=c0, c1=512, ui=ui, last=(kc == 0), dbank=3 + (hh + 1) % 2,
                                    mask=(slice(c0, c0 + 128), tri01) if j >= 0 else None,
                                    zfn=lambda z=z, c0=c0, kc=kc, r0=r0, pr=pr: P.mm(z[:, c0:512], kT[r0:r0 + 64, pr, kc * 128:(kc + 1) * 128], qT[r0:r0 + 64, pr, c0:512], start=True, stop=True),
                                    vl=lambda Ab, O=O, kc=kc, pr=pr, c0=c0: [(O[:, c0:512], Vg[:, kc, pr * 128:(pr + 1) * 128], Ab[:, c0:512])]))
                                ui += 1
                            run_units(units)
                            P.op("act", "copy", out=OT[r0:r0 + 64, pr, :], in_=O[r0:r0 + 64, :])
                        for k4 in range(4):
                            t = tiles[qb * 4 + k4]
                            for dh in range(2):
                                po = PS[5 + dh].v
                                for pr in range(4):
                                    P.mm(po, OT[:, pr, k4 * 128:(k4 + 1) * 128], wo[:, pr, dh * 512:(dh + 1) * 512], start=(pr == 0), stop=(pr == 3))
                                add_to_x(t, dh, po)
                else:
                    for k, t in enumerate(tiles):
                        pz = proj_tile(wq_, k)
                        ob = A2[k % 2]
                        P.op("act", "activation", out=ob, in_=pz, func=AF.Copy, scale=SB_SCALE)
                        to_T(qT[:, :, k * 128:(k + 1) * 128], ob)
                    ck = dr["cache_sb_k"][0]
                    cv = dr["cache_sb_v"][0]
                    ui = 0
                    P.mark("sb_b")
                    for s in range(NSS):
                        P.dma("pool", Vg[:, 0:16, :], ck[s, :, 8 * g:8 * g + 8].rearrange("(t p) h d -> p t (h d)", p=128))
                        for kt in range(16):
                            to_T(kT[:, :, kt * 128:(kt + 1) * 128], Vg[:, kt, :])
                        P.dma("pool", Vg[:, 0:16, :], cv[s, :, 8 * g:8 * g + 8].rearrange("(t p) h d -> p t (h d)", p=128))
                        P.dma("sp", vown, vownt[(s % 2) * 64:(s % 2) * 64 + 64, s // 2, :])
                        oc = slice((s % 2) * 64, (s % 2) * 64 + 64)
                        qc = slice(s * 64, (s + 1) * 64)
                        P.mark("sb_c")
                        P.emit("dve", lambda e, a=R.ap: e.memset(a, 0.0), [], R.bufs)
                        O = PS[3 + s % 2].v
                        P.mm(O, zeros[:, 0:128], zeros, start=True, stop=False)
                        units = []
                        for kc in range(16, -1, -1):
                            nk = 64 if kc == 16 else 128
                            for half in range(2):
                                r0 = 64 * half
                                cb0 = half * 256
                                z = PS[ZB[ui % 3]].v

                                def zfn(z=z, nk=nk, cb0=cb0, r0=r0, kc=kc, qc=qc):
                                    P.mm(z[0:nk, cb0:cb0 + 256], zeros[:, 0:nk], zeros[:, 0:256], start=True, stop=False)
                                    for pr in range(4):
                                        ksrc = kown[r0:r0 + 64, pr, qc] if kc == 16 else kT[r0:r0 + 64, pr, kc * 128:(kc + 1) * 128]
                                        P.mm(z[0:nk, cb0 + pr * 64:cb0 + (pr + 1) * 64], ksrc, qT[r0:r0 + 64, pr, qc], start=False, stop=(pr == 3), skip_group_check=True)

                                def vl(Ab, O=O, kc=kc, nk=nk, cb0=cb0):
                                    res = []
                                    for pr in range(4):
                                        vsrc = vown[:, pr * 128:(pr + 1) * 128] if kc == 16 else Vg[:, kc, pr * 128:(pr + 1) * 128]
                                        res.append((O[:, cb0 + pr * 64:cb0 + (pr + 1) * 64], vsrc, Ab[0:nk, cb0 + pr * 64:cb0 + (pr + 1) * 64]))
                                    return res
                                units.append(dict(z=z, nk=nk, c0=cb0, c1=cb0 + 256, ui=ui, last=(kc == 0), zfn=zfn, vl=vl, dbank=3 + (s + 1) % 2,
                                                  mask=(slice(cb0, cb0 + 256), msk_s[0:64, 0:256]) if kc == 16 else None))
                                ui += 1
                        run_units(units)
                        for hh in range(8):
                            pr, half = hh // 2, hh % 2
                            r0 = 64 * half
                            P.op("act", "copy", out=OT[r0:r0 + 64, pr, qc], in_=O[r0:r0 + 64, half * 256 + pr * 64:half * 256 + (pr + 1) * 64])
                    for k, t in enumerate(tiles):
                        for dh in range(2):
                            po = PS[5 + dh].v
                            for pr in range(4):
                                P.mm(po, OT[:, pr, k * 128:(k + 1) * 128], wo[:, pr, dh * 512:(dh + 1) * 512], start=(pr == 0), stop=(pr == 3))
                            add_to_x(t, dh, po)

        def final_norm(tiles, dst_fn):
            arena.reset()
            yb2 = [arena.alloc([128, D], F32) for _ in range(2)]
            junk = arena.alloc([128, D], BF16)
            P.dma("sp", gb.v, dr["g_final"].partition_broadcast(128))
            for k, t in enumerate(tiles):
                P.op("act", "activation", out=junk, in_=X[t].v, func=AF.Square, scale=1.0 / 32.0, accum_out=ssb[:, k:k + 1])
            rstd_from_ms(len(tiles), 0)
            for k, t in enumerate(tiles):
                yb = yb2[k % 2]
                P.op("dve", "scalar_tensor_tensor", out=yb, in0=X[t].v, scalar=rstd[:, k:k + 1], in1=gb.v, op0=ALU.mult, op1=ALU.mult)
                P.dma("sp", dst_fn(k), yb)

        def run_pass(sample, si):
            nonlocal tiles_seq
            tiles_seq = si
            if not sample:
                tiles = list(range(16))
                for t in tiles:
                    P.dma("sp", X[t].v, dr["x_prompt"][si, t * 128:(t + 1) * 128, :])
            else:
                tiles = [0, 1]
                for t in tiles:
                    P.dma("sp", X[t].v, dr["x_sample"][2 * t:2 * t + 2].rearrange("s t d -> (s t) d"))
            for layer in range(CFG["depth"]):
                m = layer % 3
                if m == 0:
                    mla(layer, tiles, sample)
                elif m == 1:
                    sb(layer, tiles, sample)
                else:
                    swa(layer, tiles, sample)
                P.mark("mix%d" % layer)
                if not sample:
                    ffn(layer, tiles, 1, 512, None, dr["ffn_conv_p"][layer, si:si + 1])
                else:
                    ffn(layer, tiles, 4, 64, dr["state_ffn_conv"][layer], dr["ffn_conv_s"][layer])
                P.mark("ffn%d" % layer)
                if not sample:
                    ple(layer, tiles, lambda t, layer=layer: dr["p_prompt"][layer, si, t * 128:(t + 1) * 128, :])
                else:
                    ple(layer, tiles, lambda t, layer=layer: dr["p_sample"][layer, 2 * t:2 * t + 2].rearrange("s t f -> (s t) f"))
                P.mark("ple%d" % layer)
            if not sample:
                final_norm(tiles, lambda k: dr["y_prompt"][si, k * 128:(k + 1) * 128, :])
            else:
                final_norm(tiles, lambda k: dr["y_sample"][2 * k:2 * k + 2].rearrange("s t d -> (s t) d"))

        if CFG["sample"]:
            run_pass(True, 0)
            P.stopped = False
        for si in range(CFG["nprompt"]):
            run_pass(False, si)
            P.stopped = False
        P.finalize(block, sems, dsems)
    return nc


def _consts():
    c = {}
    c["c_ident"] = np.eye(128, dtype=np.float32)
    half = 32
    inv = (10000.0 ** (-np.arange(half, dtype=np.float32) / half)).astype(np.float32)
    pos_t = np.concatenate([np.arange(2048), 2048 + (np.arange(128) % 64)]).astype(np.float32)
    ang = pos_t[:, None] * inv[None, :]
    c["c_cost"] = np.cos(ang).astype(np.float32)
    c["c_sint"] = np.sin(ang).astype(np.float32)
    pos_f = np.concatenate([np.arange(2048), 2048 + (np.arange(512) % 64)]).astype(np.float32)
    angf = (inv[:, None] * pos_f[None, :]).astype(np.float32)
    cf, sf = np.cos(angf).astype(np.float32), np.sin(angf).astype(np.float32)
    c["c_cos2"] = np.concatenate([cf, cf, cf, cf], 0)
    c["c_sin2"] = np.concatenate([-sf, sf, -sf, sf], 0)
    k = np.arange(128)[:, None]
    q = np.arange(128)[None, :]
    tri01 = (k < q).astype(np.float32)
    negtri = -(k >= q).astype(np.float32)
    ones = np.ones((128, 128), np.float32)
    zeros = np.zeros((128, 512), np.float32)
    m64 = np.zeros((128, 64), np.float32)
    m64[0:64, :] = (np.arange(64)[:, None] < np.arange(64)[None, :])
    msk_s = np.tile(m64, (1, 8))
    c["c_msk"] = np.concatenate([tri01, negtri, ones, zeros, msk_s], 1).astype(np.float32)
    return c


_NC = None
OUT_NAMES = ["y_prompt", "y_sample", "mla_ckv_p", "mla_krope_p", "mla_ckv_s", "mla_krope_s",
             "sb_k_p", "sb_v_p", "sb_k_s", "sb_v_s", "swa_k_p", "swa_v_p", "swa_k_s", "swa_v_s",
             "ffn_conv_p", "ffn_conv_s"]
BATCH_AXIS = {"x_prompt": 0, "x_sample": 0, "p_prompt": 1, "p_sample": 1, "cache_mla_ckv": 1, "cache_mla_krope": 1,
              "cache_sb_k": 1, "cache_sb_v": 1, "cache_swa_k": 1, "cache_swa_v": 1, "state_ffn_conv": 1}


def kernel(**inputs):
    global _NC
    if _NC is None:
        _NC = build_program()
    nc = _NC
    consts = _consts()
    in_maps = []
    for c in range(NCORES):
        m = dict(consts)
        for name, arr in inputs.items():
            a = np.asarray(arr, dtype=np.float32)
            if name in BATCH_AXIS:
                ax = BATCH_AXIS[name]
                sl = [slice(None)] * a.ndim
                sl[ax] = slice(4 * c, 4 * c + 4)
                a = np.ascontiguousarray(a[tuple(sl)])
            m[name] = a
        in_maps.append(m)
    res = run_bass_kernel_spmd(nc, in_maps, core_ids=list(range(NCORES)))
    outs = []
    for name in OUT_NAMES:
        ax = 0 if name in ("y_prompt", "y_sample") else 1
        outs.append(np.concatenate([np.asarray(r[name]) for r in res.results], axis=ax).astype(np.float32))
    return tuple(outs)
```
